# Optimizing a Trainium2 kernel written in Bass

```python
import jax, jax.numpy as jnp
from jax import lax
import numpy as np

D_MODEL = 2048
BATCH = 1
SEQ = 16384
DEPTH = 1
DEC_BATCH = 32
DEC_SEQ = 4
PAST_LEN = 16384
PAGE_SIZE = 128

POOL_WIDTH = D_MODEL // 2
POOL_WINDOWS = (2, 4, 8, 16)
POOL_GROUP = POOL_WIDTH // len(POOL_WINDOWS)
POOL_STATE = max(POOL_WINDOWS) - 1
HEAD_DIM = 128
N_HEADS = (D_MODEL - POOL_WIDTH) // HEAD_DIM
ATTN_WIDTH = N_HEADS * HEAD_DIM
DILATIONS = ((128, 1), (512, 4), (2048, 16))
MAX_WINDOW = 2048
Q_BLOCK = 128
ROPE_THETA = 10000.0
N_MEM = 256
MEM_HEADS = 4
MEM_HEAD_DIM = 128
MEM_WIDTH = MEM_HEADS * MEM_HEAD_DIM
D_FF = 5632
CONV_WIDTH = 3
EPS = 1e-6
NEG_INF = -1e30
IN_WIDTH = POOL_WIDTH + 3 * ATTN_WIDTH

kernel_name = "hybrid_pool_dilated_attn_decoder_step"


def _rms_norm(x, gain):
    xf = x.astype(jnp.float32)
    y = xf * lax.rsqrt(jnp.mean(xf * xf, axis=-1, keepdims=True) + EPS)
    return (y * gain.astype(jnp.float32)).astype(x.dtype)


def _rope(x, pos):
    half = HEAD_DIM // 2
    inv = 1.0 / (ROPE_THETA ** (jnp.arange(half, dtype=jnp.float32) * (2.0 / HEAD_DIM)))
    ang = pos.astype(jnp.float32)[:, None] * inv[None, :]
    cos = jnp.cos(ang)[None, :, None, :]
    sin = jnp.sin(ang)[None, :, None, :]
    xf = x.astype(jnp.float32)
    x1, x2 = xf[..., :half], xf[..., half:]
    return jnp.concatenate([x1 * cos - x2 * sin, x1 * sin + x2 * cos], axis=-1).astype(x.dtype)


def _multi_scale_pool(u_ext, pos, w_pool, pool_scale):
    t = pos.shape[0]
    prefix = u_ext.shape[1] - t
    uf = u_ext.astype(jnp.float32)
    cs = jnp.concatenate([jnp.zeros_like(uf[:, :1]), jnp.cumsum(uf, axis=1)], axis=1)
    end = cs[:, prefix + 1:]
    u_t = uf[:, prefix:]
    outs = []
    for g, w in enumerate(POOL_WINDOWS):
        sl = slice(g * POOL_GROUP, (g + 1) * POOL_GROUP)
        start = cs[:, prefix + 1 - w: prefix + 1 - w + t, sl]
        cnt = jnp.minimum(pos + 1, w).astype(jnp.float32)[None, :, None]
        diff = (end[..., sl] - start) / cnt - u_t[..., sl]
        outs.append(jnp.einsum("btc,cd->btd", diff, w_pool[g].astype(jnp.float32)))
    y = jnp.concatenate(outs, axis=-1) * pool_scale.astype(jnp.float32)
    return y.astype(u_ext.dtype)


def _dilated_block(q, qpos, qidx, k_ext, v_ext):
    scale = HEAD_DIM ** -0.5
    outs, lses = [], []
    for window, dil in DILATIONS:
        dist = dil * jnp.arange(window // dil + 1)
        valid = (qpos[:, None] - dist[None, :]) >= 0
        idx = jnp.maximum(qidx[:, None] - dist[None, :], 0)
        kg = jnp.take(k_ext, idx, axis=1)
        vg = jnp.take(v_ext, idx, axis=1)
        s = jnp.einsum("bthd,btkhd->bthk", q, kg, preferred_element_type=jnp.float32) * scale
        s = jnp.where(valid[None, :, None, :], s, NEG_INF)
        m = jnp.max(s, axis=-1, keepdims=True)
        e = jnp.exp(s - m)
        den = jnp.sum(e, axis=-1)
        o = jnp.einsum("bthk,btkhd->bthd", e, vg.astype(jnp.float32)) / den[..., None]
        outs.append(o)
        lses.append(m[..., 0] + jnp.log(den))
    w = jax.nn.softmax(jnp.stack(lses, axis=0), axis=0)
    out = jnp.sum(w[..., None] * jnp.stack(outs, axis=0), axis=0)
    return out.astype(q.dtype)


def _dilated_attention(q, pos, k_ext, v_ext):
    t = q.shape[1]
    prefix = k_ext.shape[1] - t
    qidx = prefix + jnp.arange(t)
    if t % Q_BLOCK == 0 and t > Q_BLOCK:
        def blk(b):
            s0 = b * Q_BLOCK
            return _dilated_block(lax.dynamic_slice_in_dim(q, s0, Q_BLOCK, 1),
                                  lax.dynamic_slice_in_dim(pos, s0, Q_BLOCK, 0),
                                  lax.dynamic_slice_in_dim(qidx, s0, Q_BLOCK, 0),
                                  k_ext, v_ext)
        o = lax.map(blk, jnp.arange(t // Q_BLOCK))
        return jnp.moveaxis(o, 0, 1).reshape(q.shape)
    return _dilated_block(q, pos, qidx, k_ext, v_ext)


def _memory_kv(mem, norm_src, w_k, w_v, k_norm):
    b, n = mem.shape[:2]
    m = _rms_norm(mem, norm_src)
    k = _rms_norm((m @ w_k).reshape(b, n, MEM_HEADS, MEM_HEAD_DIM), k_norm)
    v = (m @ w_v).reshape(b, n, MEM_HEADS, MEM_HEAD_DIM)
    return k, v


def _memory_attention(h, mem_k, mem_v, w_q, q_norm, w_o):
    b, t = h.shape[:2]
    q = _rms_norm((h @ w_q).reshape(b, t, MEM_HEADS, MEM_HEAD_DIM), q_norm)
    s = jnp.einsum("bthd,bmhd->bhtm", q, mem_k, preferred_element_type=jnp.float32) * (MEM_HEAD_DIM ** -0.5)
    p = jax.nn.softmax(s, axis=-1)
    o = jnp.einsum("bhtm,bmhd->bthd", p, mem_v.astype(jnp.float32))
    return o.reshape(b, t, MEM_WIDTH).astype(h.dtype) @ w_o


def _conv_ffn(h, conv_prev, w_gate, w_up, conv_w, conv_b, w_down):
    t = h.shape[1]
    g = h @ w_gate
    up = h @ w_up
    g_ext = jnp.concatenate([conv_prev, g], axis=1)
    c = conv_b + sum(g_ext[:, j:j + t] * conv_w[j] for j in range(CONV_WIDTH))
    y = (jax.nn.silu(c) * up) @ w_down
    return y, g_ext[:, -(CONV_WIDTH - 1):]


def _layer(x, pos, pool_prev, k_prev, v_prev, mem_k, mem_v, conv_prev, p):
    b, t = x.shape[:2]
    h = _rms_norm(x, p["norm_mix"])
    proj = h @ p["w_in"]
    u = proj[..., :POOL_WIDTH]
    q = proj[..., POOL_WIDTH:POOL_WIDTH + ATTN_WIDTH].reshape(b, t, N_HEADS, HEAD_DIM)
    k = proj[..., POOL_WIDTH + ATTN_WIDTH:POOL_WIDTH + 2 * ATTN_WIDTH].reshape(b, t, N_HEADS, HEAD_DIM)
    v = proj[..., POOL_WIDTH + 2 * ATTN_WIDTH:].reshape(b, t, N_HEADS, HEAD_DIM)
    q = _rope(_rms_norm(q, p["q_norm"]), pos)
    k = _rope(_rms_norm(k, p["k_norm"]), pos)
    u_ext = jnp.concatenate([pool_prev, u], axis=1)
    pool_out = _multi_scale_pool(u_ext, pos, p["w_pool"], p["pool_scale"])
    attn_out = _dilated_attention(q, pos, jnp.concatenate([k_prev, k], axis=1),
                                  jnp.concatenate([v_prev, v], axis=1))
    mixed = jnp.concatenate([pool_out, attn_out.reshape(b, t, ATTN_WIDTH)], axis=-1)
    x = x + mixed @ p["w_out"]
    x = x + _memory_attention(_rms_norm(x, p["norm_mem"]), mem_k, mem_v,
                              p["w_mem_q"], p["mem_q_norm"], p["w_mem_o"])
    f, conv_state = _conv_ffn(_rms_norm(x, p["norm_ffn"]), conv_prev, p["w_gate"], p["w_up"],
                              p["conv_w"], p["conv_b"], p["w_down"])
    x = x + f
    return x, u_ext[:, -POOL_STATE:], k, v, conv_state


def setup_inputs(seed: int = 0) -> dict:
    key = jax.random.key(seed)
    ks = iter(jax.random.split(key, 40))
    f32 = jnp.float32

    def nrm(shape, scale):
        return jax.random.normal(next(ks), shape, f32) * scale

    def gain(shape):
        return 1.0 + nrm(shape, 0.02)

    win_buf = min(MAX_WINDOW, PAST_LEN)
    return {
        "x_prompt": nrm((BATCH, SEQ, D_MODEL), 1.0),
        "x_sample": nrm((DEC_BATCH, DEC_SEQ, D_MODEL), 1.0),
        "state_pool": nrm((DEPTH, DEC_BATCH, POOL_STATE, POOL_WIDTH), 1.0),
        "cache_win_k": nrm((DEPTH, DEC_BATCH, win_buf, N_HEADS, HEAD_DIM), 1.0),
        "cache_win_v": nrm((DEPTH, DEC_BATCH, win_buf, N_HEADS, HEAD_DIM), 1.0),
        "cache_mem_k": nrm((DEPTH, DEC_BATCH, N_MEM, MEM_HEADS, MEM_HEAD_DIM), 1.0),
        "cache_mem_v": nrm((DEPTH, DEC_BATCH, N_MEM, MEM_HEADS, MEM_HEAD_DIM), 1.0),
        "state_conv": nrm((DEPTH, DEC_BATCH, CONV_WIDTH - 1, D_FF), 1.0),
        "mem_prompt": nrm((BATCH, N_MEM, D_MODEL), 1.0),
        "norm_mix": gain((DEPTH, D_MODEL)),
        "w_in": nrm((DEPTH, D_MODEL, IN_WIDTH), D_MODEL ** -0.5),
        "q_norm": gain((DEPTH, HEAD_DIM)),
        "k_norm": gain((DEPTH, HEAD_DIM)),
        "w_pool": nrm((DEPTH, len(POOL_WINDOWS), POOL_GROUP, POOL_GROUP), POOL_GROUP ** -0.5),
        "pool_scale": gain((DEPTH, POOL_WIDTH)),
        "w_out": nrm((DEPTH, D_MODEL, D_MODEL), D_MODEL ** -0.5),
        "norm_mem": gain((DEPTH, D_MODEL)),
        "norm_mem_src": gain((DEPTH, D_MODEL)),
        "w_mem_q": nrm((DEPTH, D_MODEL, MEM_WIDTH), D_MODEL ** -0.5),
        "w_mem_k": nrm((DEPTH, D_MODEL, MEM_WIDTH), D_MODEL ** -0.5),
        "w_mem_v": nrm((DEPTH, D_MODEL, MEM_WIDTH), D_MODEL ** -0.5),
        "mem_q_norm": gain((DEPTH, MEM_HEAD_DIM)),
        "mem_k_norm": gain((DEPTH, MEM_HEAD_DIM)),
        "w_mem_o": nrm((DEPTH, MEM_WIDTH, D_MODEL), MEM_WIDTH ** -0.5),
        "norm_ffn": gain((DEPTH, D_MODEL)),
        "w_gate": nrm((DEPTH, D_MODEL, D_FF), D_MODEL ** -0.5),
        "w_up": nrm((DEPTH, D_MODEL, D_FF), D_MODEL ** -0.5),
        "conv_w": nrm((DEPTH, CONV_WIDTH, D_FF), CONV_WIDTH ** -0.5),
        "conv_b": nrm((DEPTH, D_FF), 0.02),
        "w_down": nrm((DEPTH, D_FF, D_MODEL), D_FF ** -0.5),
    }


def reference(x_prompt, x_sample, state_pool, cache_win_k, cache_win_v, cache_mem_k, cache_mem_v,
              state_conv, mem_prompt, norm_mix, w_in, q_norm, k_norm, w_pool, pool_scale, w_out,
              norm_mem, norm_mem_src, w_mem_q, w_mem_k, w_mem_v, mem_q_norm, mem_k_norm, w_mem_o,
              norm_ffn, w_gate, w_up, conv_w, conv_b, w_down):
    bp, tp = x_prompt.shape[:2]
    ts = x_sample.shape[1]
    pos_p = jnp.arange(tp)
    pos_s = PAST_LEN + jnp.arange(ts)
    keep_p = min(MAX_WINDOW, tp)
    dt = x_prompt.dtype
    xp, xs = x_prompt, x_sample
    p_pool, p_k, p_v, p_mk, p_mv, p_conv = [], [], [], [], [], []
    s_pool, s_k, s_v, s_conv = [], [], [], []
    for l in range(DEPTH):
        prm = {
            "norm_mix": norm_mix[l], "w_in": w_in[l], "q_norm": q_norm[l], "k_norm": k_norm[l],
            "w_pool": w_pool[l], "pool_scale": pool_scale[l], "w_out": w_out[l],
            "norm_mem": norm_mem[l], "w_mem_q": w_mem_q[l], "mem_q_norm": mem_q_norm[l],
            "w_mem_o": w_mem_o[l], "norm_ffn": norm_ffn[l], "w_gate": w_gate[l], "w_up": w_up[l],
            "conv_w": conv_w[l], "conv_b": conv_b[l], "w_down": w_down[l],
        }
        mk, mv = _memory_kv(mem_prompt, norm_mem_src[l], w_mem_k[l], w_mem_v[l], mem_k_norm[l])
        xp, pool_st, kp, vp, conv_st = _layer(
            xp, pos_p,
            jnp.zeros((bp, POOL_STATE, POOL_WIDTH), dt),
            jnp.zeros((bp, MAX_WINDOW, N_HEADS, HEAD_DIM), dt),
            jnp.zeros((bp, MAX_WINDOW, N_HEADS, HEAD_DIM), dt),
            mk, mv,
            jnp.zeros((bp, CONV_WIDTH - 1, D_FF), dt), prm)
        p_pool.append(pool_st); p_k.append(kp[:, -keep_p:]); p_v.append(vp[:, -keep_p:])
        p_mk.append(mk); p_mv.append(mv); p_conv.append(conv_st)
        xs, pool_st_s, ks_new, vs_new, conv_st_s = _layer(
            xs, pos_s, state_pool[l], cache_win_k[l], cache_win_v[l],
            cache_mem_k[l], cache_mem_v[l], state_conv[l], prm)
        s_pool.append(pool_st_s); s_k.append(ks_new); s_v.append(vs_new); s_conv.append(conv_st_s)
    return (xp, xs,
            jnp.stack(p_pool), jnp.stack(p_k), jnp.stack(p_v), jnp.stack(p_mk), jnp.stack(p_mv), jnp.stack(p_conv),
            jnp.stack(s_pool), jnp.stack(s_k), jnp.stack(s_v), jnp.stack(s_conv))
```

```python
import contextlib
import numpy as np
import concourse.bass as bass
import concourse.mybir as mybir
from concourse.bass_utils import run_bass_kernel_spmd

F32 = mybir.dt.float32
BF16 = mybir.dt.bfloat16
AF = mybir.ActivationFunctionType
ALU = mybir.AluOpType
AX = mybir.AxisListType

NCORES = 8
D = 2048
KC = 16
NH = 8
PW = 1024
DFF = 5632
NFC = 44
MH = 4
OWN = 2048
HALO = 2176
EXT = HALO + OWN
SC = 32
S0 = OWN
NT = OWN + 128
PAST = 16384
EPS = 1e-6
SCALE = 128.0 ** -0.5
NEG = -30000.0
DILS = (1, 4, 16)


class Prog:
    def __init__(self, nc, stack, n_dma_sems=32):
        self.nc = nc
        self.sems = {s: stack.enter_context(nc.semaphore("s_" + s)) for s in ("pe", "act", "dve", "pool")}
        self.dsems = [stack.enter_context(nc.semaphore("d%d" % i)) for i in range(n_dma_sems)]
        self.cnt = {s: 0 for s in self.sems}
        self.dcnt = [0] * n_dma_sems
        self.dlast = [None] * n_dma_sems
        self.ndma = 0
        self.ndma_sw = 0
        self.n_sw_sems = 8
        self.known = {s: {} for s in ("pe", "act", "dve", "pool", "sp")}
        self.trace = {s: [] for s in ("pe", "act", "dve", "pool", "sp")}
        self.nstage = 0
        self.total_ops = 0
        self._reset()

    def _reset(self):
        self.ops = []
        self.last_writer = {}
        self.readers = {}

    def add(self, eng, fn, reads=(), writes=(), dma=False):
        idx = len(self.ops)
        deps = set()
        lw = self.last_writer
        for r in reads:
            w = lw.get(r)
            if w is not None:
                deps.add(w)
        for r in writes:
            w = lw.get(r)
            if w is not None:
                deps.add(w)
            rd = self.readers.get(r)
            if rd:
                deps.update(rd)
        deps.discard(idx)
        self.ops.append([eng, fn, deps, dma, False, None, None])
        for r in writes:
            lw[r] = idx
            self.readers[r] = []
        for r in reads:
            self.readers.setdefault(r, []).append(idx)
        return idx

    def flush(self):
        nc = self.nc
        ops = self.ops
        stream_of = {"pe": "pe", "act": "act", "dve": "dve", "pool": "pool", "sp": "sp", "pq": "pool", "aq": "act"}
        streams = {s: [] for s in ("pe", "act", "dve", "pool", "sp")}
        for i, o in enumerate(ops):
            streams[stream_of[o[0]]].append(i)
        for i, o in enumerate(ops):
            so = stream_of[o[0]]
            keep = []
            best = {}
            for d in o[2]:
                od = ops[d]
                if od[3]:
                    keep.append(d)
                    continue
                sd = stream_of[od[0]]
                if sd == so and not o[3] and so == "pe":
                    continue
                if d > best.get(sd, -1):
                    best[sd] = d
            for sd, d in best.items():
                keep.append(d)
                ops[d][4] = True
            o[2] = sorted(keep)
        for s in ("pe", "act", "dve", "pool"):
            for i in reversed(streams[s]):
                if not ops[i][3]:
                    ops[i][4] = True
                    break
        nd = len(self.dsems)
        nsw = self.n_sw_sems
        nhw = nd - nsw
        for i, o in enumerate(ops):
            if o[3]:
                if o[0] == "pq":
                    k = nhw + (self.ndma_sw % nsw)
                    self.ndma_sw += 1
                else:
                    k = self.ndma % nhw
                    self.ndma += 1
                self.dcnt[k] += 16
                o[5] = ("d", k, self.dcnt[k])
                o[6] = self.dlast[k]
                self.dlast[k] = o[5]
            elif o[4]:
                s = stream_of[o[0]]
                self.cnt[s] += 1
                o[5] = ("c", s, self.cnt[s])
        finals = [("c", s, self.cnt[s]) for s in self.sems if self.cnt[s]]
        finals += [("d", k, self.dcnt[k]) for k in range(nd) if self.dcnt[k]]

        def run_stream(sname, eng):
            known = self.known[sname]

            def wait_tok(tok):
                key = (tok[0], tok[1])
                if known.get(key, 0) >= tok[2]:
                    return
                known[key] = tok[2]
                self.trace[sname].append(("w", key, tok[2]))
                sem = self.dsems[tok[1]] if tok[0] == "d" else self.sems[tok[1]]
                eng.wait_ge(sem, tok[2])

            for i in streams[sname]:
                o = ops[i]
                for d in o[2]:
                    tok = ops[d][5]
                    if tok is not None:
                        wait_tok(tok)
                if o[3] and o[6] is not None:
                    wait_tok(o[6])
                ins = o[1](eng)
                tok = o[5]
                self.trace[sname].append(("o", None if tok is None else (tok[0], tok[1]), 16 if (tok and tok[0] == "d") else 1))
                if tok is not None:
                    if tok[0] == "d":
                        ins.then_inc(self.dsems[tok[1]], 16)
                    else:
                        ins.then_inc(self.sems[tok[1]], 1)
            for tok in finals:
                if not (tok[0] == "c" and tok[1] == sname):
                    wait_tok(tok)

        with nc.Block() as block:
            @block.tensor
            def _(e):
                run_stream("pe", e)

            @block.scalar
            def _(e):
                run_stream("act", e)

            @block.vector
            def _(e):
                run_stream("dve", e)

            @block.gpsimd
            def _(e):
                run_stream("pool", e)

            @block.sync
            def _(e):
                run_stream("sp", e)
        self.total_ops += len(ops)
        self.nstage += 1
        self._reset()


class Ring:
    def __init__(self, items):
        self.items = list(items)
        self.i = 0

    def next(self):
        x = self.items[self.i % len(self.items)]
        self.i += 1
        return x


class Builder:
    def __init__(self, debug=False, stop_after=None):
        self.debug = debug
        self.stop_after = stop_after
        self.nc = bass.Bass("TRN2", target_bir_lowering=False)
        self.din = {}
        self.dout = {}
        self._uid = 0

    def u(self):
        self._uid += 1
        return "_u%d" % self._uid

    def inp(self, name, shape):
        t = self.nc.dram_tensor(name, list(shape), F32, kind="ExternalInput").ap()
        self.din[name] = t
        return t

    def outp(self, name, shape):
        t = self.nc.dram_tensor(name, list(shape), F32, kind="ExternalOutput").ap()
        self.dout[name] = t
        return t

    def scratch(self, name, shape, dt):
        kind = "ExternalOutput" if self.debug else "Internal"
        t = self.nc.dram_tensor(name, list(shape), dt, kind=kind).ap()
        if self.debug:
            self.dout[name] = t
        return t

    def sb(self, st, name, shape, dt):
        return st.enter_context(self.nc.sbuf_tensor("sb_" + name, list(shape), dt))

    def psb(self, st, name, n=1, dt=F32, cols=512):
        cols = 1024 if dt == BF16 else 512
        return [st.enter_context(self.nc.psum_tensor("ps_%s%d" % (name, i), [128, cols], dt)) for i in range(n)]

    def mm(self, out, lhsT, rhs, start, stop, reads, writes):
        self.P.add("pe", lambda e: e.matmul(out, lhsT, rhs, start=start, stop=stop), reads, writes)

    def tr(self, out, in_, ident, reads, writes):
        self.P.add("pe", lambda e: e.transpose(out, in_, ident), reads, writes)

    def act(self, out, in_, func, reads, writes, scale=None, bias=None, accum=None):
        kw = {}
        if scale is not None:
            kw["scale"] = scale
        if bias is not None:
            kw["bias"] = bias
        if accum is not None:
            kw["accum_out"] = accum
        self.P.add("act", lambda e: e.activation(out=out, in_=in_, func=func, **kw), reads, writes)

    def cp(self, eng, out, in_, reads, writes):
        if eng == "act":
            self.P.add("act", lambda e: e.copy(out, in_), reads, writes)
        else:
            self.P.add(eng, lambda e: e.tensor_copy(out, in_), reads, writes)

    def tt(self, out, in0, in1, op, reads, writes, eng="dve"):
        self.P.add(eng, lambda e: e.tensor_tensor(out=out, in0=in0, in1=in1, op=op), reads, writes)

    def ts(self, out, in0, s1, s2, op0, op1, reads, writes, eng="dve"):
        if op1 is None:
            self.P.add(eng, lambda e: e.tensor_scalar(out=out, in0=in0, scalar1=s1, scalar2=None, op0=op0), reads, writes)
        else:
            self.P.add(eng, lambda e: e.tensor_scalar(out=out, in0=in0, scalar1=s1, scalar2=s2, op0=op0, op1=op1), reads, writes)

    def stt(self, out, in0, scalar, in1, op0, op1, reads, writes):
        self.P.add("dve", lambda e: e.scalar_tensor_tensor(out=out, in0=in0, scalar=scalar, in1=in1, op0=op0, op1=op1), reads, writes)

    def red(self, out, in_, op, reads, writes, absv=None):
        self.P.add("dve", lambda e: e.tensor_reduce(out=out, in_=in_, axis=AX.X, op=op, apply_absolute_value=absv), reads, writes)

    def dma(self, out, in_, reads, writes, q="sp"):
        self.P.add(q, lambda e: e.dma_start(out=out, in_=in_), reads, writes, dma=True)

    def memset(self, ap, val, writes, eng="pool"):
        self.P.add(eng, lambda e: e.memset(ap, val), (), writes)

    def build(self):
        nc = self.nc
        with contextlib.ExitStack() as g:
            self.P = Prog(nc, g)
            self.declare_dram()
            self.alloc_global(g)
            self.stage_consts()
            stages = [self.stage_proj, self.stage_pool, self.stage_attn, self.stage_attn_s,
                      self.stage_wout, self.stage_mem, self.stage_ffn_a]
            done = False
            with contextlib.ExitStack() as g1:
                self.arena1 = self.sb(g1, "arena1", [128, KC, NT], BF16)
                import os
                skip = os.environ.get("SKIP_TO")
                for fn in stages:
                    if skip and fn.__name__ != skip:
                        continue
                    skip = None
                    fn()
                    if self.debug:
                        dd = self.nc.dram_tensor("dbg_" + fn.__name__, [128, KC, NT], BF16, kind="ExternalOutput").ap()
                        self.dout["dbg_" + fn.__name__] = dd
                        self.dma(dd[:, :, :], self.arena1[:], [], [self.u()])
                        self.P.flush()
                    if self.stop_after == fn.__name__:
                        done = True
                        break
            if not done:
                self.stage_ffn_b()
        return nc

    def declare_dram(self):
        i = self.inp
        self.xall = i("xall", [EXT, D])
        self.xS = i("xS", [128, D])
        self.cosT = i("cosT", [128, EXT])
        self.sinT = i("sinT", [128, EXT])
        self.cosS = i("cosS", [128, SC])
        self.sinS = i("sinS", [128, SC])
        self.hv_d = i("hv", [128, 2])
        self.corr_d = i("corr", [128, 4 * 16])
        self.gmask_d = i("gmask", [128, 2 * 24])
        self.w_in = i("w_in", [D, 4096])
        self.w_pool = i("w_pool", [4, 256, 256])
        self.w_out = i("w_out", [D, D])
        self.w_mem_q = i("w_mem_q", [D, 512])
        self.w_mem_k = i("w_mem_k", [D, 512])
        self.w_mem_v = i("w_mem_v", [D, 512])
        self.w_mem_o = i("w_mem_o", [512, D])
        self.w_gate = i("w_gate", [D, DFF])
        self.w_up = i("w_up", [D, DFF])
        self.w_down = i("w_down", [DFF, D])
        self.norm_mix = i("norm_mix", [1, D])
        self.norm_mem = i("norm_mem", [1, D])
        self.norm_mem_src = i("norm_mem_src", [1, D])
        self.norm_ffn = i("norm_ffn", [1, D])
        self.q_norm = i("q_norm", [1, 128])
        self.k_norm = i("k_norm", [1, 128])
        self.mem_q_norm = i("mem_q_norm", [1, 128])
        self.mem_k_norm = i("mem_k_norm", [1, 128])
        self.pool_scale = i("pool_scale", [1, PW])
        self.conv_w = i("conv_w", [3, DFF])
        self.conv_b = i("conv_b", [1, DFF])
        self.mem_prompt = i("mem_prompt", [256, D])
        self.state_pool = i("state_pool", [4, 15, PW])
        self.cwk = i("cwk", [4, 2048, 1024])
        self.cwv = i("cwv", [4, 2048, 1024])
        self.cmk = i("cmk", [4, 256, 512])
        self.cmv = i("cmv", [4, 256, 512])
        self.state_conv = i("state_conv", [4, 2, DFF])
        o = self.outp
        self.y = o("y", [OWN, D])
        self.yS = o("yS", [16, D])
        self.p_pool = o("p_pool", [15, PW])
        self.pk = o("pk", [OWN, 1024])
        self.pv = o("pv", [OWN, 1024])
        self.pmk = o("pmk", [256, 512])
        self.pmv = o("pmv", [256, 512])
        self.pconv = o("pconv", [2, DFF])
        self.s_pool = o("s_pool", [4, 15, PW])
        self.s_k = o("s_k", [16, 1024])
        self.s_v = o("s_v", [16, 1024])
        self.s_conv = o("s_conv", [4, 2, DFF])
        s = self.scratch
        self.uT_s = s("uT_s", [8, 128, 128 + OWN], F32)
        self.kT_s = s("kT_s", [NH, 128, EXT], BF16)
        self.vT_s = s("vT_s", [NH, 128, EXT], BF16)
        self.qT_s = s("qT_s", [NH, 128, OWN], BF16)
        self.kh_s = s("kh_s", [HALO, 1024], F32)
        self.vh_s = s("vh_s", [HALO, 1024], F32)
        self.qS_s = s("qS_s", [SC, 1024], F32)
        self.kS_s = s("kS_s", [SC, 1024], F32)
        self.vS_s = s("vS_s", [SC, 1024], F32)
        self.x1_s = s("x1_s", [OWN + 128, D], F32)
        self.x2_s = s("x2_s", [OWN + 128, D], F32)
        self.a_s = s("a_s", [NFC, 128, NT], BF16)

    def alloc_global(self, g):
        sb = self.sb
        self.identb = sb(g, "identb", [128, 128], BF16)
        self.identf = sb(g, "identf", [128, 128], F32)
        self.onesb = sb(g, "onesb", [128, 128], BF16)
        self.rotb = sb(g, "rotb", [128, 128], BF16)
        self.negA = sb(g, "negA", [128, 128], BF16)
        self.negB = sb(g, "negB", [128, 128], BF16)
        self.negAh = sb(g, "negAh", [128, 128], BF16)
        self.hv = sb(g, "hv", [128, 2], F32)
        self.mask2 = sb(g, "mask2", [128, 256], BF16)
        self.mask2h = sb(g, "mask2h", [128, 256], BF16)
        self.colv = sb(g, "colv", [128, 8], F32)
        self.rowv = sb(g, "rowv", [1, 4 * 128 + 8], F32)
        self.pscale = sb(g, "pscale", [128, 8], F32)
        self.cw = sb(g, "cw", [128, NFC, 3], F32)
        self.cb = sb(g, "cb", [128, NFC], F32)
        self.uT_S = sb(g, "uT_S", [128, 8, SC], F32)
        self.qT_S = sb(g, "qT_S", [128, 8, SC], F32)
        self.kT_S = sb(g, "kT_S", [128, 8, SC], F32)
        self.vT_S = sb(g, "vT_S", [128, 8, SC], F32)

    def stage_consts(self):
        nc = self.nc
        P = self.P
        with contextlib.ExitStack() as st:
            tmpf = self.sb(st, "c_tmpf", [128, 128], F32)
            tmp2 = self.sb(st, "c_tmp2", [128, 128], F32)
            cwtA = self.sb(st, "c_cwtA", [128, 128], F32)
            cwtB = self.sb(st, "c_cwtB", [64, 128], F32)
            ps = self.psb(st, "c_ps", 1)[0]
            self.memset(self.identf[:], 1.0, ["identf"])
            P.add("pool", lambda e: e.affine_select(out=self.identf[:], in_=self.identf[:], pattern=[[-1, 128]],
                                                    compare_op=ALU.is_equal, fill=0.0, base=0, channel_multiplier=1),
                  ["identf"], ["identf"])
            self.cp("dve", self.identb[:], self.identf[:], ["identf"], ["identb"])
            self.memset(self.onesb[:], 1.0, ["onesb"])
            self.memset(tmpf[:], 1.0, ["tmpf"])
            P.add("pool", lambda e: e.affine_select(out=tmpf[:], in_=tmpf[:], pattern=[[-1, 128]],
                                                    compare_op=ALU.is_equal, fill=0.0, base=64, channel_multiplier=1),
                  ["tmpf"], ["tmpf"])
            self.memset(tmp2[:], 1.0, ["tmp2"])
            P.add("pool", lambda e: e.affine_select(out=tmp2[:], in_=tmp2[:], pattern=[[-1, 128]],
                                                    compare_op=ALU.is_equal, fill=0.0, base=-64, channel_multiplier=1),
                  ["tmp2"], ["tmp2"])
            self.tt(self.rotb[:], tmpf[:], tmp2[:], ALU.add, ["tmpf", "tmp2"], ["rotb"])
            self.memset(tmpf[:], 0.0, ["tmpf"])
            P.add("pool", lambda e: e.affine_select(out=tmpf[:], in_=tmpf[:], pattern=[[-1, 128]],
                                                    compare_op=ALU.is_ge, fill=NEG, base=0, channel_multiplier=1),
                  ["tmpf"], ["tmpf"])
            self.cp("dve", self.negA[:], tmpf[:], ["tmpf"], ["negA"])
            self.dma(self.hv[:], self.hv_d[:, :], [], ["hv"])
            self.ts(self.negAh[:], tmpf[:], self.hv[:, 1:2], None, ALU.add, None, ["tmpf", "hv"], ["negAh"])
            self.memset(tmp2[:], 0.0, ["tmp2"])
            P.add("pool", lambda e: e.affine_select(out=tmp2[:], in_=tmp2[:], pattern=[[1, 128]],
                                                    compare_op=ALU.is_ge, fill=NEG, base=0, channel_multiplier=-1),
                  ["tmp2"], ["tmp2"])
            self.cp("dve", self.negB[:], tmp2[:], ["tmp2"], ["negB"])
            self.ts(self.mask2[:, 0:128], tmpf[:], 0.0, None, ALU.is_equal, None, ["tmpf"], ["mask2"])
            self.ts(self.mask2[:, 128:256], tmp2[:], 0.0, None, ALU.is_equal, None, ["tmp2"], ["mask2"])
            self.ts(self.mask2h[:, 0:128], self.mask2[:, 0:128], self.hv[:, 0:1], None, ALU.mult, None, ["mask2", "hv"], ["mask2h"])
            self.cp("dve", self.mask2h[:, 128:256], self.mask2[:, 128:256], ["mask2"], ["mask2h"])
            cwr = self.conv_w.rearrange("j (c p) -> (j c) p", p=128)
            self.dma(cwtA[:, :], cwr[0:128, :], [], ["cwt"])
            self.dma(cwtB[0:4, :], cwr[128:132, :], [], ["cwt"])
            self.dma(cwtB[4:48, :], self.conv_b.rearrange("o (c p) -> (o c) p", p=128), [], ["cwt"])
            self.dma(cwtB[48:56, :], self.pool_scale.rearrange("o (c p) -> (o c) p", p=128), [], ["cwt"])
            b0 = NFC * 4 + 8
            for j, v in enumerate((self.q_norm, self.k_norm, self.mem_q_norm, self.mem_k_norm)):
                self.dma(cwtB[56 + j:57 + j, :], v[0:1, :], [], ["cwt"])
            self.tr(ps[:, 0:128], cwtA[:, :], self.identf[:], ["cwt", "identf"], ["cps"])
            self.tr(ps[:, 128:188], cwtB[0:60, :], self.identf[0:60, 0:60], ["cwt", "identf"], ["cps"])
            self.cp("dve", self.cw[:].rearrange("p c j -> p j c"), ps[:, 0:NFC * 3].rearrange("p (j c) -> p j c", j=3),
                    ["cps"], ["cw"])
            self.cp("dve", self.cb[:], ps[:, NFC * 3:NFC * 4], ["cps"], ["cb"])
            self.cp("dve", self.pscale[:], ps[:, NFC * 4:NFC * 4 + 8], ["cps"], ["pscale"])
            self.cp("dve", self.colv[:, 0:3], ps[:, b0:b0 + 3], ["cps"], ["colv"])
            rv = self.rowv
            for j, v in enumerate((self.q_norm, self.k_norm, self.mem_q_norm, self.mem_k_norm)):
                self.dma(rv[0:1, j * 128:(j + 1) * 128], v[0:1, :], [], ["rowv"])
            m0 = 512
            self.red(rv[0:1, m0:m0 + 4], rv[0:1, 0:512].rearrange("o (a b) -> o a b", a=4), ALU.max, ["rowv"], ["rowm"], absv=True)
            self.tt(rv[0:1, m0 + 4:m0 + 5], rv[0:1, m0:m0 + 1], rv[0:1, m0 + 1:m0 + 2], ALU.mult, ["rowm"], ["rowb"])
            self.tt(rv[0:1, m0 + 5:m0 + 6], rv[0:1, m0 + 2:m0 + 3], rv[0:1, m0 + 3:m0 + 4], ALU.mult, ["rowm"], ["rowb"])
            self.ts(rv[0:1, m0 + 6:m0 + 8], rv[0:1, m0 + 4:m0 + 6], -(128.0 ** 0.5), None, ALU.mult, None, ["rowb"], ["rowc"])
            self.memset(tmp2[0:1, :], 1.0, ["tmp2"], eng="dve")
            self.mm(ps[:, 256:258], tmp2[0:1, :], rv[0:1, m0 + 6:m0 + 8], True, True, ["rowc", "tmp2"], ["cps2"])
            self.cp("dve", self.colv[:, 3:5], ps[:, 256:258], ["cps2"], ["colv2"])
            P.flush()

    def norm_transpose(self, src_ap, src_reads, gain_b, gname, hT, col0, hname, bufs, i):
        xt, xn, stat, ptr = bufs
        b = i % 2
        X = "nt_x%d" % b
        XN = "nt_xn%d" % b
        STt = "nt_st%d" % b
        if src_ap is not None:
            self.dma(xt[b][:], src_ap, src_reads, [X])
        self.act(xn[b][:], xt[b][:], AF.Square, [X], [XN, STt + "a"], accum=stat[b][:, 0:1])
        self.act(stat[b][:, 1:2], stat[b][:, 0:1], AF.Ln, [STt + "a"], [STt + "b"], scale=1.0 / D, bias=EPS)
        self.act(stat[b][:, 2:3], stat[b][:, 1:2], AF.Exp, [STt + "b"], [STt + "c"], scale=-0.5)
        self.stt(xn[b][:], xt[b][:], stat[b][:, 2:3], gain_b[:], ALU.mult, ALU.mult, [X, STt + "c", gname], [XN])
        def trans_part():
            for j in range(4):
                pt = ptr.next()
                for q in range(4):
                    kc = 4 * j + q
                    self.tr(pt[1][:, q * 128:(q + 1) * 128], xn[b][:, kc * 128:(kc + 1) * 128], self.identb[:],
                            [XN, "identb"], [pt[0]])
                eng = "act" if j % 2 == 0 else "dve"
                self.cp(eng, hT[:, 4 * j:4 * j + 4, col0:col0 + 128], pt[1][:, 0:512].rearrange("p (a b) -> p a b", a=4),
                        [pt[0]], [hname])

        self.nt_flush()
        self._nt_pend = trans_part

    def nt_flush(self):
        p = getattr(self, "_nt_pend", None)
        self._nt_pend = None
        if p is not None:
            p()

    def nt_bufs(self, st, tag):
        xt = [self.sb(st, "%s_xt%d" % (tag, i), [128, D], F32) for i in range(2)]
        xn = [self.sb(st, "%s_xn%d" % (tag, i), [128, D], BF16) for i in range(2)]
        stat = [self.sb(st, "%s_st%d" % (tag, i), [128, 4], F32) for i in range(2)]
        pts = self.psb(st, tag + "_pt", 2, BF16, 512)
        ptr = Ring([("nt_pt%d" % i, pts[i]) for i in range(2)])
        return xt, xn, stat, ptr

    def load_wblock(self, slot_ap, slot_name, w_ap, c0, ncols, kchunks=KC):
        src = w_ap[:, c0:c0 + ncols].rearrange("(k p) n -> p k n", p=128)
        self.dma(slot_ap[:, 0:kchunks, 0:ncols], src, [], [slot_name], q="pq")

    def stage_proj(self):
        P = self.P
        with contextlib.ExitStack() as st:
            sb = self.sb
            hT = self.arena1
            bufs = self.nt_bufs(st, "pj")
            gain_b = sb(st, "pj_gain", [128, D], F32)
            wsl = [sb(st, "pj_w%d" % i, [128, KC, 512], BF16) for i in range(2)]
            cosT = sb(st, "pj_cos", [128, HALO], F32)
            sinT = sb(st, "pj_sin", [128, HALO], F32)
            nb = 3
            pipe = [None, None]
            sqb = [sb(st, "pj_sq%d" % i, [128, 512], BF16) for i in range(nb)]
            t1f = [sb(st, "pj_t1f%d" % i, [128, 512], F32) for i in range(nb)]
            t1b = [sb(st, "pj_t1b%d" % i, [128, 512], BF16) for i in range(nb)]
            lnv = [sb(st, "pj_ln%d" % i, [128, 512], F32) for i in range(nb)]
            ta = [sb(st, "pj_ta%d" % i, [128, 512], F32) for i in range(nb)]
            tb = [sb(st, "pj_tb%d" % i, [128, 512], F32) for i in range(nb)]
            o32 = [sb(st, "pj_o32%d" % i, [128, 512], F32) for i in range(nb)]
            o16 = [sb(st, "pj_o16%d" % i, [128, 512], BF16) for i in range(nb)]
            otk = [sb(st, "pj_otk%d" % i, [128, 512], F32) for i in range(nb)]
            raw_ps = self.psb(st, "pj_raw", 2)
            ssq_ps = self.psb(st, "pj_ssq", 1)[0]
            rot_ps = self.psb(st, "pj_rot", 1)[0]
            tok_ps = self.psb(st, "pj_tok", 2)
            rawr = Ring([("pj_raw%d" % i, raw_ps[i]) for i in range(2)])
            tokr = Ring([("pj_tok%d" % i, tok_ps[i]) for i in range(2)])
            self.dma(gain_b[:], self.norm_mix[0:1, :].to_broadcast([128, D]), [], ["gain"])
            ucount = [0]

            def do_norm(pname, i):
                e0_ = 0 if pname == "H" else HALO
                src = self.xS[:, :] if (pname == "O" and i == 16) else self.xall[e0_ + i * 128:e0_ + (i + 1) * 128, :]
                self.norm_transpose(src, [], gain_b, "gain", hT, i * 128, "hT%d" % i, bufs, i)

            def run_pass(pname, pre_normed=False, tail_hook=None):
                if pname == "H":
                    ntile, e0 = 17, 0
                    groups = [(0, 512), (512, 512), (1024, 512), (1536, 512), (2048, 128)]
                    self.dma(cosT[:, 0:HALO], self.cosT[:, 0:HALO], [], ["cos"])
                    self.dma(sinT[:, 0:HALO], self.sinT[:, 0:HALO], [], ["sin"])
                elif pname == "O":
                    ntile, e0 = 17, HALO
                    groups = [(0, 512), (512, 512), (1024, 512), (1536, 512), (OWN, SC)]
                    self.dma(cosT[:, 0:OWN], self.cosT[:, HALO:EXT], [], ["cos"])
                    self.dma(sinT[:, 0:OWN], self.sinT[:, HALO:EXT], [], ["sin"])
                    self.dma(cosT[:, OWN:OWN + SC], self.cosS[:, :], [], ["cos"])
                    self.dma(sinT[:, OWN:OWN + SC], self.sinS[:, :], [], ["sin"])
                else:
                    ntile, e0 = 1, 0
                    groups = [(0, SC)]
                    self.dma(cosT[:, 0:SC], self.cosS[:, :], [], ["cos"])
                    self.dma(sinT[:, 0:SC], self.sinS[:, :], [], ["sin"])
                if not pre_normed:
                    for i in range(ntile):
                        do_norm(pname, i)
                    self.nt_flush()
                if pname == "H":
                    blocks = [0, 1, 4, 5, 6, 7]
                else:
                    blocks = list(range(8))
                self.load_wblock(wsl[0], "pj_w0", self.w_in, blocks[0] * 512, 512)
                for bi, blk in enumerate(blocks):
                    if bi + 1 < len(blocks):
                        s2 = (bi + 1) % 2
                        self.load_wblock(wsl[s2], "pj_w%d" % s2, self.w_in, blocks[bi + 1] * 512, 512)
                    sl = bi % 2
                    kind = "uqkv"[blk // 2]
                    tail = (tail_hook is not None and bi == len(blocks) - 1)
                    if tail:
                        order = [(un, g) for g in groups for un in range(4)]
                    else:
                        order = [(un, g) for un in range(4) for g in groups]
                    for (un, (c0, n)) in order:
                        unit = (blk % 2) * 4 + un
                        if True:
                            if pname == "H" and kind == "u" and c0 != 2048:
                                continue
                            rn, rp = rawr.next()
                            tiles = ["hT%d" % t for t in range(c0 // 128, (c0 + n + 127) // 128)]
                            for kc in range(KC):
                                self.mm(rp[:, 0:n], wsl[sl][:, kc, un * 128:(un + 1) * 128], hT[:, kc, c0:c0 + n],
                                        kc == 0, kc == KC - 1, ["pj_w%d" % sl] + tiles, [rn])
                            u = ucount[0] % nb
                            ucount[0] += 1
                            ph = make_phases("S" if (pname == "O" and c0 == OWN) else pname, kind, unit, c0, n, rn, rp, u, e0)
                            ph[0]()
                            if pipe[0] is not None:
                                pipe[0][1]()
                            if pipe[1] is not None:
                                pipe[1][2]()
                            pipe[1] = pipe[0]
                            pipe[0] = ph
                            if tail and un == 3:
                                tail_hook(c0, n)
                if pipe[0] is not None:
                    pipe[0][1]()
                if pipe[1] is not None:
                    pipe[1][2]()
                if pipe[0] is not None:
                    pipe[0][2]()
                pipe[0] = pipe[1] = None

            def make_phases(pname, kind, unit, c0, n, rn, rp, u, e0):
                U = "pj_u%d_" % u
                nop = lambda: None

                def tok_major():
                    nt_ = (n + 127) // 128
                    tn, tp = tokr.next()
                    for t in range(nt_):
                        w = min(128, n - t * 128)
                        self.tr(tp[0:w, t * 128:(t + 1) * 128], o32[u][:, t * 128:t * 128 + w], self.identf[:],
                                [U + "o32", "identf"], [tn])
                    if pname == "S":
                        self.cp("act", otk[u][0:SC, 0:128], tp[0:SC, 0:128], [tn], [U + "otk"])
                        dd = {"q": self.qS_s, "k": self.kS_s, "v": self.vS_s}[kind]
                        self.dma(dd[:, unit * 128:(unit + 1) * 128], otk[u][0:SC, 0:128], [U + "otk"], [self.u()])
                        if kind in "kv":
                            do = self.s_k if kind == "k" else self.s_v
                            self.dma(do[:, unit * 128:(unit + 1) * 128], otk[u][0:16, 0:128], [U + "otk"], [self.u()])
                    else:
                        self.cp("act", otk[u][:, 0:nt_ * 128], tp[:, 0:nt_ * 128], [tn], [U + "otk"])
                        if pname == "H":
                            dd = self.kh_s if kind == "k" else self.vh_s
                        else:
                            dd = self.pk if kind == "k" else self.pv
                        dst = dd[c0:c0 + nt_ * 128, unit * 128:(unit + 1) * 128].rearrange("(t p) d -> p t d", p=128)
                        self.dma(dst, otk[u][:, 0:nt_ * 128].rearrange("p (t d) -> p t d", t=nt_), [U + "otk"], [self.u()])

                def feat_major():
                    if pname == "S":
                        dst = {"q": self.qT_S, "k": self.kT_S, "v": self.vT_S}[kind]
                        self.cp("dve", dst[:, unit, :], o32[u][:, 0:n], [U + "o32"], [kind + "T_S"])
                    elif kind == "q":
                        self.dma(self.qT_s[unit, :, c0:c0 + n], o16[u][:, 0:n], [U + "o16"], [self.u()])
                    else:
                        dsts = self.kT_s if kind == "k" else self.vT_s
                        self.dma(dsts[unit, :, e0 + c0:e0 + c0 + n], o16[u][:, 0:n], [U + "o16"], [self.u()])

                if kind == "u":
                    def a0():
                        self.cp("act", o32[u][:, 0:n], rp[:, 0:n], [rn], [U + "o32"])

                    def b_():
                        if pname == "S":
                            self.cp("dve", self.uT_S[:, unit, :], o32[u][:, 0:n], [U + "o32"], ["uT_S"])
                        else:
                            dc0 = 0 if pname == "H" else 128 + c0
                            self.dma(self.uT_s[unit, :, dc0:dc0 + n], o32[u][:, 0:n], [U + "o32"], [self.u()])
                    return (a0, nop, b_)
                if kind == "v":
                    def a0():
                        self.cp("act", o32[u][:, 0:n], rp[:, 0:n], [rn], [U + "o32"])

                    def a1():
                        self.cp("dve", o16[u][:, 0:n], o32[u][:, 0:n], [U + "o32"], [U + "o16"])

                    def b_():
                        feat_major()
                        tok_major()
                    return (a0, a1, b_)
                gcol = self.colv[:, 0:1] if kind == "q" else self.colv[:, 1:2]

                def a0():
                    self.act(sqb[u][:, 0:n], rp[:, 0:n], AF.Square, [rn], [U + "sq"])
                    self.act(t1f[u][:, 0:n], rp[:, 0:n], AF.Copy, [rn, "colv"], [U + "t1f"], scale=gcol)
                    self.cp("dve", t1b[u][:, 0:n], t1f[u][:, 0:n], [U + "t1f"], [U + "t1b"])

                def a1():
                    self.mm(ssq_ps[:, 0:n], self.onesb[:], sqb[u][:, 0:n], True, True, [U + "sq", "onesb"], ["pj_ssq"])
                    self.mm(rot_ps[:, 0:n], self.rotb[:], t1b[u][:, 0:n], True, True, [U + "t1b", "rotb"], ["pj_rot"])
                    self.act(lnv[u][:, 0:n], ssq_ps[:, 0:n], AF.Ln, ["pj_ssq"], [U + "ln"], scale=1.0 / 128, bias=EPS)
                    self.act(lnv[u][:, 0:n], lnv[u][:, 0:n], AF.Exp, [U + "ln"], [U + "ln"], scale=-0.5)
                    self.tt(ta[u][:, 0:n], t1f[u][:, 0:n], cosT[:, c0:c0 + n], ALU.mult, [U + "t1f", "cos"], [U + "ta"])
                    self.tt(tb[u][:, 0:n], rot_ps[:, 0:n], sinT[:, c0:c0 + n], ALU.mult, ["pj_rot", "sin"], [U + "tb"])
                    self.tt(ta[u][:, 0:n], ta[u][:, 0:n], tb[u][:, 0:n], ALU.add, [U + "ta", U + "tb"], [U + "ta"])
                    if kind == "q" and pname != "S":
                        self.tt(o16[u][:, 0:n], ta[u][:, 0:n], lnv[u][:, 0:n], ALU.mult, [U + "ta", U + "ln"], [U + "o16"])
                    else:
                        self.tt(o32[u][:, 0:n], ta[u][:, 0:n], lnv[u][:, 0:n], ALU.mult, [U + "ta", U + "ln"], [U + "o32"])

                def b_():
                    if kind == "k" and pname != "S":
                        self.cp("act", o16[u][:, 0:n], o32[u][:, 0:n], [U + "o32"], [U + "o16"])
                    feat_major()
                    if kind == "k" or pname == "S":
                        tok_major()
                return (a0, a1, b_)

            def hook(c0, n):
                for t in range(c0 // 128, (c0 + n + 127) // 128):
                    do_norm("O", t)

            run_pass("H")
            run_pass("O")
            P.flush()

    def stage_pool(self):
        P = self.P
        with contextlib.ExitStack() as st:
            sb = self.sb
            mixedT = self.arena1
            L = 16 + OWN
            ue = [sb(st, "pl_ue%d" % i, [128, L], F32) for i in range(2)]
            pa = sb(st, "pl_a", [128, L], F32)
            pb = sb(st, "pl_b", [128, L], F32)
            diffT = sb(st, "pl_diff", [128, 8, OWN + SC], BF16)
            wp = sb(st, "pl_wp", [128, 8, 256], BF16)
            corr = sb(st, "pl_corr", [128, 4, 16], F32)
            ppl = sb(st, "pl_pp", [128, PW], F32)
            ps = self.psb(st, "pl_ps", 2)
            tps = self.psb(st, "pl_tps", 2)
            psr = Ring([("pl_ps%d" % i, ps[i]) for i in range(2)])
            for g_ in range(4):
                self.dma(wp[:, 2 * g_:2 * g_ + 2, :], self.w_pool[g_].rearrange("(i p) o -> p i o", p=128), [], ["wp"], q="pq")
            self.dma(corr[:].rearrange("p g t -> p (g t)"), self.corr_d[:, :], [], ["corr"])
            self.memset(mixedT[:, :, S0:S0 + 128], 0.0, ["mixS"], eng="dve")
            for c in range(8):
                g_ = c // 2
                w = 2 << g_
                b = c % 2
                UE = "pl_ue%d" % b
                self.dma(ue[b][:], self.uT_s[c, :, 112:128 + OWN], [], [UE])
                cur, curname = ue[b], UE
                shift = 1
                pp = [(pa, "pl_a"), (pb, "pl_b")]
                k = 0
                lo = 0
                while shift < w:
                    dst, dname = pp[k % 2]
                    lo += shift
                    self.tt(dst[:, lo:L], cur[:, lo:L], cur[:, lo - shift:L - shift], ALU.add, [curname], [dname])
                    cur, curname = dst, dname
                    shift *= 2
                    k += 1
                self.tt(cur[:, 16:32], cur[:, 16:32], corr[:, g_, :], ALU.mult, [curname, "corr"], [curname])
                self.stt(diffT[:, c, 0:OWN], cur[:, 16:L], 1.0 / w, ue[b][:, 16:L], ALU.mult, ALU.subtract,
                         [curname, UE], ["diff%d" % c])
                tpn = "pl_tps%d" % (c // 4)
                self.tr(tps[c // 4][:, (c % 4) * 128:(c % 4 + 1) * 128], ue[b][:, L - 128:L], self.identf[:], [UE, "identf"], [tpn])
                if c % 4 == 3:
                    self.cp("act", ppl[:, (c // 4) * 512:(c // 4 + 1) * 512], tps[c // 4][:, 0:512], [tpn], ["ppl"])
            self.dma(self.p_pool[:, :], ppl[113:128, :], ["ppl"], [self.u()])
            uext = sb(st, "pl_uext", [128, 8, 5, 19], F32)
            lv = [sb(st, "pl_lv%d" % i, [128, 8, 5, 19], F32) for i in range(4)]
            sp_in = sb(st, "pl_spin", [64, PW], F32)
            stok = sb(st, "pl_stok", [SC, PW], F32)
            self.memset(uext[:], 0.0, ["uext"], eng="dve")
            for li in range(4):
                self.memset(lv[li][:], 0.0, ["pl_lv%d" % li], eng="dve")
            self.dma(sp_in[0:60, :], self.state_pool.rearrange("b t n -> (b t) n"), [], ["spin"])
            tpS = tps[0]
            for c in range(8):
                self.tr(tpS[:, c * 60:(c + 1) * 60], sp_in[0:60, c * 128:(c + 1) * 128], self.identf[0:60, 0:60],
                        ["spin", "identf"], ["pl_tps0"])
            self.cp("dve", uext[:, :, 0:4, 0:15], tpS[:, 0:480].rearrange("p (c b t) -> p c b t", c=8, b=4),
                    ["pl_tps0"], ["uext"])
            self.cp("dve", uext[:, :, 0:4, 15:19], self.uT_S[:, :, 0:16].rearrange("p c (b t) -> p c b t", b=4),
                    ["uT_S"], ["uext"])
            self.dma(uext[:, :, 4, 0:17], self.uT_s[:, :, 111:128].rearrange("c p t -> p c t"), [], ["uext"])
            cur, curname = uext, "uext"
            shift, lo = 1, 0
            for li in range(4):
                dst, dname = lv[li], "pl_lv%d" % li
                lo += shift
                self.tt(dst[:, :, :, lo:19], cur[:, :, :, lo:19], cur[:, :, :, lo - shift:19 - shift], ALU.add, [curname], [dname])
                cur, curname = dst, dname
                shift *= 2
            dS = sb(st, "pl_dS", [128, 8, 5, 19], F32)
            for g_ in range(4):
                w = 2 << g_
                cs = slice(2 * g_, 2 * g_ + 2)
                self.ts(dS[:, cs, :, :], lv[g_][:, cs, :, :], 1.0 / w, None, ALU.mult, None, ["pl_lv%d" % g_], ["pl_dS%d" % g_])
                self.tt(dS[:, cs, :, :], dS[:, cs, :, :], uext[:, cs, :, :], ALU.subtract, ["pl_dS%d" % g_, "uext"], ["pl_dS%d" % g_])
                self.cp("dve", diffT[:, cs, OWN:OWN + 16].rearrange("p c (b t) -> p c b t", b=4), dS[:, cs, 0:4, 15:19],
                        ["pl_dS%d" % g_], ["diffS"])
                self.cp("dve", diffT[:, cs, OWN + 16:OWN + 18], dS[:, cs, 4, 15:17], ["pl_dS%d" % g_], ["diffS"])
            self.memset(diffT[:, :, OWN + 18:OWN + SC], 0.0, ["diffS"], eng="dve")
            for c in range(8):
                self.tr(tps[1][0:SC, (c % 4) * 128:(c % 4 + 1) * 128], self.uT_S[:, c, :], self.identf[:], ["uT_S", "identf"], ["pl_tps1"])
                if c % 4 == 3:
                    self.cp("act", stok[0:SC, (c // 4) * 512:(c // 4 + 1) * 512], tps[1][0:SC, 0:512], ["pl_tps1"], ["stok"])
            for b_ in range(4):
                self.dma(self.s_pool[b_, 11:15, :], stok[4 * b_:4 * b_ + 4, :], ["stok"], [self.u()])
            self.dma(self.s_pool[:, 0:11, :], self.state_pool[:, 4:15, :], [], [self.u()])
            groups = [(0, 512), (512, 512), (1024, 512), (1536, 512), (OWN, SC)]
            for g_ in range(4):
                for oc in range(2):
                    for (c0, n) in groups:
                        pn, pp_ = psr.next()
                        dn = ["diffS"] if c0 == OWN else ["diff%d" % (2 * g_), "diff%d" % (2 * g_ + 1)]
                        for ic in range(2):
                            self.mm(pp_[:, 0:n], wp[:, 2 * g_ + ic, oc * 128:(oc + 1) * 128], diffT[:, 2 * g_ + ic, c0:c0 + n],
                                    ic == 0, ic == 1, ["wp"] + dn, [pn])
                        ch = 2 * g_ + oc
                        self.act(mixedT[:, ch, c0:c0 + n], pp_[:, 0:n], AF.Copy, [pn, "pscale"],
                                 ["mix%d_%d" % (ch, c0)] + (["mixS"] if c0 == OWN else []), scale=self.pscale[:, ch:ch + 1])
            P.flush()

    def stage_attn(self):
        P = self.P
        with contextlib.ExitStack() as st:
            sb = self.sb
            mixedT = self.arena1
            kT = [sb(st, "at_k%d" % i, [128, EXT], BF16) for i in range(2)]
            vT = [sb(st, "at_v%d" % i, [128, EXT], BF16) for i in range(2)]
            qT = [sb(st, "at_q%d" % i, [128, OWN], BF16) for i in range(2)]
            NVT = 17 + 20 + 32
            Vt = [sb(st, "at_vt%d" % i, [128, NVT, 128], BF16) for i in range(2)]
            ACC = [sb(st, "at_acc%d" % i, [128, 2, OWN], F32) for i in range(2)]
            Eb = [sb(st, "at_e%d" % i, [128, 256], BF16) for i in range(5)]
            tmpf = sb(st, "at_tmp", [128, OWN], F32)
            s_ps = self.psb(st, "at_s", 4)
            nd_ps = self.psb(st, "at_nd", 2)
            vt_ps = self.psb(st, "at_vp", 2, BF16, 512)
            sr = Ring([("at_s%d" % i, s_ps[i]) for i in range(4)])
            ndr = Ring([("at_nd%d" % i, nd_ps[i]) for i in range(2)])
            vpr = Ring([("at_vp%d" % i, vt_ps[i]) for i in range(2)])
            er = Ring([("at_e%d" % i, Eb[i]) for i in range(5)])
            negBias = self.colv[:, 3:4]

            def load_head(h):
                b = h % 2
                self.dma(kT[b][:], self.kT_s[h, :, :], [], ["at_k%d" % b])
                self.dma(vT[b][:], self.vT_s[h, :, :], [], ["at_v%d" % b])
                self.dma(qT[b][:], self.qT_s[h, :, :], [], ["at_q%d" % b])

            vidx = {}
            n = 0
            for d in DILS:
                for r in range(d):
                    for j in range(16 // d + 1):
                        vidx[(d, r, j)] = n
                        n += 1
            assert n == NVT
            load_head(0)
            for h in range(NH):
                b = h % 2
                if h + 1 < NH:
                    load_head(h + 1)
                KN, VN, QN, VTN, AN = "at_k%d" % b, "at_v%d" % b, "at_q%d" % b, "at_vt%d" % b, "at_acc%d" % b
                keys = sorted(vidx, key=lambda kk: vidx[kk])
                for g0 in range(0, NVT, 4):
                    pn, pp_ = vpr.next()
                    grp = keys[g0:g0 + 4]
                    for q_, (d, r, j) in enumerate(grp):
                        e0 = HALO - 128 * d + d * 128 * j + r
                        self.tr(pp_[:, q_ * 128:(q_ + 1) * 128], vT[b][:, e0:e0 + 127 * d + 1:d], self.identb[:], [VN, "identb"], [pn])
                    ng = len(grp)
                    eng = "act" if (g0 // 4) % 2 == 0 else "dve"
                    self.cp(eng, Vt[b][:, g0:g0 + ng, :], pp_[:, 0:ng * 128].rearrange("p (a b) -> p a b", a=ng), [pn],
                            [VTN + "_%d" % (g0 // 4)])
                units = [(d, r, blk) for d in DILS for r in range(d) for blk in range(16 // d)]

                def s_part(d, r, blk):
                    eA = HALO - 128 * d + d * 128 * blk + r
                    eB = eA + 128 * d
                    q0 = eB - HALO
                    kA = kT[b][:, eA:eA + 127 * d + 1:d]
                    kB = kT[b][:, eB:eB + 127 * d + 1:d]
                    qB = qT[b][:, q0:q0 + 127 * d + 1:d]
                    sn, sp_ = sr.next()
                    nA = self.negAh if blk == 0 else self.negA
                    self.mm(sp_[:, 0:128], kA, qB, True, False, [KN, QN], [sn])
                    self.mm(sp_[:, 0:128], self.identb[:], nA[:], False, True, ["identb", "negA", "negAh"], [sn])
                    self.mm(sp_[:, 128:256], kB, qB, True, False, [KN, QN], [sn])
                    self.mm(sp_[:, 128:256], self.identb[:], self.negB[:], False, True, ["identb", "negB"], [sn])
                    en, eb = er.next()
                    self.act(eb[:], sp_[:, 0:256], AF.Exp, [sn, "colv2"], [en], scale=SCALE, bias=negBias)
                    return (d, r, blk, q0, en, eb)

                def pv_part(d, r, blk, q0, en, eb):
                    ia, ib = vidx[(d, r, blk)], vidx[(d, r, blk + 1)]
                    nn, np_ = ndr.next()
                    self.mm(np_[:, 0:128], Vt[b][:, ia, :], eb[:, 0:128], True, False, [VTN + "_%d" % (ia // 4), en], [nn])
                    self.mm(np_[:, 0:128], Vt[b][:, ib, :], eb[:, 128:256], False, True, [VTN + "_%d" % (ib // 4), en], [nn])
                    self.mm(np_[:, 128:256], self.onesb[:], eb[:, 0:128], True, False, ["onesb", en], [nn])
                    self.mm(np_[:, 128:256], self.onesb[:], eb[:, 128:256], False, True, ["onesb", en], [nn])
                    dst = ACC[b][:, :, q0:q0 + 127 * d + 1:d]
                    src = np_[:, 0:256].rearrange("p (a b) -> p a b", a=2)
                    if d == 1:
                        self.cp("dve", dst, src, [nn], [AN])
                    else:
                        self.tt(dst, src, dst, ALU.add, [nn, AN], [AN])

                pend = []
                for un_ in units:
                    pend.append(s_part(*un_))
                    if len(pend) > 1:
                        pv_part(*pend.pop(0))
                while pend:
                    pv_part(*pend.pop(0))
                self.act(tmpf[:], ACC[b][:, 1, :], AF.Ln, [AN], ["at_tmp"])
                self.act(tmpf[:], tmpf[:], AF.Exp, ["at_tmp"], ["at_tmp"], scale=-1.0)
                self.tt(mixedT[:, 8 + h, 0:OWN], ACC[b][:, 0, :], tmpf[:], ALU.mult, [AN, "at_tmp"], ["mixh%d" % h])
            P.flush()

    def stage_attn_s(self):
        return

    def stage_wout(self):
        P = self.P
        with contextlib.ExitStack() as st:
            sb = self.sb
            mixedT = self.arena1
            Kg = [sb(st, "as_k%d" % i, [128, 3, 1024], F32) for i in range(2)]
            Vg = [sb(st, "as_v%d" % i, [128, 3, 1024], F32) for i in range(2)]
            qb = [sb(st, "as_q%d" % i, [128, 1024], F32) for i in range(2)]
            prod = sb(st, "as_prod", [128, 3, 1024], F32)
            prodv = sb(st, "as_prodv", [128, 24, 128], BF16)
            Sg = sb(st, "as_S", [128, 24], F32)
            Eg = sb(st, "as_E", [128, 24], F32)
            Egb = sb(st, "as_Eb", [128, 24], BF16)
            gm = sb(st, "as_gm", [128, 2, 24], F32)
            num_ps = self.psb(st, "as_num", 1)[0]
            den_ps = self.psb(st, "as_den", 1)[0]
            self_ps = self.psb(st, "as_self", 1)[0]
            negBias = self.colv[:, 3:4]
            self.dma(gm[:].rearrange("p a b -> p (a b)"), self.gmask_d[:, :], [], ["gm"])
            numv = num_ps[:, 0:8 * SC].rearrange("p (h j) -> p h j", h=8)
            denv = den_ps[:, 0:8 * SC].rearrange("p (h j) -> p h j", h=8)
            ntok = 18

            def gather(j):
                b = j % 2
                for bi, d in enumerate(DILS):
                    if j < 16:
                        bl, t = j // 4, j % 4
                        if d == 1:
                            for (G, src, newsrc, nm) in ((Kg, self.cwk, self.kS_s, "as_k%d" % b), (Vg, self.cwv, self.vS_s, "as_v%d" % b)):
                                self.dma(G[b][:, bi, :], src[bl, 1920:2048, :], [], [nm])
                                if t:
                                    self.dma(G[b][0:t, bi, :], newsrc[4 * bl:4 * bl + t, :], [], [nm])
                        else:
                            r0 = 2048 + t - 128 * d
                            for (G, src, nm) in ((Kg, self.cwk, "as_k%d" % b), (Vg, self.cwv, "as_v%d" % b)):
                                self.dma(G[b][:, bi, :], src[bl, r0:r0 + 127 * d + 1:d, :], [], [nm])
                    else:
                        t = j - 16
                        r0 = HALO - 2 + t - 128 * d
                        for (G, src, nm) in ((Kg, self.kh_s, "as_k%d" % b), (Vg, self.vh_s, "as_v%d" % b)):
                            self.dma(G[b][:, bi, :], src[r0:r0 + 127 * d + 1:d, :], [], [nm])
                self.dma(qb[b][:], self.qS_s[j:j + 1, :].to_broadcast([128, 1024]), [], ["as_q%d" % b])

            def token_step(j):
                b = j % 2
                if j + 1 < ntok:
                    gather(j + 1)
                KN, VN, QN = "as_k%d" % b, "as_v%d" % b, "as_q%d" % b
                self.tt(prod[:], Kg[b][:], qb[b][:].unsqueeze(1).to_broadcast([128, 3, 1024]), ALU.mult, [KN, QN], ["as_prod"])
                self.red(Sg[:], prod[:].rearrange("p a (h d) -> p (a h) d", h=8), ALU.add, ["as_prod"], ["as_S"])
                self.act(Eg[:], Sg[:], AF.Exp, ["as_S", "colv2"], ["as_E"], scale=SCALE, bias=negBias)
                if j >= 16:
                    self.tt(Eg[:], Eg[:], gm[:, j - 16, :], ALU.mult, ["as_E", "gm"], ["as_E"])
                self.cp("dve", Egb[:], Eg[:], ["as_E"], ["as_Eb"])
                self.tt(prodv[:], Vg[b][:].rearrange("p a (h d) -> p (a h) d", h=8), Eg[:].unsqueeze(2).to_broadcast([128, 24, 128]),
                        ALU.mult, [VN, "as_E"], ["as_prodv"])
                for h in range(NH):
                    for bi in range(3):
                        self.mm(numv[:, h, j:j + 1], prodv[:, bi * 8 + h, :], self.onesb[:, 0:1], bi == 0, bi == 2,
                                ["as_prodv", "onesb"], ["as_num"])
                for bi in range(3):
                    self.mm(denv[:, :, j], self.onesb[:], Egb[:, bi * 8:(bi + 1) * 8], bi == 0, bi == 2, ["as_Eb", "onesb"], ["as_den"])

            def finalize_s():
                pqk = sb(st, "as_pqk", [128, 8, SC], BF16)
                Es = sb(st, "as_Es", [128, 8, SC], F32)
                numS = sb(st, "as_numS", [128, 8, SC], F32)
                denS = sb(st, "as_denS", [128, 8, SC], F32)
                self.tt(pqk[:], self.qT_S[:], self.kT_S[:], ALU.mult, ["qT_S", "kT_S"], ["as_pqk"])
                self.mm(self_ps[:, 0:8 * SC], self.onesb[:], pqk[:].rearrange("p h j -> p (h j)"), True, True, ["as_pqk", "onesb"], ["as_self"])
                self.act(Es[:].rearrange("p h j -> p (h j)"), self_ps[:, 0:8 * SC], AF.Exp, ["as_self", "colv2"], ["as_Es"],
                         scale=SCALE, bias=negBias)
                self.ts(Es[:], Es[:], 3.0, None, ALU.mult, None, ["as_Es"], ["as_Es"])
                self.tt(numS[:], Es[:], self.vT_S[:], ALU.mult, ["as_Es", "vT_S"], ["as_numS"])
                self.tt(numS[:, :, 0:ntok], numS[:, :, 0:ntok], numv[:, :, 0:ntok], ALU.add, ["as_numS", "as_num"], ["as_numS"])
                self.tt(denS[:, :, 0:ntok], Es[:, :, 0:ntok], denv[:, :, 0:ntok], ALU.add, ["as_Es", "as_den"], ["as_denS"])
                P.add("dve", lambda e: e.reciprocal(denS[:, :, 0:ntok], denS[:, :, 0:ntok]), ["as_denS"], ["as_denS"])
                self.tt(mixedT[:, 8:16, S0:S0 + ntok], numS[:, :, 0:ntok], denS[:, :, 0:ntok], ALU.mult, ["as_numS", "as_denS"], ["mixS2"])

            NX = 4
            wsl = [sb(st, "wo_w%d" % i, [128, KC, 512], BF16) for i in range(2)]
            xs = [sb(st, "wo_x%d" % i, [128, 512], F32) for i in range(NX)]
            ys = [sb(st, "wo_y%d" % i, [128, 512], F32) for i in range(NX)]
            ps = self.psb(st, "wo_ps", 4)
            psr = Ring([("wo_ps%d" % i, ps[i]) for i in range(4)])
            kk = [0]

            def wout_tile(nb_, t, sl):
                b = kk[0] % NX
                kk[0] += 1
                col0 = t * 128
                if t < 16:
                    xsrc = self.xall[HALO + col0:HALO + col0 + 128, nb_ * 512:(nb_ + 1) * 512]
                else:
                    xsrc = self.xS[:, nb_ * 512:(nb_ + 1) * 512]
                self.dma(xs[b][:], xsrc, [], ["wo_x%d" % b])
                pn, pp_ = psr.next()
                extra = ["mixS2"] if t == 16 else []
                for kc in range(KC):
                    self.mm(pp_[:, :], mixedT[:, kc, col0:col0 + 128], wsl[sl][:, kc, :], kc == 0, kc == KC - 1, ["wo_w%d" % sl] + extra, [pn])
                self.tt(ys[b][:], pp_[:, :], xs[b][:], ALU.add, [pn, "wo_x%d" % b], ["wo_y%d" % b])
                self.dma(self.x1_s[col0:col0 + 128, nb_ * 512:(nb_ + 1) * 512], ys[b][:], ["wo_y%d" % b], [self.u()], q="aq")

            wseq = [0, 1, 2, 3, 0, 1, 2, 3]
            self.load_wblock(wsl[0], "wo_w0", self.w_out, 0, 512)
            gather(0)
            ntiles = 64
            done_tok = 0
            ti = 0
            for wi, nb_ in enumerate(wseq):
                if wi + 1 < len(wseq):
                    s2 = (wi + 1) % 2
                    self.load_wblock(wsl[s2], "wo_w%d" % s2, self.w_out, wseq[wi + 1] * 512, 512)
                sl = wi % 2
                if wi < 4:
                    for t in range(16):
                        wout_tile(nb_, t, sl)
                        ti += 1
                        want = min(ntok, (ti * ntok + ntiles - 1) // ntiles)
                        while done_tok < want:
                            token_step(done_tok)
                            done_tok += 1
                    if wi == 3:
                        while done_tok < ntok:
                            token_step(done_tok)
                            done_tok += 1
                        finalize_s()
                else:
                    wout_tile(nb_, 16, sl)
            P.flush()

    def stage_mem(self):
        P = self.P
        with contextlib.ExitStack() as st:
            sb = self.sb
            mkT = sb(st, "mm_mkT", [128, 5, MH, 256], BF16)
            mv = sb(st, "mm_mv", [128, 5, 2, 512], BF16)
            omT = sb(st, "mm_omT", [128, MH, NT], BF16)
            h2T = self.arena1
            with contextlib.ExitStack() as st2:
                bufs = self.nt_bufs(st2, "m1")
                gain_b = sb(st2, "m1_gain", [128, D], F32)
                wk = sb(st2, "m1_wk", [128, KC, 512], BF16)
                wv = sb(st2, "m1_wv", [128, KC, 512], BF16)
                gkb = sb(st2, "m1_gkb", [128, 128], F32)
                k32 = sb(st2, "m1_k32", [128, 512], F32)
                k32b = sb(st2, "m1_k32b", [128, 512], F32)
                kst = sb(st2, "m1_kst", [128, 12], F32)
                k16 = [sb(st2, "m1_k16_%d" % i, [128, 512], BF16) for i in range(2)]
                raw_ps = self.psb(st2, "m1_raw", 2)
                rawr = Ring([("m1_raw%d" % i, raw_ps[i]) for i in range(2)])
                tp_ps = bufs[3]
                self.load_wblock(wk, "m1_wk", self.w_mem_k, 0, 512)
                self.load_wblock(wv, "m1_wv", self.w_mem_v, 0, 512)
                self.dma(gain_b[:], self.norm_mem_src[0:1, :].to_broadcast([128, D]), [], ["gain"])
                self.dma(gkb[:], self.mem_k_norm[0:1, :].to_broadcast([128, 128]), [], ["gkb"])
                mT = h2T
                for i in range(2):
                    self.norm_transpose(self.mem_prompt[i * 128:(i + 1) * 128, :], [], gain_b, "gain", mT, i * 128, "h2T%d" % i, bufs, i)
                self.nt_flush()
                v4 = lambda ap: ap.rearrange("p (h d) -> p h d", h=4)
                for mt in range(2):
                    rn, rp = rawr.next()
                    for kc in range(KC):
                        self.mm(rp[:, :], mT[:, kc, mt * 128:(mt + 1) * 128], wk[:, kc, :], kc == 0, kc == KC - 1, ["m1_wk", "h2T%d" % mt], [rn])
                    self.cp("act", k32b[:], rp[:, :], [rn], ["m1_k32b"])
                    self.tt(k32[:], k32b[:], k32b[:], ALU.mult, ["m1_k32b"], ["m1_k32"])
                    self.red(kst[:, 0:4], v4(k32[:]), ALU.add, ["m1_k32"], ["m1_ksta"])
                    self.act(kst[:, 4:8], kst[:, 0:4], AF.Ln, ["m1_ksta"], ["m1_kstb"], scale=1.0 / 128, bias=EPS)
                    self.act(kst[:, 8:12], kst[:, 4:8], AF.Exp, ["m1_kstb"], ["m1_kstc"], scale=-0.5)
                    self.tt(v4(k32[:]), v4(k32b[:]), kst[:, 8:12].unsqueeze(2).to_broadcast([128, 4, 128]), ALU.mult,
                            ["m1_k32b", "m1_kstc"], ["m1_k32"])
                    self.tt(v4(k32[:]), v4(k32[:]), gkb[:].unsqueeze(1).to_broadcast([128, 4, 128]), ALU.mult, ["m1_k32", "gkb"], ["m1_k32"])
                    self.dma(self.pmk[mt * 128:(mt + 1) * 128, :], k32[:], ["m1_k32"], [self.u()])
                    self.cp("act", k16[0][:], k32[:], ["m1_k32"], ["m1_k16_0"])
                    pt = tp_ps.next()
                    for hm in range(MH):
                        self.tr(pt[1][:, hm * 128:(hm + 1) * 128], k16[0][:, hm * 128:(hm + 1) * 128], self.identb[:], ["m1_k16_0", "identb"], [pt[0]])
                    self.cp("dve", mkT[:, 4, :, mt * 128:(mt + 1) * 128], pt[1][:, 0:512].rearrange("p (a b) -> p a b", a=4), [pt[0]], ["mkT"])
                    rn, rp = rawr.next()
                    for kc in range(KC):
                        self.mm(rp[:, :], mT[:, kc, mt * 128:(mt + 1) * 128], wv[:, kc, :], kc == 0, kc == KC - 1, ["m1_wv", "h2T%d" % mt], [rn])
                    self.cp("act", k32b[:], rp[:, :], [rn], ["m1_k32b"])
                    self.dma(self.pmv[mt * 128:(mt + 1) * 128, :], k32b[:], ["m1_k32b"], [self.u()])
                    self.cp("dve", mv[:, 4, mt, :], k32b[:], ["m1_k32b"], ["mv"])
                ki = 0
                for bl in range(4):
                    for mt in range(2):
                        kb = ki % 2
                        ki += 1
                        KN = "m1_k16_%d" % kb
                        self.dma(k16[kb][:], self.cmk[bl, mt * 128:(mt + 1) * 128, :], [], [KN], q="pq")
                        pt = tp_ps.next()
                        for hm in range(MH):
                            self.tr(pt[1][:, hm * 128:(hm + 1) * 128], k16[kb][:, hm * 128:(hm + 1) * 128], self.identb[:], [KN, "identb"], [pt[0]])
                        self.cp("dve", mkT[:, bl, :, mt * 128:(mt + 1) * 128], pt[1][:, 0:512].rearrange("p (a b) -> p a b", a=4), [pt[0]], ["mkT"])
                        self.dma(mv[:, bl, mt, :], self.cmv[bl, mt * 128:(mt + 1) * 128, :], [], ["mv"], q="pq")
                P.flush()
            with contextlib.ExitStack() as st2:
                bufs = self.nt_bufs(st2, "m2")
                gain_b = sb(st2, "m2_gain", [128, D], F32)
                wq = sb(st2, "m2_wq", [128, KC, 512], BF16)
                raw_ps = self.psb(st2, "m2_raw", 2)
                ssq_ps = self.psb(st2, "m2_ssq", 1)[0]
                s_ps = self.psb(st2, "m2_s", 2)
                o_ps = self.psb(st2, "m2_o", 1)[0]
                rawr = Ring([("m2_raw%d" % i, raw_ps[i]) for i in range(2)])
                self.load_wblock(wq, "m2_wq", self.w_mem_q, 0, 512)
                self.dma(gain_b[:], self.norm_mem[0:1, :].to_broadcast([128, D]), [], ["gain"])
                for i in range(17):
                    self.norm_transpose(self.x1_s[i * 128:(i + 1) * 128, :], [], gain_b, "gain", h2T, i * 128, "h2T%d" % i, bufs, i)
                self.nt_flush()
                nb = 3
                sqb = [sb(st2, "m2_sq%d" % i, [128, 512], BF16) for i in range(nb)]
                t1f = [sb(st2, "m2_t1f%d" % i, [128, 512], F32) for i in range(nb)]
                lnv = [sb(st2, "m2_ln%d" % i, [128, 512], F32) for i in range(nb)]
                qm = [sb(st2, "m2_qm%d" % i, [128, 512], BF16) for i in range(nb)]
                Em = [sb(st2, "m2_E%d" % i, [128, 2, 512], BF16) for i in range(nb)]
                lnd = [sb(st2, "m2_lnd%d" % i, [128, 512], F32) for i in range(nb)]
                negBm = self.colv[:, 4:5]
                groups = [(0, 512), (512, 512), (1024, 512), (1536, 512), (S0, SC)]
                self.memset(omT[:, :, S0:S0 + 128], 0.0, ["omTpad"], eng="dve")

                def make_unit(hm, c0, n, u):
                    U = "m2_u%d_" % u
                    if c0 < S0:
                        segs = [(4, 0, n)]
                        nv = n
                    else:
                        segs = [(0, 0, 4), (1, 4, 4), (2, 8, 4), (3, 12, 4), (4, 16, 2)]
                        nv = 18
                    st_ = {}

                    def px():
                        rn, rp = rawr.next()
                        tiles = ["h2T%d" % t for t in range(c0 // 128, (c0 + n + 127) // 128)]
                        for kc in range(KC):
                            self.mm(rp[:, 0:n], wq[:, kc, hm * 128:(hm + 1) * 128], h2T[:, kc, c0:c0 + n], kc == 0, kc == KC - 1,
                                    ["m2_wq"] + tiles, [rn])
                        self.act(sqb[u][:, 0:n], rp[:, 0:n], AF.Square, [rn], [U + "sq"])
                        self.act(t1f[u][:, 0:n], rp[:, 0:n], AF.Copy, [rn, "colv"], [U + "t1f"], scale=self.colv[:, 2:3])

                    def py():
                        self.mm(ssq_ps[:, 0:n], self.onesb[:], sqb[u][:, 0:n], True, True, [U + "sq", "onesb"], ["m2_ssq"])
                        self.act(lnv[u][:, 0:n], ssq_ps[:, 0:n], AF.Ln, ["m2_ssq"], [U + "ln"], scale=1.0 / 128, bias=EPS)
                        self.act(lnv[u][:, 0:n], lnv[u][:, 0:n], AF.Exp, [U + "ln"], [U + "ln"], scale=-0.5)
                        self.tt(qm[u][:, 0:n], t1f[u][:, 0:n], lnv[u][:, 0:n], ALU.mult, [U + "t1f", U + "ln"], [U + "qm"])
                        for mc in range(2):
                            for (sidx, o0, nn) in segs:
                                self.mm(s_ps[mc][:, o0:o0 + nn], mkT[:, sidx, hm, mc * 128:(mc + 1) * 128], qm[u][:, o0:o0 + nn],
                                        True, True, ["mkT", U + "qm"], ["m2_s%d" % mc])
                        for mc in range(2):
                            self.act(Em[u][:, mc, 0:nv], s_ps[mc][:, 0:nv], AF.Exp, ["m2_s%d" % mc, "colv2"], [U + "E"], scale=SCALE, bias=negBm)

                    def pz():
                        for (sidx, o0, nn) in segs:
                            for mc in range(2):
                                self.mm(o_ps[:, o0:o0 + nn], mv[:, sidx, mc, hm * 128:(hm + 1) * 128], Em[u][:, mc, o0:o0 + nn],
                                        mc == 0, mc == 1, ["mv", U + "E"], ["m2_o"])
                        for mc in range(2):
                            self.mm(ssq_ps[:, 0:nv], self.onesb[:], Em[u][:, mc, 0:nv], mc == 0, mc == 1, ["onesb", U + "E"], ["m2_ssq"])
                        self.act(lnd[u][:, 0:nv], ssq_ps[:, 0:nv], AF.Ln, ["m2_ssq"], [U + "lnd"])
                        self.act(lnd[u][:, 0:nv], lnd[u][:, 0:nv], AF.Exp, [U + "lnd"], [U + "lnd"], scale=-1.0)
                        self.tt(omT[:, hm, c0:c0 + nv], o_ps[:, 0:nv], lnd[u][:, 0:nv], ALU.mult, ["m2_o", U + "lnd"],
                                ["omT%d_%d" % (hm, c0)] + (["omTpad"] if c0 == S0 else []))
                    return (px, py, pz)

                pipe = [None, None]
                cnt = 0
                for hm in range(MH):
                    for (c0, n) in groups:
                        ph = make_unit(hm, c0, n, cnt % nb)
                        cnt += 1
                        ph[0]()
                        if pipe[0] is not None:
                            pipe[0][1]()
                        if pipe[1] is not None:
                            pipe[1][2]()
                        pipe[1] = pipe[0]
                        pipe[0] = ph
                pipe[0][1]()
                pipe[1][2]()
                pipe[0][2]()
                P.flush()
            with contextlib.ExitStack() as st2:
                h3T = self.arena1
                bufs = self.nt_bufs(st2, "m3")
                gain_b = sb(st2, "m3_gain", [128, D], F32)
                wmo = sb(st2, "m3_wmo", [128, MH, D], BF16)
                x1t = [sb(st2, "m3_x1t%d" % i, [128, D], F32) for i in range(2)]
                big_ps = self.psb(st2, "m3_big", 4)
                self.dma(wmo[:, :, 0:1024], self.w_mem_o[:, 0:1024].rearrange("(k p) n -> p k n", p=128), [], ["m3_wmo"], q="pq")
                self.dma(wmo[:, :, 1024:2048], self.w_mem_o[:, 1024:2048].rearrange("(k p) n -> p k n", p=128), [], ["m3_wmo"], q="pq")
                self.dma(gain_b[:], self.norm_ffn[0:1, :].to_broadcast([128, D]), [], ["gain"])
                xt = bufs[0]
                self.dma(x1t[0][:], self.x1_s[0:128, :], [], ["m3_x1t0"])
                for t in range(17):
                    b = t % 2
                    col0 = t * 128
                    if t + 1 < 17:
                        self.dma(x1t[1 - b][:], self.x1_s[col0 + 128:col0 + 256, :], [], ["m3_x1t%d" % (1 - b)])
                    for nb_ in range(4):
                        for hm in range(MH):
                            self.mm(big_ps[nb_][:, :], omT[:, hm, col0:col0 + 128], wmo[:, hm, nb_ * 512:(nb_ + 1) * 512], hm == 0, hm == MH - 1,
                                    ["m3_wmo"], ["m3_big%d" % nb_])
                        self.tt(xt[b][:, nb_ * 512:(nb_ + 1) * 512], big_ps[nb_][:, :], x1t[b][:, nb_ * 512:(nb_ + 1) * 512], ALU.add,
                                ["m3_big%d" % nb_, "m3_x1t%d" % b], ["nt_x%d" % b])
                    self.dma(self.x2_s[col0:col0 + 128, :], xt[b][:], ["nt_x%d" % b], [self.u()], q="pq")
                    self.nt_flush()
                    self.norm_transpose(None, [], gain_b, "gain", h3T, col0, "h3T%d" % t, bufs, t)
                self.nt_flush()
                P.flush()

    def stage_ffn_a(self):
        P = self.P
        with contextlib.ExitStack() as st:
            sb = self.sb
            h3T = self.arena1
            wg = [sb(st, "fa_wg%d" % i, [128, KC, 512], BF16) for i in range(2)]
            wu = [sb(st, "fa_wu%d" % i, [128, KC, 512], BF16) for i in range(2)]
            gs = [sb(st, "fa_gs%d" % i, [128, 2 + OWN], F32) for i in range(2)]
            cbuf = [sb(st, "fa_c%d" % i, [128, 512], F32) for i in range(2)]
            sbuf_ = [sb(st, "fa_s%d" % i, [128, 512], F32) for i in range(2)]
            abuf = [sb(st, "fa_a%d" % i, [128, 512], BF16) for i in range(3)]
            gS = sb(st, "fa_gS", [128, SC], F32)
            extS = sb(st, "fa_extS", [128, 4, 6], F32)
            cS = sb(st, "fa_cS", [128, 4, 4], F32)
            aS = sb(st, "fa_aS", [128, SC], BF16)
            convst = sb(st, "fa_convst", [128, NFC, 4, 2], F32)
            sconvT = sb(st, "fa_sconvT", [128, 4, 2, 64], F32)
            pconvT = sb(st, "fa_pconvT", [128, 2, 64], F32)
            scin = sb(st, "fa_scin", [8, DFF], F32)
            outst = sb(st, "fa_outst", [128, 1024], F32)
            g_ps = self.psb(st, "fa_g", 2)
            u_ps = self.psb(st, "fa_u", 2)
            sm_ps = self.psb(st, "fa_sm", 2)
            gr = Ring([("fa_g%d" % i, g_ps[i]) for i in range(2)])
            ur = Ring([("fa_u%d" % i, u_ps[i]) for i in range(2)])
            self.dma(scin[:], self.state_conv.rearrange("b t n -> (b t) n"), [], ["fa_scin"])
            for fc in range(NFC):
                self.tr(sm_ps[0][:, fc * 8:(fc + 1) * 8], scin[0:8, fc * 128:(fc + 1) * 128], self.identf[0:8, 0:8], ["fa_scin", "identf"], ["fa_sm0"])
            self.cp("dve", convst[:].rearrange("p c b t -> p (c b t)"), sm_ps[0][:, 0:NFC * 8], ["fa_sm0"], ["fa_convst"])
            self.memset(aS[:], 0.0, ["fa_aS"], eng="dve")
            self.memset(sconvT[:], 0.0, ["fa_sconvT"], eng="dve")
            self.memset(pconvT[:], 0.0, ["fa_pconvT"], eng="dve")

            def loadw(blk):
                s = blk % 2
                self.load_wblock(wg[s], "fa_wg%d" % s, self.w_gate, blk * 512, 512)
                self.load_wblock(wu[s], "fa_wu%d" % s, self.w_up, blk * 512, 512)

            loadw(0)
            groups = [(0, 512), (512, 512), (1024, 512), (1536, 512)]
            tiles_of = lambda c0, n: ["h3T%d" % t for t in range(c0 // 128, (c0 + n + 127) // 128)]
            ka = 0
            import os
            nfc_run = int(os.environ.get("FFA_NFC", NFC))
            for fc in range(nfc_run):
                blk, un = fc // 4, fc % 4
                if un == 0 and blk + 1 < NFC // 4:
                    loadw(blk + 1)
                s = blk % 2
                gb_ = fc % 2
                GS = "fa_gs%d" % gb_
                w0, w1, w2, bb = self.cw[:, fc, 0:1], self.cw[:, fc, 1:2], self.cw[:, fc, 2:3], self.cb[:, fc:fc + 1]
                gn, gp = gr.next()
                un_, up = ur.next()
                for kc in range(KC):
                    self.mm(gp[:, 0:SC], wg[s][:, kc, un * 128:(un + 1) * 128], h3T[:, kc, S0:S0 + SC], kc == 0, kc == KC - 1, ["fa_wg%d" % s, "h3T16"], [gn])
                for kc in range(KC):
                    self.mm(up[:, 0:SC], wu[s][:, kc, un * 128:(un + 1) * 128], h3T[:, kc, S0:S0 + SC], kc == 0, kc == KC - 1, ["fa_wu%d" % s, "h3T16"], [un_])
                self.cp("act", gS[:], gp[:, 0:SC], [gn], ["fa_gS"])
                self.cp("dve", extS[:, :, 0:2], convst[:, fc, :, :], ["fa_convst"], ["fa_extS"])
                self.cp("dve", extS[:, :, 2:6], gS[:, 0:16].rearrange("p (b t) -> p b t", b=4), ["fa_gS"], ["fa_extS"])
                self.cp("dve", sconvT[:, :, :, fc], gS[:, 0:16].rearrange("p (b t) -> p b t", b=4)[:, :, 2:4], ["fa_gS"], ["fa_sconvT"])
                self.ts(gs[gb_][:, 0:2], gS[:, 16:18], self.hv[:, 0:1], None, ALU.mult, None, ["fa_gS", "hv"], [GS + "_pre"])
                self.ts(cS[:], extS[:, :, 2:6], w2, bb, ALU.mult, ALU.add, ["fa_extS", "cw", "cb"], ["fa_cS"])
                self.stt(cS[:], extS[:, :, 1:5], w1, cS[:], ALU.mult, ALU.add, ["fa_extS", "fa_cS"], ["fa_cS"])
                self.stt(cS[:], extS[:, :, 0:4], w0, cS[:], ALU.mult, ALU.add, ["fa_extS", "fa_cS"], ["fa_cS"])
                self.act(cS[:], cS[:], AF.Silu, ["fa_cS"], ["fa_cS"])
                self.tt(aS[:, 0:16].rearrange("p (b t) -> p b t", b=4), cS[:], up[:, 0:16].rearrange("p (b t) -> p b t", b=4), ALU.mult,
                        ["fa_cS", un_], ["fa_aS"])
                self.dma(self.a_s[fc, :, S0:S0 + SC], aS[:], ["fa_aS"], [self.u()])
                for (c0, n) in groups:
                    gn, gp = gr.next()
                    un_, up = ur.next()
                    tl = tiles_of(c0, n)
                    for kc in range(KC):
                        self.mm(gp[:, 0:n], wg[s][:, kc, un * 128:(un + 1) * 128], h3T[:, kc, c0:c0 + n], kc == 0, kc == KC - 1, ["fa_wg%d" % s] + tl, [gn])
                    for kc in range(KC):
                        self.mm(up[:, 0:n], wu[s][:, kc, un * 128:(un + 1) * 128], h3T[:, kc, c0:c0 + n], kc == 0, kc == KC - 1, ["fa_wu%d" % s] + tl, [un_])
                    GSc = GS + "_%d" % c0
                    GSp = GS + ("_%d" % (c0 - 512) if c0 > 0 else "_pre")
                    self.cp("act", gs[gb_][:, 2 + c0:2 + c0 + n], gp[:, 0:n], [gn], [GSc])
                    cb_ = ka % 2
                    ab_ = ka % 3
                    ka += 1
                    CN, SN, ANm = "fa_c%d" % cb_, "fa_s%d" % cb_, "fa_a%d" % ab_
                    self.ts(cbuf[cb_][:, 0:n], gs[gb_][:, 2 + c0:2 + c0 + n], w2, bb, ALU.mult, ALU.add, [GSc, "cw", "cb"], [CN])
                    self.stt(cbuf[cb_][:, 0:n], gs[gb_][:, 1 + c0:1 + c0 + n], w1, cbuf[cb_][:, 0:n], ALU.mult, ALU.add, [GSc, GSp, CN], [CN])
                    self.stt(cbuf[cb_][:, 0:n], gs[gb_][:, c0:c0 + n], w0, cbuf[cb_][:, 0:n], ALU.mult, ALU.add, [GSc, GSp, CN], [CN])
                    self.act(sbuf_[cb_][:, 0:n], cbuf[cb_][:, 0:n], AF.Silu, [CN], [SN])
                    self.tt(abuf[ab_][:, 0:n], sbuf_[cb_][:, 0:n], up[:, 0:n], ALU.mult, [SN, un_], [ANm])
                    self.dma(self.a_s[fc, :, c0:c0 + n], abuf[ab_][:, 0:n], [ANm], [self.u()])
                self.cp("dve", pconvT[:, :, fc], gs[gb_][:, OWN:OWN + 2], [GS + "_1536"], ["fa_pconvT"])
            epi = os.environ.get("FFA_EPI", "ps")
            if epi == "0":
                P.flush()
                return
            self.tr(sm_ps[1][:, 0:128], pconvT[:].rearrange("p t c -> p (t c)"), self.identf[:], ["fa_pconvT", "identf"], ["fa_sm1"])
            self.cp("act", outst[:, 0:128], sm_ps[1][:, 0:128], ["fa_sm1"], ["fa_outst"])
            for t in range(2 if "p" in epi else 0):
                self.dma(self.pconv[t:t + 1, :].rearrange("o (c p) -> (o c) p", p=128), outst[t * 64:t * 64 + NFC, 0:128], ["fa_outst"], [self.u()])
            kl = [int(ch) for ch in epi if ch.isdigit()] if any(ch.isdigit() for ch in epi) else list(range(4))
            for k in (kl if "s" in epi else []):
                pdst = sm_ps[1][:, 128 + k * 128:128 + (k + 1) * 128] if k < 3 else sm_ps[0][:, 384:512]
                self.tr(pdst, sconvT[:, k, :, :].rearrange("p t c -> p (t c)"), self.identf[:], ["fa_sconvT", "identf"], ["fa_sm1" if k < 3 else "fa_sm0"])
                self.cp("act", outst[:, 128 + k * 128:128 + (k + 1) * 128], pdst, ["fa_sm1" if k < 3 else "fa_sm0"], ["fa_outst2_%d" % k])
                for t in range(2):
                    self.dma(self.s_conv[k, t:t + 1, :].rearrange("o (c p) -> (o c) p", p=128),
                             outst[t * 64:t * 64 + NFC, 128 + k * 128:128 + (k + 1) * 128], ["fa_outst2_%d" % k], [self.u()])
            P.flush()

    def stage_ffn_b(self):
        P = self.P
        with contextlib.ExitStack() as st:
            sb = self.sb
            TG = 512
            wd = [sb(st, "fb_wd%d" % i, [128, NFC, 512], BF16) for i in range(2)]
            ag = [sb(st, "fb_ag%d" % i, [128, NFC, TG], BF16) for i in range(2)]
            xs = [sb(st, "fb_x%d" % i, [128, 512], F32) for i in range(4)]
            ys = [sb(st, "fb_y%d" % i, [128, 512], F32) for i in range(4)]
            ps = self.psb(st, "fb_ps", 6)
            psr = Ring([("fb_ps%d" % i, ps[i]) for i in range(6)])

            def loadwd(nb_):
                s = nb_ % 2
                src = self.w_down[:, nb_ * 512:(nb_ + 1) * 512].rearrange("(k p) n -> p k n", p=128)
                for qq in range(4):
                    self.dma(wd[s][:, 11 * qq:11 * qq + 11, :], src[:, 11 * qq:11 * qq + 11, :], [], ["fb_wd%d_%d" % (s, qq)], q="pq")

            tgroups = [(i * TG, TG) for i in range(OWN // TG)] + [(S0, 128)]
            seq = [(nb_, gi) for nb_ in range(4) for gi in range(len(tgroups))]

            def loada(idx):
                nb_, gi = seq[idx]
                c0, n = tgroups[gi]
                s = idx % 2
                nn = n if c0 < S0 else SC
                for qq in range(2):
                    self.dma(ag[s][:, 22 * qq:22 * qq + 22, 0:nn], self.a_s[22 * qq:22 * qq + 22, :, c0:c0 + nn].rearrange("c p t -> p c t"), [],
                             ["fb_ag%d_%d" % (s, qq)], q="aq")

            loadwd(0)
            loada(0)
            k = 0
            for idx, (nb_, gi) in enumerate(seq):
                if gi == 0 and nb_ + 1 < 4:
                    loadwd(nb_ + 1)
                if idx + 1 < len(seq):
                    loada(idx + 1)
                s = idx % 2
                ws = nb_ % 2
                c0, n = tgroups[gi]
                for t0 in range(0, n, 128):
                    b = k % 4
                    k += 1
                    row0 = c0 + t0
                    m = 128 if c0 < S0 else SC
                    self.dma(xs[b][:], self.x2_s[row0:row0 + 128, nb_ * 512:(nb_ + 1) * 512], [], ["fb_x%d" % b], q="aq")
                    pn, pp_ = psr.next()
                    for fc in range(NFC):
                        self.mm(pp_[0:m, :], ag[s][:, fc, t0:t0 + m], wd[ws][:, fc, :], fc == 0, fc == NFC - 1, ["fb_ag%d_%d" % (s, fc // 22), "fb_wd%d_%d" % (ws, fc // 11)], [pn])
                    self.tt(ys[b][0:m, :], pp_[0:m, :], xs[b][0:m, :], ALU.add, [pn, "fb_x%d" % b], ["fb_y%d" % b])
                    if c0 < S0:
                        self.dma(self.y[row0:row0 + 128, nb_ * 512:(nb_ + 1) * 512], ys[b][:], ["fb_y%d" % b], [self.u()])
                    else:
                        self.dma(self.yS[:, nb_ * 512:(nb_ + 1) * 512], ys[b][0:16, :], ["fb_y%d" % b], [self.u()])
            P.flush()


def _rope_tables(pos):
    half = 64
    inv = (1.0 / (10000.0 ** (np.arange(half, dtype=np.float32) * np.float32(2.0 / 128)))).astype(np.float32)
    ang = pos.astype(np.float32)[:, None] * inv[None, :]
    cos = np.cos(ang).astype(np.float32).T
    sin = np.sin(ang).astype(np.float32).T
    cosT = np.concatenate([cos, cos], axis=0)
    sinT = np.concatenate([-sin, sin], axis=0)
    return np.ascontiguousarray(cosT), np.ascontiguousarray(sinT)


def make_in_maps(inp):
    f = lambda a: np.ascontiguousarray(np.asarray(a, dtype=np.float32))
    xp = f(inp["x_prompt"])[0]
    xs_all = f(inp["x_sample"])
    shared = {
        "w_in": f(inp["w_in"])[0], "w_pool": f(inp["w_pool"])[0], "w_out": f(inp["w_out"])[0],
        "w_mem_q": f(inp["w_mem_q"])[0], "w_mem_k": f(inp["w_mem_k"])[0], "w_mem_v": f(inp["w_mem_v"])[0],
        "w_mem_o": f(inp["w_mem_o"])[0], "w_gate": f(inp["w_gate"])[0], "w_up": f(inp["w_up"])[0],
        "w_down": f(inp["w_down"])[0],
        "norm_mix": f(inp["norm_mix"]), "norm_mem": f(inp["norm_mem"]), "norm_mem_src": f(inp["norm_mem_src"]),
        "norm_ffn": f(inp["norm_ffn"]), "q_norm": f(inp["q_norm"]), "k_norm": f(inp["k_norm"]),
        "mem_q_norm": f(inp["mem_q_norm"]), "mem_k_norm": f(inp["mem_k_norm"]), "pool_scale": f(inp["pool_scale"]),
        "conv_w": f(inp["conv_w"])[0], "conv_b": f(inp["conv_b"]), "mem_prompt": f(inp["mem_prompt"])[0],
    }
    maps = []
    for c in range(NCORES):
        start = c * OWN
        xall = np.zeros((EXT, D), np.float32)
        lo = start - HALO
        s0 = max(lo, 0)
        xall[s0 - lo:, :] = xp[s0:start + OWN]
        xS = np.zeros((128, D), np.float32)
        xS[0:16] = xs_all[4 * c:4 * c + 4].reshape(16, D)
        xS[16:18] = xall[HALO - 2:HALO]
        pos = np.arange(lo, start + OWN)
        cosT, sinT = _rope_tables(pos)
        posS = np.zeros(SC, np.int64)
        posS[0:16] = np.tile(PAST + np.arange(4), 4)
        posS[16:18] = [start - 2, start - 1]
        cosS, sinS = _rope_tables(posS)
        valid = 1.0 if c > 0 else 0.0
        hv = np.zeros((128, 2), np.float32)
        hv[:, 0] = valid
        hv[:, 1] = NEG * (1.0 - valid)
        corr = np.ones((128, 4, 16), np.float32)
        for g_ in range(4):
            w = 2 << g_
            p = start + np.arange(16)
            corr[:, g_, :] = (w / np.minimum(p + 1, w)).astype(np.float32)[None, :]
        gmask = np.ones((128, 2, 3, 8), np.float32)
        for t in range(2):
            for bi, d in enumerate(DILS):
                kp = (start - 2 + t) - d * (128 - np.arange(128))
                gmask[:, t, bi, :] = (kp >= 0).astype(np.float32)[:, None]
        m = dict(shared)
        m.update({
            "xall": xall, "xS": xS, "cosT": cosT, "sinT": sinT, "cosS": cosS, "sinS": sinS, "hv": hv,
            "corr": corr.reshape(128, 64), "gmask": gmask.reshape(128, 48),
            "state_pool": f(inp["state_pool"])[0, 4 * c:4 * c + 4],
            "cwk": f(inp["cache_win_k"])[0, 4 * c:4 * c + 4].reshape(4, 2048, 1024),
            "cwv": f(inp["cache_win_v"])[0, 4 * c:4 * c + 4].reshape(4, 2048, 1024),
            "cmk": f(inp["cache_mem_k"])[0, 4 * c:4 * c + 4].reshape(4, 256, 512),
            "cmv": f(inp["cache_mem_v"])[0, 4 * c:4 * c + 4].reshape(4, 256, 512),
            "state_conv": f(inp["state_conv"])[0, 4 * c:4 * c + 4],
        })
        maps.append(m)
    return maps


_NC_CACHE = {}


def get_nc(debug=False, stop_after=None):
    key = (debug, stop_after)
    if key not in _NC_CACHE:
        b = Builder(debug=debug, stop_after=stop_after)
        b.build()
        _NC_CACHE[key] = b
    return _NC_CACHE[key]


def kernel(**inputs):
    b = get_nc()
    maps = make_in_maps(inputs)
    keys = set(b.din.keys())
    maps = [{k: v for k, v in m.items() if k in keys} for m in maps]
    res = run_bass_kernel_spmd(b.nc, maps, core_ids=list(range(NCORES)))
    r = res.results
    cat = lambda k: np.concatenate([np.asarray(r[c][k]) for c in range(NCORES)], axis=0)
    y_prompt = cat("y").reshape(1, NCORES * OWN, D)
    y_sample = cat("yS").reshape(32, 4, D)
    last = r[NCORES - 1]
    p_state_pool = np.asarray(last["p_pool"]).reshape(1, 1, 15, PW)
    p_win_k = np.asarray(last["pk"]).reshape(1, 1, 2048, NH, 128)
    p_win_v = np.asarray(last["pv"]).reshape(1, 1, 2048, NH, 128)
    p_mem_k = np.asarray(r[0]["pmk"]).reshape(1, 1, 256, MH, 128)
    p_mem_v = np.asarray(r[0]["pmv"]).reshape(1, 1, 256, MH, 128)
    p_state_conv = np.asarray(last["pconv"]).reshape(1, 1, 2, DFF)
    s_state_pool = cat("s_pool").reshape(1, 32, 15, PW)
    s_k = cat("s_k").reshape(1, 32, 4, NH, 128)
    s_v = cat("s_v").reshape(1, 32, 4, NH, 128)
    s_conv = cat("s_conv").reshape(1, 32, 2, DFF)
    outs = (y_prompt, y_sample, p_state_pool, p_win_k, p_win_v, p_mem_k, p_mem_v, p_state_conv,
            s_state_pool, s_k, s_v, s_conv)
    return tuple(np.ascontiguousarray(o, dtype=np.float32) for o in outs)
```

```python
import contextlib
import numpy as np
import concourse.bass as bass
import concourse.mybir as mybir
from concourse.bass_utils import run_bass_kernel_spmd

F32 = mybir.dt.float32
BF16 = mybir.dt.bfloat16
AF = mybir.ActivationFunctionType
ALU = mybir.AluOpType
AX = mybir.AxisListType

NCORES = 8
D = 2048
KC = 16
NH = 8
PW = 1024
DFF = 5632
NFC = 44
MH = 4
OWN = 2048
HALO = 2176
EXT = HALO + OWN
SC = 32
S0 = OWN
NT = OWN + 128
PAST = 16384
EPS = 1e-6
SCALE = 128.0 ** -0.5
NEG = -30000.0
DILS = (1, 4, 16)


class Prog:
    def __init__(self, nc, stack, n_dma_sems=32):
        self.nc = nc
        self.sems = {s: stack.enter_context(nc.semaphore("s_" + s)) for s in ("pe", "act", "dve", "pool")}
        self.dsems = [stack.enter_context(nc.semaphore("d%d" % i)) for i in range(n_dma_sems)]
        self.cnt = {s: 0 for s in self.sems}
        self.dcnt = [0] * n_dma_sems
        self.dlast = [None] * n_dma_sems
        self.ndma = 0
        self.ndma_sw = 0
        self.n_sw_sems = 8
        self.known = {s: {} for s in ("pe", "act", "dve", "pool", "sp")}
        self.trace = {s: [] for s in ("pe", "act", "dve", "pool", "sp")}
        self.nstage = 0
        self.total_ops = 0
        self._reset()

    def _reset(self):
        self.ops = []
        self.last_writer = {}
        self.readers = {}

    def add(self, eng, fn, reads=(), writes=(), dma=False):
        idx = len(self.ops)
        deps = set()
        lw = self.last_writer
        for r in reads:
            w = lw.get(r)
            if w is not None:
                deps.add(w)
        for r in writes:
            w = lw.get(r)
            if w is not None:
                deps.add(w)
            rd = self.readers.get(r)
            if rd:
                deps.update(rd)
        deps.discard(idx)
        self.ops.append([eng, fn, deps, dma, False, None, None])
        for r in writes:
            lw[r] = idx
            self.readers[r] = []
        for r in reads:
            self.readers.setdefault(r, []).append(idx)
        return idx

    def flush(self):
        nc = self.nc
        ops = self.ops
        stream_of = {"pe": "pe", "act": "act", "dve": "dve", "pool": "pool", "sp": "sp", "pq": "pool", "aq": "act"}
        streams = {s: [] for s in ("pe", "act", "dve", "pool", "sp")}
        for i, o in enumerate(ops):
            streams[stream_of[o[0]]].append(i)
        for i, o in enumerate(ops):
            so = stream_of[o[0]]
            keep = []
            best = {}
            for d in o[2]:
                od = ops[d]
                if od[3]:
                    keep.append(d)
                    continue
                sd = stream_of[od[0]]
                if sd == so and not o[3] and so == "pe":
                    continue
                if d > best.get(sd, -1):
                    best[sd] = d
            for sd, d in best.items():
                keep.append(d)
                ops[d][4] = True
            o[2] = sorted(keep)
        for s in ("pe", "act", "dve", "pool"):
            for i in reversed(streams[s]):
                if not ops[i][3]:
                    ops[i][4] = True
                    break
        nd = len(self.dsems)
        nsw = self.n_sw_sems
        nhw = nd - nsw
        for i, o in enumerate(ops):
            if o[3]:
                if o[0] == "pq":
                    k = nhw + (self.ndma_sw % nsw)
                    self.ndma_sw += 1
                else:
                    k = self.ndma % nhw
                    self.ndma += 1
                self.dcnt[k] += 16
                o[5] = ("d", k, self.dcnt[k])
                o[6] = self.dlast[k]
                self.dlast[k] = o[5]
            elif o[4]:
                s = stream_of[o[0]]
                self.cnt[s] += 1
                o[5] = ("c", s, self.cnt[s])
        finals = [("c", s, self.cnt[s]) for s in self.sems if self.cnt[s]]
        finals += [("d", k, self.dcnt[k]) for k in range(nd) if self.dcnt[k]]

        def run_stream(sname, eng):
            known = self.known[sname]

            def wait_tok(tok):
                key = (tok[0], tok[1])
                if known.get(key, 0) >= tok[2]:
                    return
                known[key] = tok[2]
                self.trace[sname].append(("w", key, tok[2]))
                sem = self.dsems[tok[1]] if tok[0] == "d" else self.sems[tok[1]]
                eng.wait_ge(sem, tok[2])

            for i in streams[sname]:
                o = ops[i]
                for d in o[2]:
                    tok = ops[d][5]
                    if tok is not None:
                        wait_tok(tok)
                if o[3] and o[6] is not None:
                    wait_tok(o[6])
                ins = o[1](eng)
                tok = o[5]
                self.trace[sname].append(("o", None if tok is None else (tok[0], tok[1]), 16 if (tok and tok[0] == "d") else 1))
                if tok is not None:
                    if tok[0] == "d":
                        ins.then_inc(self.dsems[tok[1]], 16)
                    else:
                        ins.then_inc(self.sems[tok[1]], 1)
            for tok in finals:
                if not (tok[0] == "c" and tok[1] == sname):
                    wait_tok(tok)

        with nc.Block() as block:
            @block.tensor
            def _(e):
                run_stream("pe", e)

            @block.scalar
            def _(e):
                run_stream("act", e)

            @block.vector
            def _(e):
                run_stream("dve", e)

            @block.gpsimd
            def _(e):
                run_stream("pool", e)

            @block.sync
            def _(e):
                run_stream("sp", e)
        self.total_ops += len(ops)
        self.nstage += 1
        self._reset()


class Ring:
    def __init__(self, items):
        self.items = list(items)
        self.i = 0

    def next(self):
        x = self.items[self.i % len(self.items)]
        self.i += 1
        return x


class Builder:
    def __init__(self, debug=False, stop_after=None):
        self.debug = debug
        self.stop_after = stop_after
        self.nc = bass.Bass("TRN2", target_bir_lowering=False)
        self.din = {}
        self.dout = {}
        self._uid = 0

    def u(self):
        self._uid += 1
        return "_u%d" % self._uid

    def inp(self, name, shape):
        t = self.nc.dram_tensor(name, list(shape), F32, kind="ExternalInput").ap()
        self.din[name] = t
        return t

    def outp(self, name, shape):
        t = self.nc.dram_tensor(name, list(shape), F32, kind="ExternalOutput").ap()
        self.dout[name] = t
        return t

    def scratch(self, name, shape, dt):
        kind = "ExternalOutput" if self.debug else "Internal"
        t = self.nc.dram_tensor(name, list(shape), dt, kind=kind).ap()
        if self.debug:
            self.dout[name] = t
        return t

    def sb(self, st, name, shape, dt):
        return st.enter_context(self.nc.sbuf_tensor("sb_" + name, list(shape), dt))

    def psb(self, st, name, n=1, dt=F32, cols=512):
        cols = 1024 if dt == BF16 else 512
        return [st.enter_context(self.nc.psum_tensor("ps_%s%d" % (name, i), [128, cols], dt)) for i in range(n)]

    def mm(self, out, lhsT, rhs, start, stop, reads, writes):
        self.P.add("pe", lambda e: e.matmul(out, lhsT, rhs, start=start, stop=stop), reads, writes)

    def tr(self, out, in_, ident, reads, writes):
        self.P.add("pe", lambda e: e.transpose(out, in_, ident), reads, writes)

    def act(self, out, in_, func, reads, writes, scale=None, bias=None, accum=None):
        kw = {}
        if scale is not None:
            kw["scale"] = scale
        if bias is not None:
            kw["bias"] = bias
        if accum is not None:
            kw["accum_out"] = accum
        self.P.add("act", lambda e: e.activation(out=out, in_=in_, func=func, **kw), reads, writes)

    def cp(self, eng, out, in_, reads, writes):
        if eng == "act":
            self.P.add("act", lambda e: e.copy(out, in_), reads, writes)
        else:
            self.P.add(eng, lambda e: e.tensor_copy(out, in_), reads, writes)

    def tt(self, out, in0, in1, op, reads, writes, eng="dve"):
        self.P.add(eng, lambda e: e.tensor_tensor(out=out, in0=in0, in1=in1, op=op), reads, writes)

    def ts(self, out, in0, s1, s2, op0, op1, reads, writes, eng="dve"):
        if op1 is None:
            self.P.add(eng, lambda e: e.tensor_scalar(out=out, in0=in0, scalar1=s1, scalar2=None, op0=op0), reads, writes)
        else:
            self.P.add(eng, lambda e: e.tensor_scalar(out=out, in0=in0, scalar1=s1, scalar2=s2, op0=op0, op1=op1), reads, writes)

    def stt(self, out, in0, scalar, in1, op0, op1, reads, writes):
        self.P.add("dve", lambda e: e.scalar_tensor_tensor(out=out, in0=in0, scalar=scalar, in1=in1, op0=op0, op1=op1), reads, writes)

    def red(self, out, in_, op, reads, writes, absv=None):
        self.P.add("dve", lambda e: e.tensor_reduce(out=out, in_=in_, axis=AX.X, op=op, apply_absolute_value=absv), reads, writes)

    def dma(self, out, in_, reads, writes, q="sp"):
        self.P.add(q, lambda e: e.dma_start(out=out, in_=in_), reads, writes, dma=True)

    def memset(self, ap, val, writes, eng="pool"):
        self.P.add(eng, lambda e: e.memset(ap, val), (), writes)

    def build(self):
        nc = self.nc
        with contextlib.ExitStack() as g:
            self.P = Prog(nc, g)
            self.declare_dram()
            self.alloc_global(g)
            self.stage_consts()
            stages = [self.stage_proj, self.stage_pool, self.stage_attn, self.stage_attn_s,
                      self.stage_wout, self.stage_mem, self.stage_ffn_a]
            done = False
            with contextlib.ExitStack() as g1:
                self.arena1 = self.sb(g1, "arena1", [128, KC, NT], BF16)
                import os
                skip = os.environ.get("SKIP_TO")
                for fn in stages:
                    if skip and fn.__name__ != skip:
                        continue
                    skip = None
                    fn()
                    if self.debug:
                        dd = self.nc.dram_tensor("dbg_" + fn.__name__, [128, KC, NT], BF16, kind="ExternalOutput").ap()
                        self.dout["dbg_" + fn.__name__] = dd
                        self.dma(dd[:, :, :], self.arena1[:], [], [self.u()])
                        self.P.flush()
                    if self.stop_after == fn.__name__:
                        done = True
                        break
            if not done:
                self.stage_ffn_b()
        return nc

    def declare_dram(self):
        i = self.inp
        self.xall = i("xall", [EXT, D])
        self.xS = i("xS", [128, D])
        self.cosT = i("cosT", [128, EXT])
        self.sinT = i("sinT", [128, EXT])
        self.cosS = i("cosS", [128, SC])
        self.sinS = i("sinS", [128, SC])
        self.hv_d = i("hv", [128, 2])
        self.corr_d = i("corr", [128, 4 * 16])
        self.gmask_d = i("gmask", [128, 2 * 24])
        self.w_in = i("w_in", [D, 4096])
        self.w_pool = i("w_pool", [4, 256, 256])
        self.w_out = i("w_out", [D, D])
        self.w_mem_q = i("w_mem_q", [D, 512])
        self.w_mem_k = i("w_mem_k", [D, 512])
        self.w_mem_v = i("w_mem_v", [D, 512])
        self.w_mem_o = i("w_mem_o", [512, D])
        self.w_gate = i("w_gate", [D, DFF])
        self.w_up = i("w_up", [D, DFF])
        self.w_down = i("w_down", [DFF, D])
        self.norm_mix = i("norm_mix", [1, D])
        self.norm_mem = i("norm_mem", [1, D])
        self.norm_mem_src = i("norm_mem_src", [1, D])
        self.norm_ffn = i("norm_ffn", [1, D])
        self.q_norm = i("q_norm", [1, 128])
        self.k_norm = i("k_norm", [1, 128])
        self.mem_q_norm = i("mem_q_norm", [1, 128])
        self.mem_k_norm = i("mem_k_norm", [1, 128])
        self.pool_scale = i("pool_scale", [1, PW])
        self.conv_w = i("conv_w", [3, DFF])
        self.conv_b = i("conv_b", [1, DFF])
        self.mem_prompt = i("mem_prompt", [256, D])
        self.state_pool = i("state_pool", [4, 15, PW])
        self.cwk = i("cwk", [4, 2048, 1024])
        self.cwv = i("cwv", [4, 2048, 1024])
        self.cmk = i("cmk", [4, 256, 512])
        self.cmv = i("cmv", [4, 256, 512])
        self.state_conv = i("state_conv", [4, 2, DFF])
        o = self.outp
        self.y = o("y", [OWN, D])
        self.yS = o("yS", [16, D])
        self.p_pool = o("p_pool", [15, PW])
        self.pk = o("pk", [OWN, 1024])
        self.pv = o("pv", [OWN, 1024])
        self.pmk = o("pmk", [256, 512])
        self.pmv = o("pmv", [256, 512])
        self.pconv = o("pconv", [2, DFF])
        self.s_pool = o("s_pool", [4, 15, PW])
        self.s_k = o("s_k", [16, 1024])
        self.s_v = o("s_v", [16, 1024])
        self.s_conv = o("s_conv", [4, 2, DFF])
        s = self.scratch
        self.uT_s = s("uT_s", [8, 128, 128 + OWN], F32)
        self.kT_s = s("kT_s", [NH, 128, EXT], BF16)
        self.vT_s = s("vT_s", [NH, 128, EXT], BF16)
        self.qT_s = s("qT_s", [NH, 128, OWN], BF16)
        self.kh_s = s("kh_s", [HALO, 1024], F32)
        self.vh_s = s("vh_s", [HALO, 1024], F32)
        self.qS_s = s("qS_s", [SC, 1024], F32)
        self.kS_s = s("kS_s", [SC, 1024], F32)
        self.vS_s = s("vS_s", [SC, 1024], F32)
        self.x1_s = s("x1_s", [OWN + 128, D], F32)
        self.x2_s = s("x2_s", [OWN + 128, D], F32)
        self.a_s = s("a_s", [NFC, 128, NT], BF16)

    def alloc_global(self, g):
        sb = self.sb
        self.identb = sb(g, "identb", [128, 128], BF16)
        self.identf = sb(g, "identf", [128, 128], F32)
        self.onesb = sb(g, "onesb", [128, 128], BF16)
        self.rotb = sb(g, "rotb", [128, 128], BF16)
        self.negA = sb(g, "negA", [128, 128], BF16)
        self.negB = sb(g, "negB", [128, 128], BF16)
        self.negAh = sb(g, "negAh", [128, 128], BF16)
        self.hv = sb(g, "hv", [128, 2], F32)
        self.mask2 = sb(g, "mask2", [128, 256], BF16)
        self.mask2h = sb(g, "mask2h", [128, 256], BF16)
        self.colv = sb(g, "colv", [128, 8], F32)
        self.rowv = sb(g, "rowv", [1, 4 * 128 + 8], F32)
        self.pscale = sb(g, "pscale", [128, 8], F32)
        self.cw = sb(g, "cw", [128, NFC, 3], F32)
        self.cb = sb(g, "cb", [128, NFC], F32)
        self.uT_S = sb(g, "uT_S", [128, 8, SC], F32)
        self.qT_S = sb(g, "qT_S", [128, 8, SC], F32)
        self.kT_S = sb(g, "kT_S", [128, 8, SC], F32)
        self.vT_S = sb(g, "vT_S", [128, 8, SC], F32)

    def stage_consts(self):
        nc = self.nc
        P = self.P
        with contextlib.ExitStack() as st:
            tmpf = self.sb(st, "c_tmpf", [128, 128], F32)
            tmp2 = self.sb(st, "c_tmp2", [128, 128], F32)
            cwtA = self.sb(st, "c_cwtA", [128, 128], F32)
            cwtB = self.sb(st, "c_cwtB", [64, 128], F32)
            ps = self.psb(st, "c_ps", 1)[0]
            self.memset(self.identf[:], 1.0, ["identf"])
            P.add("pool", lambda e: e.affine_select(out=self.identf[:], in_=self.identf[:], pattern=[[-1, 128]],
                                                    compare_op=ALU.is_equal, fill=0.0, base=0, channel_multiplier=1),
                  ["identf"], ["identf"])
            self.cp("dve", self.identb[:], self.identf[:], ["identf"], ["identb"])
            self.memset(self.onesb[:], 1.0, ["onesb"])
            self.memset(tmpf[:], 1.0, ["tmpf"])
            P.add("pool", lambda e: e.affine_select(out=tmpf[:], in_=tmpf[:], pattern=[[-1, 128]],
                                                    compare_op=ALU.is_equal, fill=0.0, base=64, channel_multiplier=1),
                  ["tmpf"], ["tmpf"])
            self.memset(tmp2[:], 1.0, ["tmp2"])
            P.add("pool", lambda e: e.affine_select(out=tmp2[:], in_=tmp2[:], pattern=[[-1, 128]],
                                                    compare_op=ALU.is_equal, fill=0.0, base=-64, channel_multiplier=1),
                  ["tmp2"], ["tmp2"])
            self.tt(self.rotb[:], tmpf[:], tmp2[:], ALU.add, ["tmpf", "tmp2"], ["rotb"])
            self.memset(tmpf[:], 0.0, ["tmpf"])
            P.add("pool", lambda e: e.affine_select(out=tmpf[:], in_=tmpf[:], pattern=[[-1, 128]],
                                                    compare_op=ALU.is_ge, fill=NEG, base=0, channel_multiplier=1),
                  ["tmpf"], ["tmpf"])
            self.cp("dve", self.negA[:], tmpf[:], ["tmpf"], ["negA"])
            self.dma(self.hv[:], self.hv_d[:, :], [], ["hv"])
            self.ts(self.negAh[:], tmpf[:], self.hv[:, 1:2], None, ALU.add, None, ["tmpf", "hv"], ["negAh"])
            self.memset(tmp2[:], 0.0, ["tmp2"])
            P.add("pool", lambda e: e.affine_select(out=tmp2[:], in_=tmp2[:], pattern=[[1, 128]],
                                                    compare_op=ALU.is_ge, fill=NEG, base=0, channel_multiplier=-1),
                  ["tmp2"], ["tmp2"])
            self.cp("dve", self.negB[:], tmp2[:], ["tmp2"], ["negB"])
            self.ts(self.mask2[:, 0:128], tmpf[:], 0.0, None, ALU.is_equal, None, ["tmpf"], ["mask2"])
            self.ts(self.mask2[:, 128:256], tmp2[:], 0.0, None, ALU.is_equal, None, ["tmp2"], ["mask2"])
            self.ts(self.mask2h[:, 0:128], self.mask2[:, 0:128], self.hv[:, 0:1], None, ALU.mult, None, ["mask2", "hv"], ["mask2h"])
            self.cp("dve", self.mask2h[:, 128:256], self.mask2[:, 128:256], ["mask2"], ["mask2h"])
            cwr = self.conv_w.rearrange("j (c p) -> (j c) p", p=128)
            self.dma(cwtA[:, :], cwr[0:128, :], [], ["cwt"])
            self.dma(cwtB[0:4, :], cwr[128:132, :], [], ["cwt"])
            self.dma(cwtB[4:48, :], self.conv_b.rearrange("o (c p) -> (o c) p", p=128), [], ["cwt"])
            self.dma(cwtB[48:56, :], self.pool_scale.rearrange("o (c p) -> (o c) p", p=128), [], ["cwt"])
            b0 = NFC * 4 + 8
            for j, v in enumerate((self.q_norm, self.k_norm, self.mem_q_norm, self.mem_k_norm)):
                self.dma(cwtB[56 + j:57 + j, :], v[0:1, :], [], ["cwt"])
            self.tr(ps[:, 0:128], cwtA[:, :], self.identf[:], ["cwt", "identf"], ["cps"])
            self.tr(ps[:, 128:188], cwtB[0:60, :], self.identf[0:60, 0:60], ["cwt", "identf"], ["cps"])
            self.cp("dve", self.cw[:].rearrange("p c j -> p j c"), ps[:, 0:NFC * 3].rearrange("p (j c) -> p j c", j=3),
                    ["cps"], ["cw"])
            self.cp("dve", self.cb[:], ps[:, NFC * 3:NFC * 4], ["cps"], ["cb"])
            self.cp("dve", self.pscale[:], ps[:, NFC * 4:NFC * 4 + 8], ["cps"], ["pscale"])
            self.cp("dve", self.colv[:, 0:3], ps[:, b0:b0 + 3], ["cps"], ["colv"])
            rv = self.rowv
            for j, v in enumerate((self.q_norm, self.k_norm, self.mem_q_norm, self.mem_k_norm)):
                self.dma(rv[0:1, j * 128:(j + 1) * 128], v[0:1, :], [], ["rowv"])
            m0 = 512
            self.red(rv[0:1, m0:m0 + 4], rv[0:1, 0:512].rearrange("o (a b) -> o a b", a=4), ALU.max, ["rowv"], ["rowm"], absv=True)
            self.tt(rv[0:1, m0 + 4:m0 + 5], rv[0:1, m0:m0 + 1], rv[0:1, m0 + 1:m0 + 2], ALU.mult, ["rowm"], ["rowb"])
            self.tt(rv[0:1, m0 + 5:m0 + 6], rv[0:1, m0 + 2:m0 + 3], rv[0:1, m0 + 3:m0 + 4], ALU.mult, ["rowm"], ["rowb"])
            self.ts(rv[0:1, m0 + 6:m0 + 8], rv[0:1, m0 + 4:m0 + 6], -(128.0 ** 0.5), None, ALU.mult, None, ["rowb"], ["rowc"])
            self.memset(tmp2[0:1, :], 1.0, ["tmp2"], eng="dve")
            self.mm(ps[:, 256:258], tmp2[0:1, :], rv[0:1, m0 + 6:m0 + 8], True, True, ["rowc", "tmp2"], ["cps2"])
            self.cp("dve", self.colv[:, 3:5], ps[:, 256:258], ["cps2"], ["colv2"])
            P.flush()

    def norm_transpose(self, src_ap, src_reads, gain_b, gname, hT, col0, hname, bufs, i):
        xt, xn, stat, ptr = bufs
        b = i % 2
        X = "nt_x%d" % b
        XN = "nt_xn%d" % b
        STt = "nt_st%d" % b
        if src_ap is not None:
            self.dma(xt[b][:], src_ap, src_reads, [X])
        self.act(xn[b][:], xt[b][:], AF.Square, [X], [XN, STt + "a"], accum=stat[b][:, 0:1])
        self.act(stat[b][:, 1:2], stat[b][:, 0:1], AF.Ln, [STt + "a"], [STt + "b"], scale=1.0 / D, bias=EPS)
        self.act(stat[b][:, 2:3], stat[b][:, 1:2], AF.Exp, [STt + "b"], [STt + "c"], scale=-0.5)
        self.stt(xn[b][:], xt[b][:], stat[b][:, 2:3], gain_b[:], ALU.mult, ALU.mult, [X, STt + "c", gname], [XN])
        def trans_part():
            for j in range(4):
                pt = ptr.next()
                for q in range(4):
                    kc = 4 * j + q
                    self.tr(pt[1][:, q * 128:(q + 1) * 128], xn[b][:, kc * 128:(kc + 1) * 128], self.identb[:],
                            [XN, "identb"], [pt[0]])
                eng = "act" if j % 2 == 0 else "dve"
                self.cp(eng, hT[:, 4 * j:4 * j + 4, col0:col0 + 128], pt[1][:, 0:512].rearrange("p (a b) -> p a b", a=4),
                        [pt[0]], [hname])

        self.nt_flush()
        self._nt_pend = trans_part

    def nt_flush(self):
        p = getattr(self, "_nt_pend", None)
        self._nt_pend = None
        if p is not None:
            p()

    def nt_bufs(self, st, tag):
        xt = [self.sb(st, "%s_xt%d" % (tag, i), [128, D], F32) for i in range(2)]
        xn = [self.sb(st, "%s_xn%d" % (tag, i), [128, D], BF16) for i in range(2)]
        stat = [self.sb(st, "%s_st%d" % (tag, i), [128, 4], F32) for i in range(2)]
        pts = self.psb(st, tag + "_pt", 2, BF16, 512)
        ptr = Ring([("nt_pt%d" % i, pts[i]) for i in range(2)])
        return xt, xn, stat, ptr

    def load_wblock(self, slot_ap, slot_name, w_ap, c0, ncols, kchunks=KC):
        src = w_ap[:, c0:c0 + ncols].rearrange("(k p) n -> p k n", p=128)
        self.dma(slot_ap[:, 0:kchunks, 0:ncols], src, [], [slot_name], q="pq")

    def stage_proj(self):
        P = self.P
        with contextlib.ExitStack() as st:
            sb = self.sb
            hT = self.arena1
            bufs = self.nt_bufs(st, "pj")
            gain_b = sb(st, "pj_gain", [128, D], F32)
            wsl = [sb(st, "pj_w%d" % i, [128, KC, 512], BF16) for i in range(2)]
            cosT = sb(st, "pj_cos", [128, HALO], F32)
            sinT = sb(st, "pj_sin", [128, HALO], F32)
            nb = 3
            pipe = [None, None]
            sqb = [sb(st, "pj_sq%d" % i, [128, 512], BF16) for i in range(nb)]
            t1f = [sb(st, "pj_t1f%d" % i, [128, 512], F32) for i in range(nb)]
            t1b = [sb(st, "pj_t1b%d" % i, [128, 512], BF16) for i in range(nb)]
            lnv = [sb(st, "pj_ln%d" % i, [128, 512], F32) for i in range(nb)]
            ta = [sb(st, "pj_ta%d" % i, [128, 512], F32) for i in range(nb)]
            tb = [sb(st, "pj_tb%d" % i, [128, 512], F32) for i in range(nb)]
            o32 = [sb(st, "pj_o32%d" % i, [128, 512], F32) for i in range(nb)]
            o16 = [sb(st, "pj_o16%d" % i, [128, 512], BF16) for i in range(nb)]
            otk = [sb(st, "pj_otk%d" % i, [128, 512], F32) for i in range(nb)]
            raw_ps = self.psb(st, "pj_raw", 2)
            ssq_ps = self.psb(st, "pj_ssq", 1)[0]
            rot_ps = self.psb(st, "pj_rot", 1)[0]
            tok_ps = self.psb(st, "pj_tok", 2)
            rawr = Ring([("pj_raw%d" % i, raw_ps[i]) for i in range(2)])
            tokr = Ring([("pj_tok%d" % i, tok_ps[i]) for i in range(2)])
            self.dma(gain_b[:], self.norm_mix[0:1, :].to_broadcast([128, D]), [], ["gain"])
            ucount = [0]

            def do_norm(pname, i):
                e0_ = 0 if pname == "H" else HALO
                src = self.xS[:, :] if (pname == "O" and i == 16) else self.xall[e0_ + i * 128:e0_ + (i + 1) * 128, :]
                self.norm_transpose(src, [], gain_b, "gain", hT, i * 128, "hT%d" % i, bufs, i)

            def run_pass(pname, pre_normed=False, tail_hook=None):
                if pname == "H":
                    ntile, e0 = 17, 0
                    groups = [(0, 512), (512, 512), (1024, 512), (1536, 512), (2048, 128)]
                    self.dma(cosT[:, 0:HALO], self.cosT[:, 0:HALO], [], ["cos"])
                    self.dma(sinT[:, 0:HALO], self.sinT[:, 0:HALO], [], ["sin"])
                elif pname == "O":
                    ntile, e0 = 17, HALO
                    groups = [(0, 512), (512, 512), (1024, 512), (1536, 512), (OWN, SC)]
                    self.dma(cosT[:, 0:OWN], self.cosT[:, HALO:EXT], [], ["cos"])
                    self.dma(sinT[:, 0:OWN], self.sinT[:, HALO:EXT], [], ["sin"])
                    self.dma(cosT[:, OWN:OWN + SC], self.cosS[:, :], [], ["cos"])
                    self.dma(sinT[:, OWN:OWN + SC], self.sinS[:, :], [], ["sin"])
                else:
                    ntile, e0 = 1, 0
                    groups = [(0, SC)]
                    self.dma(cosT[:, 0:SC], self.cosS[:, :], [], ["cos"])
                    self.dma(sinT[:, 0:SC], self.sinS[:, :], [], ["sin"])
                if not pre_normed:
                    for i in range(ntile):
                        do_norm(pname, i)
                    self.nt_flush()
                if pname == "H":
                    blocks = [0, 1, 4, 5, 6, 7]
                else:
                    blocks = list(range(8))
                self.load_wblock(wsl[0], "pj_w0", self.w_in, blocks[0] * 512, 512)
                for bi, blk in enumerate(blocks):
                    if bi + 1 < len(blocks):
                        s2 = (bi + 1) % 2
                        self.load_wblock(wsl[s2], "pj_w%d" % s2, self.w_in, blocks[bi + 1] * 512, 512)
                    sl = bi % 2
                    kind = "uqkv"[blk // 2]
                    tail = (tail_hook is not None and bi == len(blocks) - 1)
                    if tail:
                        order = [(un, g) for g in groups for un in range(4)]
                    else:
                        order = [(un, g) for un in range(4) for g in groups]
                    for (un, (c0, n)) in order:
                        unit = (blk % 2) * 4 + un
                        if True:
                            if pname == "H" and kind == "u" and c0 != 2048:
                                continue
                            rn, rp = rawr.next()
                            tiles = ["hT%d" % t for t in range(c0 // 128, (c0 + n + 127) // 128)]
                            for kc in range(KC):
                                self.mm(rp[:, 0:n], wsl[sl][:, kc, un * 128:(un + 1) * 128], hT[:, kc, c0:c0 + n],
                                        kc == 0, kc == KC - 1, ["pj_w%d" % sl] + tiles, [rn])
                            u = ucount[0] % nb
                            ucount[0] += 1
                            ph = make_phases("S" if (pname == "O" and c0 == OWN) else pname, kind, unit, c0, n, rn, rp, u, e0)
                            ph[0]()
                            if pipe[0] is not None:
                                pipe[0][1]()
                            if pipe[1] is not None:
                                pipe[1][2]()
                            pipe[1] = pipe[0]
                            pipe[0] = ph
                            if tail and un == 3:
                                tail_hook(c0, n)
                if pipe[0] is not None:
                    pipe[0][1]()
                if pipe[1] is not None:
                    pipe[1][2]()
                if pipe[0] is not None:
                    pipe[0][2]()
                pipe[0] = pipe[1] = None

            def make_phases(pname, kind, unit, c0, n, rn, rp, u, e0):
                U = "pj_u%d_" % u
                nop = lambda: None

                def tok_major():
                    nt_ = (n + 127) // 128
                    tn, tp = tokr.next()
                    for t in range(nt_):
                        w = min(128, n - t * 128)
                        self.tr(tp[0:w, t * 128:(t + 1) * 128], o32[u][:, t * 128:t * 128 + w], self.identf[:],
                                [U + "o32", "identf"], [tn])
                    if pname == "S":
                        self.cp("act", otk[u][0:SC, 0:128], tp[0:SC, 0:128], [tn], [U + "otk"])
                        dd = {"q": self.qS_s, "k": self.kS_s, "v": self.vS_s}[kind]
                        self.dma(dd[:, unit * 128:(unit + 1) * 128], otk[u][0:SC, 0:128], [U + "otk"], [self.u()])
                        if kind in "kv":
                            do = self.s_k if kind == "k" else self.s_v
                            self.dma(do[:, unit * 128:(unit + 1) * 128], otk[u][0:16, 0:128], [U + "otk"], [self.u()])
                    else:
                        self.cp("act", otk[u][:, 0:nt_ * 128], tp[:, 0:nt_ * 128], [tn], [U + "otk"])
                        if pname == "H":
                            dd = self.kh_s if kind == "k" else self.vh_s
                        else:
                            dd = self.pk if kind == "k" else self.pv
                        dst = dd[c0:c0 + nt_ * 128, unit * 128:(unit + 1) * 128].rearrange("(t p) d -> p t d", p=128)
                        self.dma(dst, otk[u][:, 0:nt_ * 128].rearrange("p (t d) -> p t d", t=nt_), [U + "otk"], [self.u()])

                def feat_major():
                    if pname == "S":
                        dst = {"q": self.qT_S, "k": self.kT_S, "v": self.vT_S}[kind]
                        self.cp("dve", dst[:, unit, :], o32[u][:, 0:n], [U + "o32"], [kind + "T_S"])
                    elif kind == "q":
                        self.dma(self.qT_s[unit, :, c0:c0 + n], o16[u][:, 0:n], [U + "o16"], [self.u()])
                    else:
                        dsts = self.kT_s if kind == "k" else self.vT_s
                        self.dma(dsts[unit, :, e0 + c0:e0 + c0 + n], o16[u][:, 0:n], [U + "o16"], [self.u()])

                if kind == "u":
                    def a0():
                        self.cp("act", o32[u][:, 0:n], rp[:, 0:n], [rn], [U + "o32"])

                    def b_():
                        if pname == "S":
                            self.cp("dve", self.uT_S[:, unit, :], o32[u][:, 0:n], [U + "o32"], ["uT_S"])
                        else:
                            dc0 = 0 if pname == "H" else 128 + c0
                            self.dma(self.uT_s[unit, :, dc0:dc0 + n], o32[u][:, 0:n], [U + "o32"], [self.u()])
                    return (a0, nop, b_)
                if kind == "v":
                    def a0():
                        self.cp("act", o32[u][:, 0:n], rp[:, 0:n], [rn], [U + "o32"])

                    def a1():
                        self.cp("dve", o16[u][:, 0:n], o32[u][:, 0:n], [U + "o32"], [U + "o16"])

                    def b_():
                        feat_major()
                        tok_major()
                    return (a0, a1, b_)
                gcol = self.colv[:, 0:1] if kind == "q" else self.colv[:, 1:2]

                def a0():
                    self.act(sqb[u][:, 0:n], rp[:, 0:n], AF.Square, [rn], [U + "sq"])
                    self.act(t1f[u][:, 0:n], rp[:, 0:n], AF.Copy, [rn, "colv"], [U + "t1f"], scale=gcol)
                    self.cp("dve", t1b[u][:, 0:n], t1f[u][:, 0:n], [U + "t1f"], [U + "t1b"])

                def a1():
                    self.mm(ssq_ps[:, 0:n], self.onesb[:], sqb[u][:, 0:n], True, True, [U + "sq", "onesb"], ["pj_ssq"])
                    self.mm(rot_ps[:, 0:n], self.rotb[:], t1b[u][:, 0:n], True, True, [U + "t1b", "rotb"], ["pj_rot"])
                    self.act(lnv[u][:, 0:n], ssq_ps[:, 0:n], AF.Ln, ["pj_ssq"], [U + "ln"], scale=1.0 / 128, bias=EPS)
                    self.act(lnv[u][:, 0:n], lnv[u][:, 0:n], AF.Exp, [U + "ln"], [U + "ln"], scale=-0.5)
                    self.tt(ta[u][:, 0:n], t1f[u][:, 0:n], cosT[:, c0:c0 + n], ALU.mult, [U + "t1f", "cos"], [U + "ta"])
                    self.tt(tb[u][:, 0:n], rot_ps[:, 0:n], sinT[:, c0:c0 + n], ALU.mult, ["pj_rot", "sin"], [U + "tb"])
                    self.tt(ta[u][:, 0:n], ta[u][:, 0:n], tb[u][:, 0:n], ALU.add, [U + "ta", U + "tb"], [U + "ta"])
                    if kind == "q" and pname != "S":
                        self.tt(o16[u][:, 0:n], ta[u][:, 0:n], lnv[u][:, 0:n], ALU.mult, [U + "ta", U + "ln"], [U + "o16"])
                    else:
                        self.tt(o32[u][:, 0:n], ta[u][:, 0:n], lnv[u][:, 0:n], ALU.mult, [U + "ta", U + "ln"], [U + "o32"])

                def b_():
                    if kind == "k" and pname != "S":
                        self.cp("act", o16[u][:, 0:n], o32[u][:, 0:n], [U + "o32"], [U + "o16"])
                    feat_major()
                    if kind == "k" or pname == "S":
                        tok_major()
                return (a0, a1, b_)

            def hook(c0, n):
                for t in range(c0 // 128, (c0 + n + 127) // 128):
                    do_norm("O", t)

            run_pass("H")
            run_pass("O")
            P.flush()

    def stage_pool(self):
        P = self.P
        with contextlib.ExitStack() as st:
            sb = self.sb
            mixedT = self.arena1
            L = 16 + OWN
            ue = [sb(st, "pl_ue%d" % i, [128, L], F32) for i in range(2)]
            pa = sb(st, "pl_a", [128, L], F32)
            pb = sb(st, "pl_b", [128, L], F32)
            diffT = sb(st, "pl_diff", [128, 8, OWN + SC], BF16)
            wp = sb(st, "pl_wp", [128, 8, 256], BF16)
            corr = sb(st, "pl_corr", [128, 4, 16], F32)
            ppl = sb(st, "pl_pp", [128, PW], F32)
            ps = self.psb(st, "pl_ps", 2)
            tps = self.psb(st, "pl_tps", 2)
            psr = Ring([("pl_ps%d" % i, ps[i]) for i in range(2)])
            for g_ in range(4):
                self.dma(wp[:, 2 * g_:2 * g_ + 2, :], self.w_pool[g_].rearrange("(i p) o -> p i o", p=128), [], ["wp"], q="pq")
            self.dma(corr[:].rearrange("p g t -> p (g t)"), self.corr_d[:, :], [], ["corr"])
            self.memset(mixedT[:, :, S0:S0 + 128], 0.0, ["mixS"], eng="dve")
            for c in range(8):
                g_ = c // 2
                w = 2 << g_
                b = c % 2
                UE = "pl_ue%d" % b
                self.dma(ue[b][:], self.uT_s[c, :, 112:128 + OWN], [], [UE])
                cur, curname = ue[b], UE
                shift = 1
                pp = [(pa, "pl_a"), (pb, "pl_b")]
                k = 0
                lo = 0
                while shift < w:
                    dst, dname = pp[k % 2]
                    lo += shift
                    self.tt(dst[:, lo:L], cur[:, lo:L], cur[:, lo - shift:L - shift], ALU.add, [curname], [dname])
                    cur, curname = dst, dname
                    shift *= 2
                    k += 1
                self.tt(cur[:, 16:32], cur[:, 16:32], corr[:, g_, :], ALU.mult, [curname, "corr"], [curname])
                self.stt(diffT[:, c, 0:OWN], cur[:, 16:L], 1.0 / w, ue[b][:, 16:L], ALU.mult, ALU.subtract,
                         [curname, UE], ["diff%d" % c])
                tpn = "pl_tps%d" % (c // 4)
                self.tr(tps[c // 4][:, (c % 4) * 128:(c % 4 + 1) * 128], ue[b][:, L - 128:L], self.identf[:], [UE, "identf"], [tpn])
                if c % 4 == 3:
                    self.cp("act", ppl[:, (c // 4) * 512:(c // 4 + 1) * 512], tps[c // 4][:, 0:512], [tpn], ["ppl"])
            self.dma(self.p_pool[:, :], ppl[113:128, :], ["ppl"], [self.u()])
            uext = sb(st, "pl_uext", [128, 8, 5, 19], F32)
            lv = [sb(st, "pl_lv%d" % i, [128, 8, 5, 19], F32) for i in range(4)]
            sp_in = sb(st, "pl_spin", [64, PW], F32)
            stok = sb(st, "pl_stok", [SC, PW], F32)
            self.memset(uext[:], 0.0, ["uext"], eng="dve")
            for li in range(4):
                self.memset(lv[li][:], 0.0, ["pl_lv%d" % li], eng="dve")
            self.dma(sp_in[0:60, :], self.state_pool.rearrange("b t n -> (b t) n"), [], ["spin"])
            tpS = tps[0]
            for c in range(8):
                self.tr(tpS[:, c * 60:(c + 1) * 60], sp_in[0:60, c * 128:(c + 1) * 128], self.identf[0:60, 0:60],
                        ["spin", "identf"], ["pl_tps0"])
            self.cp("dve", uext[:, :, 0:4, 0:15], tpS[:, 0:480].rearrange("p (c b t) -> p c b t", c=8, b=4),
                    ["pl_tps0"], ["uext"])
            self.cp("dve", uext[:, :, 0:4, 15:19], self.uT_S[:, :, 0:16].rearrange("p c (b t) -> p c b t", b=4),
                    ["uT_S"], ["uext"])
            self.dma(uext[:, :, 4, 0:17], self.uT_s[:, :, 111:128].rearrange("c p t -> p c t"), [], ["uext"])
            cur, curname = uext, "uext"
            shift, lo = 1, 0
            for li in range(4):
                dst, dname = lv[li], "pl_lv%d" % li
                lo += shift
                self.tt(dst[:, :, :, lo:19], cur[:, :, :, lo:19], cur[:, :, :, lo - shift:19 - shift], ALU.add, [curname], [dname])
                cur, curname = dst, dname
                shift *= 2
            dS = sb(st, "pl_dS", [128, 8, 5, 19], F32)
            for g_ in range(4):
                w = 2 << g_
                cs = slice(2 * g_, 2 * g_ + 2)
                self.ts(dS[:, cs, :, :], lv[g_][:, cs, :, :], 1.0 / w, None, ALU.mult, None, ["pl_lv%d" % g_], ["pl_dS%d" % g_])
                self.tt(dS[:, cs, :, :], dS[:, cs, :, :], uext[:, cs, :, :], ALU.subtract, ["pl_dS%d" % g_, "uext"], ["pl_dS%d" % g_])
                self.cp("dve", diffT[:, cs, OWN:OWN + 16].rearrange("p c (b t) -> p c b t", b=4), dS[:, cs, 0:4, 15:19],
                        ["pl_dS%d" % g_], ["diffS"])
                self.cp("dve", diffT[:, cs, OWN + 16:OWN + 18], dS[:, cs, 4, 15:17], ["pl_dS%d" % g_], ["diffS"])
            self.memset(diffT[:, :, OWN + 18:OWN + SC], 0.0, ["diffS"], eng="dve")
            for c in range(8):
                self.tr(tps[1][0:SC, (c % 4) * 128:(c % 4 + 1) * 128], self.uT_S[:, c, :], self.identf[:], ["uT_S", "identf"], ["pl_tps1"])
                if c % 4 == 3:
                    self.cp("act", stok[0:SC, (c // 4) * 512:(c // 4 + 1) * 512], tps[1][0:SC, 0:512], ["pl_tps1"], ["stok"])
            for b_ in range(4):
                self.dma(self.s_pool[b_, 11:15, :], stok[4 * b_:4 * b_ + 4, :], ["stok"], [self.u()])
            self.dma(self.s_pool[:, 0:11, :], self.state_pool[:, 4:15, :], [], [self.u()])
            groups = [(0, 512), (512, 512), (1024, 512), (1536, 512), (OWN, SC)]
            for g_ in range(4):
                for oc in range(2):
                    for (c0, n) in groups:
                        pn, pp_ = psr.next()
                        dn = ["diffS"] if c0 == OWN else ["diff%d" % (2 * g_), "diff%d" % (2 * g_ + 1)]
                        for ic in range(2):
                            self.mm(pp_[:, 0:n], wp[:, 2 * g_ + ic, oc * 128:(oc + 1) * 128], diffT[:, 2 * g_ + ic, c0:c0 + n],
                                    ic == 0, ic == 1, ["wp"] + dn, [pn])
                        ch = 2 * g_ + oc
                        self.act(mixedT[:, ch, c0:c0 + n], pp_[:, 0:n], AF.Copy, [pn, "pscale"],
                                 ["mix%d_%d" % (ch, c0)] + (["mixS"] if c0 == OWN else []), scale=self.pscale[:, ch:ch + 1])
            P.flush()

    def stage_attn(self):
        P = self.P
        with contextlib.ExitStack() as st:
            sb = self.sb
            mixedT = self.arena1
            kT = [sb(st, "at_k%d" % i, [128, EXT], BF16) for i in range(2)]
            vT = [sb(st, "at_v%d" % i, [128, EXT], BF16) for i in range(2)]
            qT = [sb(st, "at_q%d" % i, [128, OWN], BF16) for i in range(2)]
            NVT = 17 + 20 + 32
            Vt = [sb(st, "at_vt%d" % i, [128, NVT, 128], BF16) for i in range(2)]
            ACC = [sb(st, "at_acc%d" % i, [128, 2, OWN], F32) for i in range(2)]
            Eb = [sb(st, "at_e%d" % i, [128, 256], BF16) for i in range(5)]
            tmpf = sb(st, "at_tmp", [128, OWN], F32)
            s_ps = self.psb(st, "at_s", 4)
            nd_ps = self.psb(st, "at_nd", 2)
            vt_ps = self.psb(st, "at_vp", 2, BF16, 512)
            sr = Ring([("at_s%d" % i, s_ps[i]) for i in range(4)])
            ndr = Ring([("at_nd%d" % i, nd_ps[i]) for i in range(2)])
            vpr = Ring([("at_vp%d" % i, vt_ps[i]) for i in range(2)])
            er = Ring([("at_e%d" % i, Eb[i]) for i in range(5)])
            negBias = self.colv[:, 3:4]

            def load_head(h):
                b = h % 2
                self.dma(kT[b][:], self.kT_s[h, :, :], [], ["at_k%d" % b])
                self.dma(vT[b][:], self.vT_s[h, :, :], [], ["at_v%d" % b])
                self.dma(qT[b][:], self.qT_s[h, :, :], [], ["at_q%d" % b])

            vidx = {}
            n = 0
            for d in DILS:
                for r in range(d):
                    for j in range(16 // d + 1):
                        vidx[(d, r, j)] = n
                        n += 1
            assert n == NVT
            load_head(0)
            for h in range(NH):
                b = h % 2
                if h + 1 < NH:
                    load_head(h + 1)
                KN, VN, QN, VTN, AN = "at_k%d" % b, "at_v%d" % b, "at_q%d" % b, "at_vt%d" % b, "at_acc%d" % b
                keys = sorted(vidx, key=lambda kk: vidx[kk])
                for g0 in range(0, NVT, 4):
                    pn, pp_ = vpr.next()
                    grp = keys[g0:g0 + 4]
                    for q_, (d, r, j) in enumerate(grp):
                        e0 = HALO - 128 * d + d * 128 * j + r
                        self.tr(pp_[:, q_ * 128:(q_ + 1) * 128], vT[b][:, e0:e0 + 127 * d + 1:d], self.identb[:], [VN, "identb"], [pn])
                    ng = len(grp)
                    eng = "act" if (g0 // 4) % 2 == 0 else "dve"
                    self.cp(eng, Vt[b][:, g0:g0 + ng, :], pp_[:, 0:ng * 128].rearrange("p (a b) -> p a b", a=ng), [pn],
                            [VTN + "_%d" % (g0 // 4)])
                units = [(d, r, blk) for d in DILS for r in range(d) for blk in range(16 // d)]

                def s_part(d, r, blk):
                    eA = HALO - 128 * d + d * 128 * blk + r
                    eB = eA + 128 * d
                    q0 = eB - HALO
                    kA = kT[b][:, eA:eA + 127 * d + 1:d]
                    kB = kT[b][:, eB:eB + 127 * d + 1:d]
                    qB = qT[b][:, q0:q0 + 127 * d + 1:d]
                    sn, sp_ = sr.next()
                    mk_ = self.mask2h if blk == 0 else self.mask2
                    self.mm(sp_[:, 0:128], kA, qB, True, True, [KN, QN], [sn])
                    self.mm(sp_[:, 128:256], kB, qB, True, True, [KN, QN], [sn])
                    en, eb = er.next()
                    self.act(eb[:], sp_[:, 0:256], AF.Exp, [sn, "colv2"], [en], scale=SCALE, bias=negBias)
                    self.tt(eb[:], eb[:], mk_[:], ALU.mult, [en], [en])
                    return (d, r, blk, q0, en, eb)

                def pv_part(d, r, blk, q0, en, eb):
                    ia, ib = vidx[(d, r, blk)], vidx[(d, r, blk + 1)]
                    nn, np_ = ndr.next()
                    self.mm(np_[:, 0:128], Vt[b][:, ia, :], eb[:, 0:128], True, False, [VTN + "_%d" % (ia // 4), en], [nn])
                    self.mm(np_[:, 0:128], Vt[b][:, ib, :], eb[:, 128:256], False, True, [VTN + "_%d" % (ib // 4), en], [nn])
                    self.mm(np_[:, 128:256], self.onesb[:], eb[:, 0:128], True, False, ["onesb", en], [nn])
                    self.mm(np_[:, 128:256], self.onesb[:], eb[:, 128:256], False, True, ["onesb", en], [nn])
                    dst = ACC[b][:, :, q0:q0 + 127 * d + 1:d]
                    src = np_[:, 0:256].rearrange("p (a b) -> p a b", a=2)
                    if d == 1:
                        self.cp("dve", dst, src, [nn], [AN])
                    else:
                        self.tt(dst, src, dst, ALU.add, [nn, AN], [AN])

                pend = []
                for un_ in units:
                    pend.append(s_part(*un_))
                    if len(pend) > 2:
                        pv_part(*pend.pop(0))
                while pend:
                    pv_part(*pend.pop(0))
                self.act(tmpf[:], ACC[b][:, 1, :], AF.Ln, [AN], ["at_tmp"])
                self.act(tmpf[:], tmpf[:], AF.Exp, ["at_tmp"], ["at_tmp"], scale=-1.0)
                self.tt(mixedT[:, 8 + h, 0:OWN], ACC[b][:, 0, :], tmpf[:], ALU.mult, [AN, "at_tmp"], ["mixh%d" % h])
            P.flush()

    def stage_attn_s(self):
        return

    def stage_wout(self):
        P = self.P
        with contextlib.ExitStack() as st:
            sb = self.sb
            mixedT = self.arena1
            Kg = [sb(st, "as_k%d" % i, [128, 3, 1024], F32) for i in range(2)]
            Vg = [sb(st, "as_v%d" % i, [128, 3, 1024], F32) for i in range(2)]
            qb = [sb(st, "as_q%d" % i, [128, 1024], F32) for i in range(2)]
            prod = sb(st, "as_prod", [128, 3, 1024], F32)
            prodv = sb(st, "as_prodv", [128, 24, 128], BF16)
            Sg = sb(st, "as_S", [128, 24], F32)
            Eg = sb(st, "as_E", [128, 24], F32)
            Egb = sb(st, "as_Eb", [128, 24], BF16)
            gm = sb(st, "as_gm", [128, 2, 24], F32)
            num_ps = self.psb(st, "as_num", 1)[0]
            den_ps = self.psb(st, "as_den", 1)[0]
            self_ps = self.psb(st, "as_self", 1)[0]
            negBias = self.colv[:, 3:4]
            self.dma(gm[:].rearrange("p a b -> p (a b)"), self.gmask_d[:, :], [], ["gm"])
            numv = num_ps[:, 0:8 * SC].rearrange("p (h j) -> p h j", h=8)
            denv = den_ps[:, 0:8 * SC].rearrange("p (h j) -> p h j", h=8)
            ntok = 18

            def gather(j):
                b = j % 2
                for bi, d in enumerate(DILS):
                    if j < 16:
                        bl, t = j // 4, j % 4
                        if d == 1:
                            for (G, src, newsrc, nm) in ((Kg, self.cwk, self.kS_s, "as_k%d" % b), (Vg, self.cwv, self.vS_s, "as_v%d" % b)):
                                self.dma(G[b][:, bi, :], src[bl, 1920:2048, :], [], [nm])
                                if t:
                                    self.dma(G[b][0:t, bi, :], newsrc[4 * bl:4 * bl + t, :], [], [nm])
                        else:
                            r0 = 2048 + t - 128 * d
                            for (G, src, nm) in ((Kg, self.cwk, "as_k%d" % b), (Vg, self.cwv, "as_v%d" % b)):
                                self.dma(G[b][:, bi, :], src[bl, r0:r0 + 127 * d + 1:d, :], [], [nm])
                    else:
                        t = j - 16
                        r0 = HALO - 2 + t - 128 * d
                        for (G, src, nm) in ((Kg, self.kh_s, "as_k%d" % b), (Vg, self.vh_s, "as_v%d" % b)):
                            self.dma(G[b][:, bi, :], src[r0:r0 + 127 * d + 1:d, :], [], [nm])
                self.dma(qb[b][:], self.qS_s[j:j + 1, :].to_broadcast([128, 1024]), [], ["as_q%d" % b])

            def token_step(j):
                b = j % 2
                if j + 1 < ntok:
                    gather(j + 1)
                KN, VN, QN = "as_k%d" % b, "as_v%d" % b, "as_q%d" % b
                self.tt(prod[:], Kg[b][:], qb[b][:].unsqueeze(1).to_broadcast([128, 3, 1024]), ALU.mult, [KN, QN], ["as_prod"])
                self.red(Sg[:], prod[:].rearrange("p a (h d) -> p (a h) d", h=8), ALU.add, ["as_prod"], ["as_S"])
                self.act(Eg[:], Sg[:], AF.Exp, ["as_S", "colv2"], ["as_E"], scale=SCALE, bias=negBias)
                if j >= 16:
                    self.tt(Eg[:], Eg[:], gm[:, j - 16, :], ALU.mult, ["as_E", "gm"], ["as_E"])
                self.cp("dve", Egb[:], Eg[:], ["as_E"], ["as_Eb"])
                self.tt(prodv[:], Vg[b][:].rearrange("p a (h d) -> p (a h) d", h=8), Eg[:].unsqueeze(2).to_broadcast([128, 24, 128]),
                        ALU.mult, [VN, "as_E"], ["as_prodv"])
                for h in range(NH):
                    for bi in range(3):
                        self.mm(numv[:, h, j:j + 1], prodv[:, bi * 8 + h, :], self.onesb[:, 0:1], bi == 0, bi == 2,
                                ["as_prodv", "onesb"], ["as_num"])
                for bi in range(3):
                    self.mm(denv[:, :, j], self.onesb[:], Egb[:, bi * 8:(bi + 1) * 8], bi == 0, bi == 2, ["as_Eb", "onesb"], ["as_den"])

            def finalize_s():
                pqk = sb(st, "as_pqk", [128, 8, SC], BF16)
                Es = sb(st, "as_Es", [128, 8, SC], F32)
                numS = sb(st, "as_numS", [128, 8, SC], F32)
                denS = sb(st, "as_denS", [128, 8, SC], F32)
                self.tt(pqk[:], self.qT_S[:], self.kT_S[:], ALU.mult, ["qT_S", "kT_S"], ["as_pqk"])
                self.mm(self_ps[:, 0:8 * SC], self.onesb[:], pqk[:].rearrange("p h j -> p (h j)"), True, True, ["as_pqk", "onesb"], ["as_self"])
                self.act(Es[:].rearrange("p h j -> p (h j)"), self_ps[:, 0:8 * SC], AF.Exp, ["as_self", "colv2"], ["as_Es"],
                         scale=SCALE, bias=negBias)
                self.ts(Es[:], Es[:], 3.0, None, ALU.mult, None, ["as_Es"], ["as_Es"])
                self.tt(numS[:], Es[:], self.vT_S[:], ALU.mult, ["as_Es", "vT_S"], ["as_numS"])
                self.tt(numS[:, :, 0:ntok], numS[:, :, 0:ntok], numv[:, :, 0:ntok], ALU.add, ["as_numS", "as_num"], ["as_numS"])
                self.tt(denS[:, :, 0:ntok], Es[:, :, 0:ntok], denv[:, :, 0:ntok], ALU.add, ["as_Es", "as_den"], ["as_denS"])
                P.add("dve", lambda e: e.reciprocal(denS[:, :, 0:ntok], denS[:, :, 0:ntok]), ["as_denS"], ["as_denS"])
                self.tt(mixedT[:, 8:16, S0:S0 + ntok], numS[:, :, 0:ntok], denS[:, :, 0:ntok], ALU.mult, ["as_numS", "as_denS"], ["mixS2"])

            NX = 4
            wsl = [sb(st, "wo_w%d" % i, [128, KC, 512], BF16) for i in range(2)]
            xs = [sb(st, "wo_x%d" % i, [128, 512], F32) for i in range(NX)]
            ys = [sb(st, "wo_y%d" % i, [128, 512], F32) for i in range(NX)]
            ps = self.psb(st, "wo_ps", 4)
            psr = Ring([("wo_ps%d" % i, ps[i]) for i in range(4)])
            kk = [0]

            def wout_tile(nb_, t, sl):
                b = kk[0] % NX
                kk[0] += 1
                col0 = t * 128
                if t < 16:
                    xsrc = self.xall[HALO + col0:HALO + col0 + 128, nb_ * 512:(nb_ + 1) * 512]
                else:
                    xsrc = self.xS[:, nb_ * 512:(nb_ + 1) * 512]
                self.dma(xs[b][:], xsrc, [], ["wo_x%d" % b])
                pn, pp_ = psr.next()
                extra = ["mixS2"] if t == 16 else []
                for kc in range(KC):
                    self.mm(pp_[:, :], mixedT[:, kc, col0:col0 + 128], wsl[sl][:, kc, :], kc == 0, kc == KC - 1, ["wo_w%d" % sl] + extra, [pn])
                self.tt(ys[b][:], pp_[:, :], xs[b][:], ALU.add, [pn, "wo_x%d" % b], ["wo_y%d" % b])
                self.dma(self.x1_s[col0:col0 + 128, nb_ * 512:(nb_ + 1) * 512], ys[b][:], ["wo_y%d" % b], [self.u()], q="aq")

            wseq = [0, 1, 2, 3, 0, 1, 2, 3]
            self.load_wblock(wsl[0], "wo_w0", self.w_out, 0, 512)
            gather(0)
            ntiles = 64
            done_tok = 0
            ti = 0
            for wi, nb_ in enumerate(wseq):
                if wi + 1 < len(wseq):
                    s2 = (wi + 1) % 2
                    self.load_wblock(wsl[s2], "wo_w%d" % s2, self.w_out, wseq[wi + 1] * 512, 512)
                sl = wi % 2
                if wi < 4:
                    for t in range(16):
                        wout_tile(nb_, t, sl)
                        ti += 1
                        want = min(ntok, (ti * ntok + ntiles - 1) // ntiles)
                        while done_tok < want:
                            token_step(done_tok)
                            done_tok += 1
                    if wi == 3:
                        while done_tok < ntok:
                            token_step(done_tok)
                            done_tok += 1
                        finalize_s()
                else:
                    wout_tile(nb_, 16, sl)
            P.flush()

    def stage_mem(self):
        P = self.P
        with contextlib.ExitStack() as st:
            sb = self.sb
            mkT = sb(st, "mm_mkT", [128, 5, MH, 256], BF16)
            mv = sb(st, "mm_mv", [128, 5, 2, 512], BF16)
            omT = sb(st, "mm_omT", [128, MH, NT], BF16)
            h2T = self.arena1
            with contextlib.ExitStack() as st2:
                bufs = self.nt_bufs(st2, "m1")
                gain_b = sb(st2, "m1_gain", [128, D], F32)
                wk = sb(st2, "m1_wk", [128, KC, 512], BF16)
                wv = sb(st2, "m1_wv", [128, KC, 512], BF16)
                gkb = sb(st2, "m1_gkb", [128, 128], F32)
                k32 = sb(st2, "m1_k32", [128, 512], F32)
                k32b = sb(st2, "m1_k32b", [128, 512], F32)
                kst = sb(st2, "m1_kst", [128, 12], F32)
                k16 = [sb(st2, "m1_k16_%d" % i, [128, 512], BF16) for i in range(2)]
                raw_ps = self.psb(st2, "m1_raw", 2)
                rawr = Ring([("m1_raw%d" % i, raw_ps[i]) for i in range(2)])
                tp_ps = bufs[3]
                self.load_wblock(wk, "m1_wk", self.w_mem_k, 0, 512)
                self.load_wblock(wv, "m1_wv", self.w_mem_v, 0, 512)
                self.dma(gain_b[:], self.norm_mem_src[0:1, :].to_broadcast([128, D]), [], ["gain"])
                self.dma(gkb[:], self.mem_k_norm[0:1, :].to_broadcast([128, 128]), [], ["gkb"])
                mT = h2T
                for i in range(2):
                    self.norm_transpose(self.mem_prompt[i * 128:(i + 1) * 128, :], [], gain_b, "gain", mT, i * 128, "h2T%d" % i, bufs, i)
                self.nt_flush()
                v4 = lambda ap: ap.rearrange("p (h d) -> p h d", h=4)
                for mt in range(2):
                    rn, rp = rawr.next()
                    for kc in range(KC):
                        self.mm(rp[:, :], mT[:, kc, mt * 128:(mt + 1) * 128], wk[:, kc, :], kc == 0, kc == KC - 1, ["m1_wk", "h2T%d" % mt], [rn])
                    self.cp("act", k32b[:], rp[:, :], [rn], ["m1_k32b"])
                    self.tt(k32[:], k32b[:], k32b[:], ALU.mult, ["m1_k32b"], ["m1_k32"])
                    self.red(kst[:, 0:4], v4(k32[:]), ALU.add, ["m1_k32"], ["m1_ksta"])
                    self.act(kst[:, 4:8], kst[:, 0:4], AF.Ln, ["m1_ksta"], ["m1_kstb"], scale=1.0 / 128, bias=EPS)
                    self.act(kst[:, 8:12], kst[:, 4:8], AF.Exp, ["m1_kstb"], ["m1_kstc"], scale=-0.5)
                    self.tt(v4(k32[:]), v4(k32b[:]), kst[:, 8:12].unsqueeze(2).to_broadcast([128, 4, 128]), ALU.mult,
                            ["m1_k32b", "m1_kstc"], ["m1_k32"])
                    self.tt(v4(k32[:]), v4(k32[:]), gkb[:].unsqueeze(1).to_broadcast([128, 4, 128]), ALU.mult, ["m1_k32", "gkb"], ["m1_k32"])
                    self.dma(self.pmk[mt * 128:(mt + 1) * 128, :], k32[:], ["m1_k32"], [self.u()])
                    self.cp("act", k16[0][:], k32[:], ["m1_k32"], ["m1_k16_0"])
                    pt = tp_ps.next()
                    for hm in range(MH):
                        self.tr(pt[1][:, hm * 128:(hm + 1) * 128], k16[0][:, hm * 128:(hm + 1) * 128], self.identb[:], ["m1_k16_0", "identb"], [pt[0]])
                    self.cp("dve", mkT[:, 4, :, mt * 128:(mt + 1) * 128], pt[1][:, 0:512].rearrange("p (a b) -> p a b", a=4), [pt[0]], ["mkT"])
                    rn, rp = rawr.next()
                    for kc in range(KC):
                        self.mm(rp[:, :], mT[:, kc, mt * 128:(mt + 1) * 128], wv[:, kc, :], kc == 0, kc == KC - 1, ["m1_wv", "h2T%d" % mt], [rn])
                    self.cp("act", k32b[:], rp[:, :], [rn], ["m1_k32b"])
                    self.dma(self.pmv[mt * 128:(mt + 1) * 128, :], k32b[:], ["m1_k32b"], [self.u()])
                    self.cp("dve", mv[:, 4, mt, :], k32b[:], ["m1_k32b"], ["mv"])
                ki = 0
                for bl in range(4):
                    for mt in range(2):
                        kb = ki % 2
                        ki += 1
                        KN = "m1_k16_%d" % kb
                        self.dma(k16[kb][:], self.cmk[bl, mt * 128:(mt + 1) * 128, :], [], [KN], q="pq")
                        pt = tp_ps.next()
                        for hm in range(MH):
                            self.tr(pt[1][:, hm * 128:(hm + 1) * 128], k16[kb][:, hm * 128:(hm + 1) * 128], self.identb[:], [KN, "identb"], [pt[0]])
                        self.cp("dve", mkT[:, bl, :, mt * 128:(mt + 1) * 128], pt[1][:, 0:512].rearrange("p (a b) -> p a b", a=4), [pt[0]], ["mkT"])
                        self.dma(mv[:, bl, mt, :], self.cmv[bl, mt * 128:(mt + 1) * 128, :], [], ["mv"], q="pq")
                P.flush()
            with contextlib.ExitStack() as st2:
                bufs = self.nt_bufs(st2, "m2")
                gain_b = sb(st2, "m2_gain", [128, D], F32)
                wq = sb(st2, "m2_wq", [128, KC, 512], BF16)
                raw_ps = self.psb(st2, "m2_raw", 2)
                ssq_ps = self.psb(st2, "m2_ssq", 1)[0]
                s_ps = self.psb(st2, "m2_s", 2)
                o_ps = self.psb(st2, "m2_o", 1)[0]
                rawr = Ring([("m2_raw%d" % i, raw_ps[i]) for i in range(2)])
                self.load_wblock(wq, "m2_wq", self.w_mem_q, 0, 512)
                self.dma(gain_b[:], self.norm_mem[0:1, :].to_broadcast([128, D]), [], ["gain"])
                for i in range(17):
                    self.norm_transpose(self.x1_s[i * 128:(i + 1) * 128, :], [], gain_b, "gain", h2T, i * 128, "h2T%d" % i, bufs, i)
                self.nt_flush()
                nb = 3
                sqb = [sb(st2, "m2_sq%d" % i, [128, 512], BF16) for i in range(nb)]
                t1f = [sb(st2, "m2_t1f%d" % i, [128, 512], F32) for i in range(nb)]
                lnv = [sb(st2, "m2_ln%d" % i, [128, 512], F32) for i in range(nb)]
                qm = [sb(st2, "m2_qm%d" % i, [128, 512], BF16) for i in range(nb)]
                Em = [sb(st2, "m2_E%d" % i, [128, 2, 512], BF16) for i in range(nb)]
                lnd = [sb(st2, "m2_lnd%d" % i, [128, 512], F32) for i in range(nb)]
                negBm = self.colv[:, 4:5]
                groups = [(0, 512), (512, 512), (1024, 512), (1536, 512), (S0, SC)]
                self.memset(omT[:, :, S0:S0 + 128], 0.0, ["omTpad"], eng="dve")

                def make_unit(hm, c0, n, u):
                    U = "m2_u%d_" % u
                    if c0 < S0:
                        segs = [(4, 0, n)]
                        nv = n
                    else:
                        segs = [(0, 0, 4), (1, 4, 4), (2, 8, 4), (3, 12, 4), (4, 16, 2)]
                        nv = 18
                    st_ = {}

                    def px():
                        rn, rp = rawr.next()
                        tiles = ["h2T%d" % t for t in range(c0 // 128, (c0 + n + 127) // 128)]
                        for kc in range(KC):
                            self.mm(rp[:, 0:n], wq[:, kc, hm * 128:(hm + 1) * 128], h2T[:, kc, c0:c0 + n], kc == 0, kc == KC - 1,
                                    ["m2_wq"] + tiles, [rn])
                        self.act(sqb[u][:, 0:n], rp[:, 0:n], AF.Square, [rn], [U + "sq"])
                        self.act(t1f[u][:, 0:n], rp[:, 0:n], AF.Copy, [rn, "colv"], [U + "t1f"], scale=self.colv[:, 2:3])

                    def py():
                        self.mm(ssq_ps[:, 0:n], self.onesb[:], sqb[u][:, 0:n], True, True, [U + "sq", "onesb"], ["m2_ssq"])
                        self.act(lnv[u][:, 0:n], ssq_ps[:, 0:n], AF.Ln, ["m2_ssq"], [U + "ln"], scale=1.0 / 128, bias=EPS)
                        self.act(lnv[u][:, 0:n], lnv[u][:, 0:n], AF.Exp, [U + "ln"], [U + "ln"], scale=-0.5)
                        self.tt(qm[u][:, 0:n], t1f[u][:, 0:n], lnv[u][:, 0:n], ALU.mult, [U + "t1f", U + "ln"], [U + "qm"])
                        for mc in range(2):
                            for (sidx, o0, nn) in segs:
                                self.mm(s_ps[mc][:, o0:o0 + nn], mkT[:, sidx, hm, mc * 128:(mc + 1) * 128], qm[u][:, o0:o0 + nn],
                                        True, True, ["mkT", U + "qm"], ["m2_s%d" % mc])
                        for mc in range(2):
                            self.act(Em[u][:, mc, 0:nv], s_ps[mc][:, 0:nv], AF.Exp, ["m2_s%d" % mc, "colv2"], [U + "E"], scale=SCALE, bias=negBm)

                    def pz():
                        for (sidx, o0, nn) in segs:
                            for mc in range(2):
                                self.mm(o_ps[:, o0:o0 + nn], mv[:, sidx, mc, hm * 128:(hm + 1) * 128], Em[u][:, mc, o0:o0 + nn],
                                        mc == 0, mc == 1, ["mv", U + "E"], ["m2_o"])
                        for mc in range(2):
                            self.mm(ssq_ps[:, 0:nv], self.onesb[:], Em[u][:, mc, 0:nv], mc == 0, mc == 1, ["onesb", U + "E"], ["m2_ssq"])
                        self.act(lnd[u][:, 0:nv], ssq_ps[:, 0:nv], AF.Ln, ["m2_ssq"], [U + "lnd"])
                        self.act(lnd[u][:, 0:nv], lnd[u][:, 0:nv], AF.Exp, [U + "lnd"], [U + "lnd"], scale=-1.0)
                        self.tt(omT[:, hm, c0:c0 + nv], o_ps[:, 0:nv], lnd[u][:, 0:nv], ALU.mult, ["m2_o", U + "lnd"],
                                ["omT%d_%d" % (hm, c0)] + (["omTpad"] if c0 == S0 else []))
                    return (px, py, pz)

                pipe = [None, None]
                cnt = 0
                for hm in range(MH):
                    for (c0, n) in groups:
                        ph = make_unit(hm, c0, n, cnt % nb)
                        cnt += 1
                        ph[0]()
                        if pipe[0] is not None:
                            pipe[0][1]()
                        if pipe[1] is not None:
                            pipe[1][2]()
                        pipe[1] = pipe[0]
                        pipe[0] = ph
                pipe[0][1]()
                pipe[1][2]()
                pipe[0][2]()
                P.flush()
            with contextlib.ExitStack() as st2:
                h3T = self.arena1
                bufs = self.nt_bufs(st2, "m3")
                gain_b = sb(st2, "m3_gain", [128, D], F32)
                wmo = sb(st2, "m3_wmo", [128, MH, D], BF16)
                x1t = [sb(st2, "m3_x1t%d" % i, [128, D], F32) for i in range(2)]
                big_ps = self.psb(st2, "m3_big", 4)
                self.dma(wmo[:, :, 0:1024], self.w_mem_o[:, 0:1024].rearrange("(k p) n -> p k n", p=128), [], ["m3_wmo"], q="pq")
                self.dma(wmo[:, :, 1024:2048], self.w_mem_o[:, 1024:2048].rearrange("(k p) n -> p k n", p=128), [], ["m3_wmo"], q="pq")
                self.dma(gain_b[:], self.norm_ffn[0:1, :].to_broadcast([128, D]), [], ["gain"])
                xt = bufs[0]
                self.dma(x1t[0][:], self.x1_s[0:128, :], [], ["m3_x1t0"])
                for t in range(17):
                    b = t % 2
                    col0 = t * 128
                    if t + 1 < 17:
                        self.dma(x1t[1 - b][:], self.x1_s[col0 + 128:col0 + 256, :], [], ["m3_x1t%d" % (1 - b)])
                    for nb_ in range(4):
                        for hm in range(MH):
                            self.mm(big_ps[nb_][:, :], omT[:, hm, col0:col0 + 128], wmo[:, hm, nb_ * 512:(nb_ + 1) * 512], hm == 0, hm == MH - 1,
                                    ["m3_wmo"], ["m3_big%d" % nb_])
                        self.tt(xt[b][:, nb_ * 512:(nb_ + 1) * 512], big_ps[nb_][:, :], x1t[b][:, nb_ * 512:(nb_ + 1) * 512], ALU.add,
                                ["m3_big%d" % nb_, "m3_x1t%d" % b], ["nt_x%d" % b])
                    self.dma(self.x2_s[col0:col0 + 128, :], xt[b][:], ["nt_x%d" % b], [self.u()], q="pq")
                    self.nt_flush()
                    self.norm_transpose(None, [], gain_b, "gain", h3T, col0, "h3T%d" % t, bufs, t)
                self.nt_flush()
                P.flush()

    def stage_ffn_a(self):
        P = self.P
        with contextlib.ExitStack() as st:
            sb = self.sb
            h3T = self.arena1
            wg = [sb(st, "fa_wg%d" % i, [128, KC, 512], BF16) for i in range(2)]
            wu = [sb(st, "fa_wu%d" % i, [128, KC, 512], BF16) for i in range(2)]
            gs = [sb(st, "fa_gs%d" % i, [128, 2 + OWN], F32) for i in range(2)]
            cbuf = [sb(st, "fa_c%d" % i, [128, 512], F32) for i in range(2)]
            sbuf_ = [sb(st, "fa_s%d" % i, [128, 512], F32) for i in range(2)]
            abuf = [sb(st, "fa_a%d" % i, [128, 512], BF16) for i in range(3)]
            gS = sb(st, "fa_gS", [128, SC], F32)
            extS = sb(st, "fa_extS", [128, 4, 6], F32)
            cS = sb(st, "fa_cS", [128, 4, 4], F32)
            aS = sb(st, "fa_aS", [128, SC], BF16)
            convst = sb(st, "fa_convst", [128, NFC, 4, 2], F32)
            sconvT = sb(st, "fa_sconvT", [128, 4, 2, 64], F32)
            pconvT = sb(st, "fa_pconvT", [128, 2, 64], F32)
            scin = sb(st, "fa_scin", [8, DFF], F32)
            outst = sb(st, "fa_outst", [128, 1024], F32)
            g_ps = self.psb(st, "fa_g", 2)
            u_ps = self.psb(st, "fa_u", 2)
            sm_ps = self.psb(st, "fa_sm", 2)
            gr = Ring([("fa_g%d" % i, g_ps[i]) for i in range(2)])
            ur = Ring([("fa_u%d" % i, u_ps[i]) for i in range(2)])
            self.dma(scin[:], self.state_conv.rearrange("b t n -> (b t) n"), [], ["fa_scin"])
            for fc in range(NFC):
                self.tr(sm_ps[0][:, fc * 8:(fc + 1) * 8], scin[0:8, fc * 128:(fc + 1) * 128], self.identf[0:8, 0:8], ["fa_scin", "identf"], ["fa_sm0"])
            self.cp("dve", convst[:].rearrange("p c b t -> p (c b t)"), sm_ps[0][:, 0:NFC * 8], ["fa_sm0"], ["fa_convst"])
            self.memset(aS[:], 0.0, ["fa_aS"], eng="dve")
            self.memset(sconvT[:], 0.0, ["fa_sconvT"], eng="dve")
            self.memset(pconvT[:], 0.0, ["fa_pconvT"], eng="dve")

            def loadw(blk):
                s = blk % 2
                self.load_wblock(wg[s], "fa_wg%d" % s, self.w_gate, blk * 512, 512)
                self.load_wblock(wu[s], "fa_wu%d" % s, self.w_up, blk * 512, 512)

            loadw(0)
            groups = [(0, 512), (512, 512), (1024, 512), (1536, 512)]
            tiles_of = lambda c0, n: ["h3T%d" % t for t in range(c0 // 128, (c0 + n + 127) // 128)]
            ka = 0
            import os
            nfc_run = int(os.environ.get("FFA_NFC", NFC))
            for fc in range(nfc_run):
                blk, un = fc // 4, fc % 4
                if un == 0 and blk + 1 < NFC // 4:
                    loadw(blk + 1)
                s = blk % 2
                gb_ = fc % 2
                GS = "fa_gs%d" % gb_
                w0, w1, w2, bb = self.cw[:, fc, 0:1], self.cw[:, fc, 1:2], self.cw[:, fc, 2:3], self.cb[:, fc:fc + 1]
                gn, gp = gr.next()
                un_, up = ur.next()
                for kc in range(KC):
                    self.mm(gp[:, 0:SC], wg[s][:, kc, un * 128:(un + 1) * 128], h3T[:, kc, S0:S0 + SC], kc == 0, kc == KC - 1, ["fa_wg%d" % s, "h3T16"], [gn])
                for kc in range(KC):
                    self.mm(up[:, 0:SC], wu[s][:, kc, un * 128:(un + 1) * 128], h3T[:, kc, S0:S0 + SC], kc == 0, kc == KC - 1, ["fa_wu%d" % s, "h3T16"], [un_])
                self.cp("act", gS[:], gp[:, 0:SC], [gn], ["fa_gS"])
                self.cp("dve", extS[:, :, 0:2], convst[:, fc, :, :], ["fa_convst"], ["fa_extS"])
                self.cp("dve", extS[:, :, 2:6], gS[:, 0:16].rearrange("p (b t) -> p b t", b=4), ["fa_gS"], ["fa_extS"])
                self.cp("dve", sconvT[:, :, :, fc], gS[:, 0:16].rearrange("p (b t) -> p b t", b=4)[:, :, 2:4], ["fa_gS"], ["fa_sconvT"])
                self.ts(gs[gb_][:, 0:2], gS[:, 16:18], self.hv[:, 0:1], None, ALU.mult, None, ["fa_gS", "hv"], [GS + "_pre"])
                self.ts(cS[:], extS[:, :, 2:6], w2, bb, ALU.mult, ALU.add, ["fa_extS", "cw", "cb"], ["fa_cS"])
                self.stt(cS[:], extS[:, :, 1:5], w1, cS[:], ALU.mult, ALU.add, ["fa_extS", "fa_cS"], ["fa_cS"])
                self.stt(cS[:], extS[:, :, 0:4], w0, cS[:], ALU.mult, ALU.add, ["fa_extS", "fa_cS"], ["fa_cS"])
                self.act(cS[:], cS[:], AF.Silu, ["fa_cS"], ["fa_cS"])
                self.tt(aS[:, 0:16].rearrange("p (b t) -> p b t", b=4), cS[:], up[:, 0:16].rearrange("p (b t) -> p b t", b=4), ALU.mult,
                        ["fa_cS", un_], ["fa_aS"])
                self.dma(self.a_s[fc, :, S0:S0 + SC], aS[:], ["fa_aS"], [self.u()])
                for (c0, n) in groups:
                    gn, gp = gr.next()
                    un_, up = ur.next()
                    tl = tiles_of(c0, n)
                    for kc in range(KC):
                        self.mm(gp[:, 0:n], wg[s][:, kc, un * 128:(un + 1) * 128], h3T[:, kc, c0:c0 + n], kc == 0, kc == KC - 1, ["fa_wg%d" % s] + tl, [gn])
                    for kc in range(KC):
                        self.mm(up[:, 0:n], wu[s][:, kc, un * 128:(un + 1) * 128], h3T[:, kc, c0:c0 + n], kc == 0, kc == KC - 1, ["fa_wu%d" % s] + tl, [un_])
                    GSc = GS + "_%d" % c0
                    GSp = GS + ("_%d" % (c0 - 512) if c0 > 0 else "_pre")
                    self.cp("act", gs[gb_][:, 2 + c0:2 + c0 + n], gp[:, 0:n], [gn], [GSc])
                    cb_ = ka % 2
                    ab_ = ka % 3
                    ka += 1
                    CN, SN, ANm = "fa_c%d" % cb_, "fa_s%d" % cb_, "fa_a%d" % ab_
                    self.ts(cbuf[cb_][:, 0:n], gs[gb_][:, 2 + c0:2 + c0 + n], w2, bb, ALU.mult, ALU.add, [GSc, "cw", "cb"], [CN])
                    self.stt(cbuf[cb_][:, 0:n], gs[gb_][:, 1 + c0:1 + c0 + n], w1, cbuf[cb_][:, 0:n], ALU.mult, ALU.add, [GSc, GSp, CN], [CN])
                    self.stt(cbuf[cb_][:, 0:n], gs[gb_][:, c0:c0 + n], w0, cbuf[cb_][:, 0:n], ALU.mult, ALU.add, [GSc, GSp, CN], [CN])
                    self.act(sbuf_[cb_][:, 0:n], cbuf[cb_][:, 0:n], AF.Silu, [CN], [SN])
                    self.tt(abuf[ab_][:, 0:n], sbuf_[cb_][:, 0:n], up[:, 0:n], ALU.mult, [SN, un_], [ANm])
                    self.dma(self.a_s[fc, :, c0:c0 + n], abuf[ab_][:, 0:n], [ANm], [self.u()])
                self.cp("dve", pconvT[:, :, fc], gs[gb_][:, OWN:OWN + 2], [GS + "_1536"], ["fa_pconvT"])
            epi = os.environ.get("FFA_EPI", "ps")
            if epi == "0":
                P.flush()
                return
            self.tr(sm_ps[1][:, 0:128], pconvT[:].rearrange("p t c -> p (t c)"), self.identf[:], ["fa_pconvT", "identf"], ["fa_sm1"])
            self.cp("act", outst[:, 0:128], sm_ps[1][:, 0:128], ["fa_sm1"], ["fa_outst"])
            for t in range(2 if "p" in epi else 0):
                self.dma(self.pconv[t:t + 1, :].rearrange("o (c p) -> (o c) p", p=128), outst[t * 64:t * 64 + NFC, 0:128], ["fa_outst"], [self.u()])
            kl = [int(ch) for ch in epi if ch.isdigit()] if any(ch.isdigit() for ch in epi) else list(range(4))
            for k in (kl if "s" in epi else []):
                pdst = sm_ps[1][:, 128 + k * 128:128 + (k + 1) * 128] if k < 3 else sm_ps[0][:, 384:512]
                self.tr(pdst, sconvT[:, k, :, :].rearrange("p t c -> p (t c)"), self.identf[:], ["fa_sconvT", "identf"], ["fa_sm1" if k < 3 else "fa_sm0"])
                self.cp("act", outst[:, 128 + k * 128:128 + (k + 1) * 128], pdst, ["fa_sm1" if k < 3 else "fa_sm0"], ["fa_outst2_%d" % k])
                for t in range(2):
                    self.dma(self.s_conv[k, t:t + 1, :].rearrange("o (c p) -> (o c) p", p=128),
                             outst[t * 64:t * 64 + NFC, 128 + k * 128:128 + (k + 1) * 128], ["fa_outst2_%d" % k], [self.u()])
            P.flush()

    def stage_ffn_b(self):
        P = self.P
        with contextlib.ExitStack() as st:
            sb = self.sb
            TG = 512
            wd = [sb(st, "fb_wd%d" % i, [128, NFC, 512], BF16) for i in range(2)]
            ag = [sb(st, "fb_ag%d" % i, [128, NFC, TG], BF16) for i in range(2)]
            xs = [sb(st, "fb_x%d" % i, [128, 512], F32) for i in range(4)]
            ys = [sb(st, "fb_y%d" % i, [128, 512], F32) for i in range(4)]
            ps = self.psb(st, "fb_ps", 6)
            psr = Ring([("fb_ps%d" % i, ps[i]) for i in range(6)])

            def loadwd(nb_):
                s = nb_ % 2
                src = self.w_down[:, nb_ * 512:(nb_ + 1) * 512].rearrange("(k p) n -> p k n", p=128)
                for qq in range(4):
                    self.dma(wd[s][:, 11 * qq:11 * qq + 11, :], src[:, 11 * qq:11 * qq + 11, :], [], ["fb_wd%d_%d" % (s, qq)], q="pq")

            tgroups = [(i * TG, TG) for i in range(OWN // TG)] + [(S0, 128)]
            seq = [(nb_, gi) for nb_ in range(4) for gi in range(len(tgroups))]

            def loada(idx):
                nb_, gi = seq[idx]
                c0, n = tgroups[gi]
                s = idx % 2
                nn = n if c0 < S0 else SC
                for qq in range(2):
                    self.dma(ag[s][:, 22 * qq:22 * qq + 22, 0:nn], self.a_s[22 * qq:22 * qq + 22, :, c0:c0 + nn].rearrange("c p t -> p c t"), [],
                             ["fb_ag%d_%d" % (s, qq)], q="aq")

            loadwd(0)
            loada(0)
            k = 0
            for idx, (nb_, gi) in enumerate(seq):
                if gi == 0 and nb_ + 1 < 4:
                    loadwd(nb_ + 1)
                if idx + 1 < len(seq):
                    loada(idx + 1)
                s = idx % 2
                ws = nb_ % 2
                c0, n = tgroups[gi]
                for t0 in range(0, n, 128):
                    b = k % 4
                    k += 1
                    row0 = c0 + t0
                    m = 128 if c0 < S0 else SC
                    self.dma(xs[b][:], self.x2_s[row0:row0 + 128, nb_ * 512:(nb_ + 1) * 512], [], ["fb_x%d" % b], q="aq")
                    pn, pp_ = psr.next()
                    for fc in range(NFC):
                        self.mm(pp_[0:m, :], ag[s][:, fc, t0:t0 + m], wd[ws][:, fc, :], fc == 0, fc == NFC - 1, ["fb_ag%d_%d" % (s, fc // 22), "fb_wd%d_%d" % (ws, fc // 11)], [pn])
                    self.tt(ys[b][0:m, :], pp_[0:m, :], xs[b][0:m, :], ALU.add, [pn, "fb_x%d" % b], ["fb_y%d" % b])
                    if c0 < S0:
                        self.dma(self.y[row0:row0 + 128, nb_ * 512:(nb_ + 1) * 512], ys[b][:], ["fb_y%d" % b], [self.u()])
                    else:
                        self.dma(self.yS[:, nb_ * 512:(nb_ + 1) * 512], ys[b][0:16, :], ["fb_y%d" % b], [self.u()])
            P.flush()


def _rope_tables(pos):
    half = 64
    inv = (1.0 / (10000.0 ** (np.arange(half, dtype=np.float32) * np.float32(2.0 / 128)))).astype(np.float32)
    ang = pos.astype(np.float32)[:, None] * inv[None, :]
    cos = np.cos(ang).astype(np.float32).T
    sin = np.sin(ang).astype(np.float32).T
    cosT = np.concatenate([cos, cos], axis=0)
    sinT = np.concatenate([-sin, sin], axis=0)
    return np.ascontiguousarray(cosT), np.ascontiguousarray(sinT)


def make_in_maps(inp):
    f = lambda a: np.ascontiguousarray(np.asarray(a, dtype=np.float32))
    xp = f(inp["x_prompt"])[0]
    xs_all = f(inp["x_sample"])
    shared = {
        "w_in": f(inp["w_in"])[0], "w_pool": f(inp["w_pool"])[0], "w_out": f(inp["w_out"])[0],
        "w_mem_q": f(inp["w_mem_q"])[0], "w_mem_k": f(inp["w_mem_k"])[0], "w_mem_v": f(inp["w_mem_v"])[0],
        "w_mem_o": f(inp["w_mem_o"])[0], "w_gate": f(inp["w_gate"])[0], "w_up": f(inp["w_up"])[0],
        "w_down": f(inp["w_down"])[0],
        "norm_mix": f(inp["norm_mix"]), "norm_mem": f(inp["norm_mem"]), "norm_mem_src": f(inp["norm_mem_src"]),
        "norm_ffn": f(inp["norm_ffn"]), "q_norm": f(inp["q_norm"]), "k_norm": f(inp["k_norm"]),
        "mem_q_norm": f(inp["mem_q_norm"]), "mem_k_norm": f(inp["mem_k_norm"]), "pool_scale": f(inp["pool_scale"]),
        "conv_w": f(inp["conv_w"])[0], "conv_b": f(inp["conv_b"]), "mem_prompt": f(inp["mem_prompt"])[0],
    }
    maps = []
    for c in range(NCORES):
        start = c * OWN
        xall = np.zeros((EXT, D), np.float32)
        lo = start - HALO
        s0 = max(lo, 0)
        xall[s0 - lo:, :] = xp[s0:start + OWN]
        xS = np.zeros((128, D), np.float32)
        xS[0:16] = xs_all[4 * c:4 * c + 4].reshape(16, D)
        xS[16:18] = xall[HALO - 2:HALO]
        pos = np.arange(lo, start + OWN)
        cosT, sinT = _rope_tables(pos)
        posS = np.zeros(SC, np.int64)
        posS[0:16] = np.tile(PAST + np.arange(4), 4)
        posS[16:18] = [start - 2, start - 1]
        cosS, sinS = _rope_tables(posS)
        valid = 1.0 if c > 0 else 0.0
        hv = np.zeros((128, 2), np.float32)
        hv[:, 0] = valid
        hv[:, 1] = NEG * (1.0 - valid)
        corr = np.ones((128, 4, 16), np.float32)
        for g_ in range(4):
            w = 2 << g_
            p = start + np.arange(16)
            corr[:, g_, :] = (w / np.minimum(p + 1, w)).astype(np.float32)[None, :]
        gmask = np.ones((128, 2, 3, 8), np.float32)
        for t in range(2):
            for bi, d in enumerate(DILS):
                kp = (start - 2 + t) - d * (128 - np.arange(128))
                gmask[:, t, bi, :] = (kp >= 0).astype(np.float32)[:, None]
        m = dict(shared)
        m.update({
            "xall": xall, "xS": xS, "cosT": cosT, "sinT": sinT, "cosS": cosS, "sinS": sinS, "hv": hv,
            "corr": corr.reshape(128, 64), "gmask": gmask.reshape(128, 48),
            "state_pool": f(inp["state_pool"])[0, 4 * c:4 * c + 4],
            "cwk": f(inp["cache_win_k"])[0, 4 * c:4 * c + 4].reshape(4, 2048, 1024),
            "cwv": f(inp["cache_win_v"])[0, 4 * c:4 * c + 4].reshape(4, 2048, 1024),
            "cmk": f(inp["cache_mem_k"])[0, 4 * c:4 * c + 4].reshape(4, 256, 512),
            "cmv": f(inp["cache_mem_v"])[0, 4 * c:4 * c + 4].reshape(4, 256, 512),
            "state_conv": f(inp["state_conv"])[0, 4 * c:4 * c + 4],
        })
        maps.append(m)
    return maps


_NC_CACHE = {}


def get_nc(debug=False, stop_after=None):
    key = (debug, stop_after)
    if key not in _NC_CACHE:
        b = Builder(debug=debug, stop_after=stop_after)
        b.build()
        _NC_CACHE[key] = b
    return _NC_CACHE[key]


def kernel(**inputs):
    b = get_nc()
    maps = make_in_maps(inputs)
    keys = set(b.din.keys())
    maps = [{k: v for k, v in m.items() if k in keys} for m in maps]
    res = run_bass_kernel_spmd(b.nc, maps, core_ids=list(range(NCORES)))
    r = res.results
    cat = lambda k: np.concatenate([np.asarray(r[c][k]) for c in range(NCORES)], axis=0)
    y_prompt = cat("y").reshape(1, NCORES * OWN, D)
    y_sample = cat("yS").reshape(32, 4, D)
    last = r[NCORES - 1]
    p_state_pool = np.asarray(last["p_pool"]).reshape(1, 1, 15, PW)
    p_win_k = np.asarray(last["pk"]).reshape(1, 1, 2048, NH, 128)
    p_win_v = np.asarray(last["pv"]).reshape(1, 1, 2048, NH, 128)
    p_mem_k = np.asarray(r[0]["pmk"]).reshape(1, 1, 256, MH, 128)
    p_mem_v = np.asarray(r[0]["pmv"]).reshape(1, 1, 256, MH, 128)
    p_state_conv = np.asarray(last["pconv"]).reshape(1, 1, 2, DFF)
    s_state_pool = cat("s_pool").reshape(1, 32, 15, PW)
    s_k = cat("s_k").reshape(1, 32, 4, NH, 128)
    s_v = cat("s_v").reshape(1, 32, 4, NH, 128)
    s_conv = cat("s_conv").reshape(1, 32, 2, DFF)
    outs = (y_prompt, y_sample, p_state_pool, p_win_k, p_win_v, p_mem_k, p_mem_v, p_state_conv,
            s_state_pool, s_k, s_v, s_conv)
    return tuple(np.ascontiguousarray(o, dtype=np.float32) for o in outs)
```

```python
import contextlib
import numpy as np
import concourse.bass as bass
import concourse.mybir as mybir
from concourse.bass_utils import run_bass_kernel_spmd

F32 = mybir.dt.float32
BF16 = mybir.dt.bfloat16
AF = mybir.ActivationFunctionType
ALU = mybir.AluOpType
AX = mybir.AxisListType

NCORES = 8
D = 2048
KC = 16
NH = 8
PW = 1024
DFF = 5632
NFC = 44
MH = 4
OWN = 2048
HALO = 2176
EXT = HALO + OWN
SC = 32
S0 = OWN
NT = OWN + 128
PAST = 16384
EPS = 1e-6
SCALE = 128.0 ** -0.5
NEG = -30000.0
DILS = (1, 4, 16)


class Prog:
    def __init__(self, nc, stack, n_dma_sems=32):
        self.nc = nc
        self.sems = {s: stack.enter_context(nc.semaphore("s_" + s)) for s in ("pe", "act", "dve", "pool")}
        self.dsems = [stack.enter_context(nc.semaphore("d%d" % i)) for i in range(n_dma_sems)]
        self.cnt = {s: 0 for s in self.sems}
        self.dcnt = [0] * n_dma_sems
        self.dlast = [None] * n_dma_sems
        self.ndma = 0
        self.ndma_sw = 0
        self.n_sw_sems = 8
        self.known = {s: {} for s in ("pe", "act", "dve", "pool", "sp")}
        self.trace = {s: [] for s in ("pe", "act", "dve", "pool", "sp")}
        self.nstage = 0
        self.total_ops = 0
        self._reset()

    def _reset(self):
        self.ops = []
        self.last_writer = {}
        self.readers = {}

    def add(self, eng, fn, reads=(), writes=(), dma=False):
        idx = len(self.ops)
        deps = set()
        lw = self.last_writer
        for r in reads:
            w = lw.get(r)
            if w is not None:
                deps.add(w)
        for r in writes:
            w = lw.get(r)
            if w is not None:
                deps.add(w)
            rd = self.readers.get(r)
            if rd:
                deps.update(rd)
        deps.discard(idx)
        self.ops.append([eng, fn, deps, dma, False, None, None])
        for r in writes:
            lw[r] = idx
            self.readers[r] = []
        for r in reads:
            self.readers.setdefault(r, []).append(idx)
        return idx

    def flush(self):
        nc = self.nc
        ops = self.ops
        stream_of = {"pe": "pe", "act": "act", "dve": "dve", "pool": "pool", "sp": "sp", "pq": "pool", "aq": "act"}
        streams = {s: [] for s in ("pe", "act", "dve", "pool", "sp")}
        for i, o in enumerate(ops):
            streams[stream_of[o[0]]].append(i)
        for i, o in enumerate(ops):
            so = stream_of[o[0]]
            keep = []
            best = {}
            for d in o[2]:
                od = ops[d]
                if od[3]:
                    keep.append(d)
                    continue
                sd = stream_of[od[0]]
                if sd == so and not o[3] and so == "pe":
                    continue
                if d > best.get(sd, -1):
                    best[sd] = d
            for sd, d in best.items():
                keep.append(d)
                ops[d][4] = True
            o[2] = sorted(keep)
        for s in ("pe", "act", "dve", "pool"):
            for i in reversed(streams[s]):
                if not ops[i][3]:
                    ops[i][4] = True
                    break
        nd = len(self.dsems)
        nsw = self.n_sw_sems
        nhw = nd - nsw
        for i, o in enumerate(ops):
            if o[3]:
                if o[0] == "pq":
                    k = nhw + (self.ndma_sw % nsw)
                    self.ndma_sw += 1
                else:
                    k = self.ndma % nhw
                    self.ndma += 1
                self.dcnt[k] += 16
                o[5] = ("d", k, self.dcnt[k])
                o[6] = self.dlast[k]
                self.dlast[k] = o[5]
            elif o[4]:
                s = stream_of[o[0]]
                self.cnt[s] += 1
                o[5] = ("c", s, self.cnt[s])
        finals = [("c", s, self.cnt[s]) for s in self.sems if self.cnt[s]]
        finals += [("d", k, self.dcnt[k]) for k in range(nd) if self.dcnt[k]]

        def run_stream(sname, eng):
            known = self.known[sname]

            def wait_tok(tok):
                key = (tok[0], tok[1])
                if known.get(key, 0) >= tok[2]:
                    return
                known[key] = tok[2]
                self.trace[sname].append(("w", key, tok[2]))
                sem = self.dsems[tok[1]] if tok[0] == "d" else self.sems[tok[1]]
                eng.wait_ge(sem, tok[2])

            for i in streams[sname]:
                o = ops[i]
                for d in o[2]:
                    tok = ops[d][5]
                    if tok is not None:
                        wait_tok(tok)
                if o[3] and o[6] is not None:
                    wait_tok(o[6])
                ins = o[1](eng)
                tok = o[5]
                self.trace[sname].append(("o", None if tok is None else (tok[0], tok[1]), 16 if (tok and tok[0] == "d") else 1))
                if tok is not None:
                    if tok[0] == "d":
                        ins.then_inc(self.dsems[tok[1]], 16)
                    else:
                        ins.then_inc(self.sems[tok[1]], 1)
            for tok in finals:
                if not (tok[0] == "c" and tok[1] == sname):
                    wait_tok(tok)

        with nc.Block() as block:
            @block.tensor
            def _(e):
                run_stream("pe", e)

            @block.scalar
            def _(e):
                run_stream("act", e)

            @block.vector
            def _(e):
                run_stream("dve", e)

            @block.gpsimd
            def _(e):
                run_stream("pool", e)

            @block.sync
            def _(e):
                run_stream("sp", e)
        self.total_ops += len(ops)
        self.nstage += 1
        self._reset()


class Ring:
    def __init__(self, items):
        self.items = list(items)
        self.i = 0

    def next(self):
        x = self.items[self.i % len(self.items)]
        self.i += 1
        return x


class Builder:
    def __init__(self, debug=False, stop_after=None):
        self.debug = debug
        self.stop_after = stop_after
        self.nc = bass.Bass("TRN2", target_bir_lowering=False)
        self.din = {}
        self.dout = {}
        self._uid = 0

    def u(self):
        self._uid += 1
        return "_u%d" % self._uid

    def inp(self, name, shape):
        t = self.nc.dram_tensor(name, list(shape), F32, kind="ExternalInput").ap()
        self.din[name] = t
        return t

    def outp(self, name, shape):
        t = self.nc.dram_tensor(name, list(shape), F32, kind="ExternalOutput").ap()
        self.dout[name] = t
        return t

    def scratch(self, name, shape, dt):
        kind = "ExternalOutput" if self.debug else "Internal"
        t = self.nc.dram_tensor(name, list(shape), dt, kind=kind).ap()
        if self.debug:
            self.dout[name] = t
        return t

    def sb(self, st, name, shape, dt):
        return st.enter_context(self.nc.sbuf_tensor("sb_" + name, list(shape), dt))

    def psb(self, st, name, n=1, dt=F32, cols=512):
        cols = 1024 if dt == BF16 else 512
        return [st.enter_context(self.nc.psum_tensor("ps_%s%d" % (name, i), [128, cols], dt)) for i in range(n)]

    def mm(self, out, lhsT, rhs, start, stop, reads, writes):
        self.P.add("pe", lambda e: e.matmul(out, lhsT, rhs, start=start, stop=stop), reads, writes)

    def tr(self, out, in_, ident, reads, writes):
        self.P.add("pe", lambda e: e.transpose(out, in_, ident), reads, writes)

    def act(self, out, in_, func, reads, writes, scale=None, bias=None, accum=None):
        kw = {}
        if scale is not None:
            kw["scale"] = scale
        if bias is not None:
            kw["bias"] = bias
        if accum is not None:
            kw["accum_out"] = accum
        self.P.add("act", lambda e: e.activation(out=out, in_=in_, func=func, **kw), reads, writes)

    def cp(self, eng, out, in_, reads, writes):
        if eng == "act":
            self.P.add("act", lambda e: e.copy(out, in_), reads, writes)
        else:
            self.P.add(eng, lambda e: e.tensor_copy(out, in_), reads, writes)

    def tt(self, out, in0, in1, op, reads, writes, eng="dve"):
        self.P.add(eng, lambda e: e.tensor_tensor(out=out, in0=in0, in1=in1, op=op), reads, writes)

    def ts(self, out, in0, s1, s2, op0, op1, reads, writes, eng="dve"):
        if op1 is None:
            self.P.add(eng, lambda e: e.tensor_scalar(out=out, in0=in0, scalar1=s1, scalar2=None, op0=op0), reads, writes)
        else:
            self.P.add(eng, lambda e: e.tensor_scalar(out=out, in0=in0, scalar1=s1, scalar2=s2, op0=op0, op1=op1), reads, writes)

    def stt(self, out, in0, scalar, in1, op0, op1, reads, writes):
        self.P.add("dve", lambda e: e.scalar_tensor_tensor(out=out, in0=in0, scalar=scalar, in1=in1, op0=op0, op1=op1), reads, writes)

    def red(self, out, in_, op, reads, writes, absv=None):
        self.P.add("dve", lambda e: e.tensor_reduce(out=out, in_=in_, axis=AX.X, op=op, apply_absolute_value=absv), reads, writes)

    def dma(self, out, in_, reads, writes, q="sp"):
        self.P.add(q, lambda e: e.dma_start(out=out, in_=in_), reads, writes, dma=True)

    def memset(self, ap, val, writes, eng="pool"):
        self.P.add(eng, lambda e: e.memset(ap, val), (), writes)

    def build(self):
        nc = self.nc
        with contextlib.ExitStack() as g:
            self.P = Prog(nc, g)
            self.declare_dram()
            self.alloc_global(g)
            self.stage_consts()
            stages = [self.stage_proj, self.stage_pool, self.stage_attn, self.stage_attn_s,
                      self.stage_wout, self.stage_mem, self.stage_ffn_a]
            done = False
            with contextlib.ExitStack() as g1:
                self.arena1 = self.sb(g1, "arena1", [128, KC, NT], BF16)
                import os
                skip = os.environ.get("SKIP_TO")
                for fn in stages:
                    if skip and fn.__name__ != skip:
                        continue
                    skip = None
                    fn()
                    if self.debug:
                        dd = self.nc.dram_tensor("dbg_" + fn.__name__, [128, KC, NT], BF16, kind="ExternalOutput").ap()
                        self.dout["dbg_" + fn.__name__] = dd
                        self.dma(dd[:, :, :], self.arena1[:], [], [self.u()])
                        self.P.flush()
                    if self.stop_after == fn.__name__:
                        done = True
                        break
            if not done:
                self.stage_ffn_b()
        return nc

    def declare_dram(self):
        i = self.inp
        self.xall = i("xall", [EXT, D])
        self.xS = i("xS", [128, D])
        self.cosT = i("cosT", [128, EXT])
        self.sinT = i("sinT", [128, EXT])
        self.cosS = i("cosS", [128, SC])
        self.sinS = i("sinS", [128, SC])
        self.hv_d = i("hv", [128, 2])
        self.corr_d = i("corr", [128, 4 * 16])
        self.gmask_d = i("gmask", [128, 2 * 24])
        self.w_in = i("w_in", [D, 4096])
        self.w_pool = i("w_pool", [4, 256, 256])
        self.w_out = i("w_out", [D, D])
        self.w_mem_q = i("w_mem_q", [D, 512])
        self.w_mem_k = i("w_mem_k", [D, 512])
        self.w_mem_v = i("w_mem_v", [D, 512])
        self.w_mem_o = i("w_mem_o", [512, D])
        self.w_gate = i("w_gate", [D, DFF])
        self.w_up = i("w_up", [D, DFF])
        self.w_down = i("w_down", [DFF, D])
        self.norm_mix = i("norm_mix", [1, D])
        self.norm_mem = i("norm_mem", [1, D])
        self.norm_mem_src = i("norm_mem_src", [1, D])
        self.norm_ffn = i("norm_ffn", [1, D])
        self.q_norm = i("q_norm", [1, 128])
        self.k_norm = i("k_norm", [1, 128])
        self.mem_q_norm = i("mem_q_norm", [1, 128])
        self.mem_k_norm = i("mem_k_norm", [1, 128])
        self.pool_scale = i("pool_scale", [1, PW])
        self.conv_w = i("conv_w", [3, DFF])
        self.conv_b = i("conv_b", [1, DFF])
        self.mem_prompt = i("mem_prompt", [256, D])
        self.state_pool = i("state_pool", [4, 15, PW])
        self.cwk = i("cwk", [4, 2048, 1024])
        self.cwv = i("cwv", [4, 2048, 1024])
        self.cmk = i("cmk", [4, 256, 512])
        self.cmv = i("cmv", [4, 256, 512])
        self.state_conv = i("state_conv", [4, 2, DFF])
        o = self.outp
        self.y = o("y", [OWN, D])
        self.yS = o("yS", [16, D])
        self.p_pool = o("p_pool", [15, PW])
        self.pk = o("pk", [OWN, 1024])
        self.pv = o("pv", [OWN, 1024])
        self.pmk = o("pmk", [256, 512])
        self.pmv = o("pmv", [256, 512])
        self.pconv = o("pconv", [2, DFF])
        self.s_pool = o("s_pool", [4, 15, PW])
        self.s_k = o("s_k", [16, 1024])
        self.s_v = o("s_v", [16, 1024])
        self.s_conv = o("s_conv", [4, 2, DFF])
        s = self.scratch
        self.uT_s = s("uT_s", [8, 128, 128 + OWN], F32)
        self.kT_s = s("kT_s", [NH, 128, EXT], BF16)
        self.vT_s = s("vT_s", [NH, 128, EXT], BF16)
        self.qT_s = s("qT_s", [NH, 128, OWN], BF16)
        self.kh_s = s("kh_s", [HALO, 1024], F32)
        self.vh_s = s("vh_s", [HALO, 1024], F32)
        self.qS_s = s("qS_s", [SC, 1024], F32)
        self.kS_s = s("kS_s", [SC, 1024], F32)
        self.vS_s = s("vS_s", [SC, 1024], F32)
        self.x1_s = s("x1_s", [OWN + 128, D], F32)
        self.x2_s = s("x2_s", [OWN + 128, D], F32)
        self.a_s = s("a_s", [NFC, 128, NT], BF16)

    def alloc_global(self, g):
        sb = self.sb
        self.identb = sb(g, "identb", [128, 128], BF16)
        self.identf = sb(g, "identf", [128, 128], F32)
        self.onesb = sb(g, "onesb", [128, 128], BF16)
        self.rotb = sb(g, "rotb", [128, 128], BF16)
        self.negA = sb(g, "negA", [128, 128], BF16)
        self.negB = sb(g, "negB", [128, 128], BF16)
        self.negAh = sb(g, "negAh", [128, 128], BF16)
        self.hv = sb(g, "hv", [128, 2], F32)
        self.mask2 = sb(g, "mask2", [128, 256], BF16)
        self.mask2h = sb(g, "mask2h", [128, 256], BF16)
        self.colv = sb(g, "colv", [128, 8], F32)
        self.rowv = sb(g, "rowv", [1, 4 * 128 + 8], F32)
        self.pscale = sb(g, "pscale", [128, 8], F32)
        self.cw = sb(g, "cw", [128, NFC, 3], F32)
        self.cb = sb(g, "cb", [128, NFC], F32)
        self.uT_S = sb(g, "uT_S", [128, 8, SC], F32)
        self.qT_S = sb(g, "qT_S", [128, 8, SC], F32)
        self.kT_S = sb(g, "kT_S", [128, 8, SC], F32)
        self.vT_S = sb(g, "vT_S", [128, 8, SC], F32)

    def stage_consts(self):
        nc = self.nc
        P = self.P
        with contextlib.ExitStack() as st:
            tmpf = self.sb(st, "c_tmpf", [128, 128], F32)
            tmp2 = self.sb(st, "c_tmp2", [128, 128], F32)
            cwtA = self.sb(st, "c_cwtA", [128, 128], F32)
            cwtB = self.sb(st, "c_cwtB", [64, 128], F32)
            ps = self.psb(st, "c_ps", 1)[0]
            self.memset(self.identf[:], 1.0, ["identf"])
            P.add("pool", lambda e: e.affine_select(out=self.identf[:], in_=self.identf[:], pattern=[[-1, 128]],
                                                    compare_op=ALU.is_equal, fill=0.0, base=0, channel_multiplier=1),
                  ["identf"], ["identf"])
            self.cp("dve", self.identb[:], self.identf[:], ["identf"], ["identb"])
            self.memset(self.onesb[:], 1.0, ["onesb"])
            self.memset(tmpf[:], 1.0, ["tmpf"])
            P.add("pool", lambda e: e.affine_select(out=tmpf[:], in_=tmpf[:], pattern=[[-1, 128]],
                                                    compare_op=ALU.is_equal, fill=0.0, base=64, channel_multiplier=1),
                  ["tmpf"], ["tmpf"])
            self.memset(tmp2[:], 1.0, ["tmp2"])
            P.add("pool", lambda e: e.affine_select(out=tmp2[:], in_=tmp2[:], pattern=[[-1, 128]],
                                                    compare_op=ALU.is_equal, fill=0.0, base=-64, channel_multiplier=1),
                  ["tmp2"], ["tmp2"])
            self.tt(self.rotb[:], tmpf[:], tmp2[:], ALU.add, ["tmpf", "tmp2"], ["rotb"])
            self.memset(tmpf[:], 0.0, ["tmpf"])
            P.add("pool", lambda e: e.affine_select(out=tmpf[:], in_=tmpf[:], pattern=[[-1, 128]],
                                                    compare_op=ALU.is_ge, fill=NEG, base=0, channel_multiplier=1),
                  ["tmpf"], ["tmpf"])
            self.cp("dve", self.negA[:], tmpf[:], ["tmpf"], ["negA"])
            self.dma(self.hv[:], self.hv_d[:, :], [], ["hv"])
            self.ts(self.negAh[:], tmpf[:], self.hv[:, 1:2], None, ALU.add, None, ["tmpf", "hv"], ["negAh"])
            self.memset(tmp2[:], 0.0, ["tmp2"])
            P.add("pool", lambda e: e.affine_select(out=tmp2[:], in_=tmp2[:], pattern=[[1, 128]],
                                                    compare_op=ALU.is_ge, fill=NEG, base=0, channel_multiplier=-1),
                  ["tmp2"], ["tmp2"])
            self.cp("dve", self.negB[:], tmp2[:], ["tmp2"], ["negB"])
            self.ts(self.mask2[:, 0:128], tmpf[:], 0.0, None, ALU.is_equal, None, ["tmpf"], ["mask2"])
            self.ts(self.mask2[:, 128:256], tmp2[:], 0.0, None, ALU.is_equal, None, ["tmp2"], ["mask2"])
            self.ts(self.mask2h[:, 0:128], self.mask2[:, 0:128], self.hv[:, 0:1], None, ALU.mult, None, ["mask2", "hv"], ["mask2h"])
            self.cp("dve", self.mask2h[:, 128:256], self.mask2[:, 128:256], ["mask2"], ["mask2h"])
            cwr = self.conv_w.rearrange("j (c p) -> (j c) p", p=128)
            self.dma(cwtA[:, :], cwr[0:128, :], [], ["cwt"])
            self.dma(cwtB[0:4, :], cwr[128:132, :], [], ["cwt"])
            self.dma(cwtB[4:48, :], self.conv_b.rearrange("o (c p) -> (o c) p", p=128), [], ["cwt"])
            self.dma(cwtB[48:56, :], self.pool_scale.rearrange("o (c p) -> (o c) p", p=128), [], ["cwt"])
            b0 = NFC * 4 + 8
            for j, v in enumerate((self.q_norm, self.k_norm, self.mem_q_norm, self.mem_k_norm)):
                self.dma(cwtB[56 + j:57 + j, :], v[0:1, :], [], ["cwt"])
            self.tr(ps[:, 0:128], cwtA[:, :], self.identf[:], ["cwt", "identf"], ["cps"])
            self.tr(ps[:, 128:188], cwtB[0:60, :], self.identf[0:60, 0:60], ["cwt", "identf"], ["cps"])
            self.cp("dve", self.cw[:].rearrange("p c j -> p j c"), ps[:, 0:NFC * 3].rearrange("p (j c) -> p j c", j=3),
                    ["cps"], ["cw"])
            self.cp("dve", self.cb[:], ps[:, NFC * 3:NFC * 4], ["cps"], ["cb"])
            self.cp("dve", self.pscale[:], ps[:, NFC * 4:NFC * 4 + 8], ["cps"], ["pscale"])
            self.cp("dve", self.colv[:, 0:3], ps[:, b0:b0 + 3], ["cps"], ["colv"])
            rv = self.rowv
            for j, v in enumerate((self.q_norm, self.k_norm, self.mem_q_norm, self.mem_k_norm)):
                self.dma(rv[0:1, j * 128:(j + 1) * 128], v[0:1, :], [], ["rowv"])
            m0 = 512
            self.red(rv[0:1, m0:m0 + 4], rv[0:1, 0:512].rearrange("o (a b) -> o a b", a=4), ALU.max, ["rowv"], ["rowm"], absv=True)
            self.tt(rv[0:1, m0 + 4:m0 + 5], rv[0:1, m0:m0 + 1], rv[0:1, m0 + 1:m0 + 2], ALU.mult, ["rowm"], ["rowb"])
            self.tt(rv[0:1, m0 + 5:m0 + 6], rv[0:1, m0 + 2:m0 + 3], rv[0:1, m0 + 3:m0 + 4], ALU.mult, ["rowm"], ["rowb"])
            self.ts(rv[0:1, m0 + 6:m0 + 8], rv[0:1, m0 + 4:m0 + 6], -(128.0 ** 0.5), None, ALU.mult, None, ["rowb"], ["rowc"])
            self.memset(tmp2[0:1, :], 1.0, ["tmp2"], eng="dve")
            self.mm(ps[:, 256:258], tmp2[0:1, :], rv[0:1, m0 + 6:m0 + 8], True, True, ["rowc", "tmp2"], ["cps2"])
            self.cp("dve", self.colv[:, 3:5], ps[:, 256:258], ["cps2"], ["colv2"])
            P.flush()

    def norm_transpose(self, src_ap, src_reads, gain_b, gname, hT, col0, hname, bufs, i):
        xt, xn, stat, ptr = bufs
        b = i % 2
        X = "nt_x%d" % b
        XN = "nt_xn%d" % b
        STt = "nt_st%d" % b
        if src_ap is not None:
            self.dma(xt[b][:], src_ap, src_reads, [X])
        self.act(xn[b][:], xt[b][:], AF.Square, [X], [XN, STt + "a"], accum=stat[b][:, 0:1])
        self.act(stat[b][:, 1:2], stat[b][:, 0:1], AF.Ln, [STt + "a"], [STt + "b"], scale=1.0 / D, bias=EPS)
        self.act(stat[b][:, 2:3], stat[b][:, 1:2], AF.Exp, [STt + "b"], [STt + "c"], scale=-0.5)
        self.stt(xn[b][:], xt[b][:], stat[b][:, 2:3], gain_b[:], ALU.mult, ALU.mult, [X, STt + "c", gname], [XN])
        evac = getattr(self, "_nt_evac", "adad")

        def trans_part():
            for j in range(4):
                pt = ptr.next()
                for q in range(4):
                    kc = 4 * j + q
                    self.tr(pt[1][:, q * 128:(q + 1) * 128], xn[b][:, kc * 128:(kc + 1) * 128], self.identb[:],
                            [XN, "identb"], [pt[0]])
                eng = "act" if evac[j] == "a" else "dve"
                self.cp(eng, hT[:, 4 * j:4 * j + 4, col0:col0 + 128], pt[1][:, 0:512].rearrange("p (a b) -> p a b", a=4),
                        [pt[0]], [hname])

        self.nt_flush()
        self._nt_pend = trans_part

    def nt_flush(self):
        p = getattr(self, "_nt_pend", None)
        self._nt_pend = None
        if p is not None:
            p()

    def nt_bufs(self, st, tag):
        xt = [self.sb(st, "%s_xt%d" % (tag, i), [128, D], F32) for i in range(2)]
        xn = [self.sb(st, "%s_xn%d" % (tag, i), [128, D], BF16) for i in range(2)]
        stat = [self.sb(st, "%s_st%d" % (tag, i), [128, 4], F32) for i in range(2)]
        pts = self.psb(st, tag + "_pt", 2, BF16, 512)
        ptr = Ring([("nt_pt%d" % i, pts[i]) for i in range(2)])
        return xt, xn, stat, ptr

    def load_wblock(self, slot_ap, slot_name, w_ap, c0, ncols, kchunks=KC):
        src = w_ap[:, c0:c0 + ncols].rearrange("(k p) n -> p k n", p=128)
        self.dma(slot_ap[:, 0:kchunks, 0:ncols], src, [], [slot_name], q="pq")

    def stage_proj(self):
        P = self.P
        with contextlib.ExitStack() as st:
            sb = self.sb
            hT = self.arena1
            bufs = self.nt_bufs(st, "pj")
            gain_b = sb(st, "pj_gain", [128, D], F32)
            wsl = [sb(st, "pj_w%d" % i, [128, KC, 512], BF16) for i in range(2)]
            cosT = sb(st, "pj_cos", [128, HALO], F32)
            sinT = sb(st, "pj_sin", [128, HALO], F32)
            nb = 3
            pipe = [None, None]
            sqb = [sb(st, "pj_sq%d" % i, [128, 512], BF16) for i in range(nb)]
            t1f = [sb(st, "pj_t1f%d" % i, [128, 512], F32) for i in range(nb)]
            t1b = [sb(st, "pj_t1b%d" % i, [128, 512], BF16) for i in range(nb)]
            lnv = [sb(st, "pj_ln%d" % i, [128, 512], F32) for i in range(nb)]
            ta = [sb(st, "pj_ta%d" % i, [128, 512], F32) for i in range(nb)]
            tb = [sb(st, "pj_tb%d" % i, [128, 512], F32) for i in range(nb)]
            o32 = [sb(st, "pj_o32%d" % i, [128, 512], F32) for i in range(nb)]
            o16 = [sb(st, "pj_o16%d" % i, [128, 512], BF16) for i in range(nb)]
            otk = [sb(st, "pj_otk%d" % i, [128, 512], F32) for i in range(nb)]
            raw_ps = self.psb(st, "pj_raw", 2)
            ssq_ps = self.psb(st, "pj_ssq", 1)[0]
            rot_ps = self.psb(st, "pj_rot", 1)[0]
            tok_ps = self.psb(st, "pj_tok", 2)
            rawr = Ring([("pj_raw%d" % i, raw_ps[i]) for i in range(2)])
            tokr = Ring([("pj_tok%d" % i, tok_ps[i]) for i in range(2)])
            self.dma(gain_b[:], self.norm_mix[0:1, :].to_broadcast([128, D]), [], ["gain"])
            ucount = [0]

            def do_norm(pname, i):
                e0_ = 0 if pname == "H" else HALO
                src = self.xS[:, :] if (pname == "O" and i == 16) else self.xall[e0_ + i * 128:e0_ + (i + 1) * 128, :]
                self.norm_transpose(src, [], gain_b, "gain", hT, i * 128, "hT%d" % i, bufs, i)

            def run_pass(pname, pre_normed=False, tail_hook=None):
                if pname == "H":
                    ntile, e0 = 17, 0
                    groups = [(0, 512), (512, 512), (1024, 512), (1536, 512), (2048, 128)]
                    self.dma(cosT[:, 0:HALO], self.cosT[:, 0:HALO], [], ["cos"])
                    self.dma(sinT[:, 0:HALO], self.sinT[:, 0:HALO], [], ["sin"])
                elif pname == "O":
                    ntile, e0 = 17, HALO
                    groups = [(0, 512), (512, 512), (1024, 512), (1536, 512), (OWN, SC)]
                    self.dma(cosT[:, 0:OWN], self.cosT[:, HALO:EXT], [], ["cos"])
                    self.dma(sinT[:, 0:OWN], self.sinT[:, HALO:EXT], [], ["sin"])
                    self.dma(cosT[:, OWN:OWN + SC], self.cosS[:, :], [], ["cos"])
                    self.dma(sinT[:, OWN:OWN + SC], self.sinS[:, :], [], ["sin"])
                else:
                    ntile, e0 = 1, 0
                    groups = [(0, SC)]
                    self.dma(cosT[:, 0:SC], self.cosS[:, :], [], ["cos"])
                    self.dma(sinT[:, 0:SC], self.sinS[:, :], [], ["sin"])
                if not pre_normed:
                    for i in range(ntile):
                        do_norm(pname, i)
                    self.nt_flush()
                if pname == "H":
                    blocks = [0, 1, 4, 5, 6, 7]
                else:
                    blocks = list(range(8))
                self.load_wblock(wsl[0], "pj_w0", self.w_in, blocks[0] * 512, 512)
                for bi, blk in enumerate(blocks):
                    if bi + 1 < len(blocks):
                        s2 = (bi + 1) % 2
                        self.load_wblock(wsl[s2], "pj_w%d" % s2, self.w_in, blocks[bi + 1] * 512, 512)
                    sl = bi % 2
                    kind = "uqkv"[blk // 2]
                    tail = (tail_hook is not None and bi == len(blocks) - 1)
                    if tail:
                        order = [(un, g) for g in groups for un in range(4)]
                    else:
                        order = [(un, g) for un in range(4) for g in groups]
                    for (un, (c0, n)) in order:
                        unit = (blk % 2) * 4 + un
                        if True:
                            if pname == "H" and kind == "u" and c0 != 2048:
                                continue
                            rn, rp = rawr.next()
                            tiles = ["hT%d" % t for t in range(c0 // 128, (c0 + n + 127) // 128)]
                            for kc in range(KC):
                                self.mm(rp[:, 0:n], wsl[sl][:, kc, un * 128:(un + 1) * 128], hT[:, kc, c0:c0 + n],
                                        kc == 0, kc == KC - 1, ["pj_w%d" % sl] + tiles, [rn])
                            u = ucount[0] % nb
                            ucount[0] += 1
                            ph = make_phases("S" if (pname == "O" and c0 == OWN) else pname, kind, unit, c0, n, rn, rp, u, e0)
                            ph[0]()
                            if pipe[0] is not None:
                                pipe[0][1]()
                            if pipe[1] is not None:
                                pipe[1][2]()
                            pipe[1] = pipe[0]
                            pipe[0] = ph
                            if tail and un == 3:
                                tail_hook(c0, n)
                if pipe[0] is not None:
                    pipe[0][1]()
                if pipe[1] is not None:
                    pipe[1][2]()
                if pipe[0] is not None:
                    pipe[0][2]()
                pipe[0] = pipe[1] = None

            def make_phases(pname, kind, unit, c0, n, rn, rp, u, e0):
                U = "pj_u%d_" % u
                nop = lambda: None

                def tok_major():
                    nt_ = (n + 127) // 128
                    tn, tp = tokr.next()
                    for t in range(nt_):
                        w = min(128, n - t * 128)
                        self.tr(tp[0:w, t * 128:(t + 1) * 128], o32[u][:, t * 128:t * 128 + w], self.identf[:],
                                [U + "o32", "identf"], [tn])
                    if pname == "S":
                        self.cp("act", otk[u][0:SC, 0:128], tp[0:SC, 0:128], [tn], [U + "otk"])
                        dd = {"q": self.qS_s, "k": self.kS_s, "v": self.vS_s}[kind]
                        self.dma(dd[:, unit * 128:(unit + 1) * 128], otk[u][0:SC, 0:128], [U + "otk"], [self.u()])
                        if kind in "kv":
                            do = self.s_k if kind == "k" else self.s_v
                            self.dma(do[:, unit * 128:(unit + 1) * 128], otk[u][0:16, 0:128], [U + "otk"], [self.u()])
                    else:
                        self.cp("act", otk[u][:, 0:nt_ * 128], tp[:, 0:nt_ * 128], [tn], [U + "otk"])
                        if pname == "H":
                            dd = self.kh_s if kind == "k" else self.vh_s
                        else:
                            dd = self.pk if kind == "k" else self.pv
                        dst = dd[c0:c0 + nt_ * 128, unit * 128:(unit + 1) * 128].rearrange("(t p) d -> p t d", p=128)
                        self.dma(dst, otk[u][:, 0:nt_ * 128].rearrange("p (t d) -> p t d", t=nt_), [U + "otk"], [self.u()])

                def feat_major():
                    if pname == "S":
                        dst = {"q": self.qT_S, "k": self.kT_S, "v": self.vT_S}[kind]
                        self.cp("dve", dst[:, unit, :], o32[u][:, 0:n], [U + "o32"], [kind + "T_S"])
                    elif kind == "q":
                        self.dma(self.qT_s[unit, :, c0:c0 + n], o16[u][:, 0:n], [U + "o16"], [self.u()])
                    else:
                        dsts = self.kT_s if kind == "k" else self.vT_s
                        self.dma(dsts[unit, :, e0 + c0:e0 + c0 + n], o16[u][:, 0:n], [U + "o16"], [self.u()])

                if kind == "u":
                    def a0():
                        self.cp("act", o32[u][:, 0:n], rp[:, 0:n], [rn], [U + "o32"])

                    def b_():
                        if pname == "S":
                            self.cp("dve", self.uT_S[:, unit, :], o32[u][:, 0:n], [U + "o32"], ["uT_S"])
                        else:
                            dc0 = 0 if pname == "H" else 128 + c0
                            self.dma(self.uT_s[unit, :, dc0:dc0 + n], o32[u][:, 0:n], [U + "o32"], [self.u()])
                    return (a0, nop, b_)
                if kind == "v":
                    def a0():
                        self.cp("act", o32[u][:, 0:n], rp[:, 0:n], [rn], [U + "o32"])

                    def a1():
                        self.cp("dve", o16[u][:, 0:n], o32[u][:, 0:n], [U + "o32"], [U + "o16"])

                    def b_():
                        feat_major()
                        tok_major()
                    return (a0, a1, b_)
                gcol = self.colv[:, 0:1] if kind == "q" else self.colv[:, 1:2]

                def a0():
                    self.act(sqb[u][:, 0:n], rp[:, 0:n], AF.Square, [rn], [U + "sq"])
                    self.act(t1f[u][:, 0:n], rp[:, 0:n], AF.Copy, [rn, "colv"], [U + "t1f"], scale=gcol)
                    self.cp("dve", t1b[u][:, 0:n], t1f[u][:, 0:n], [U + "t1f"], [U + "t1b"])

                def a1():
                    self.mm(ssq_ps[:, 0:n], self.onesb[:], sqb[u][:, 0:n], True, True, [U + "sq", "onesb"], ["pj_ssq"])
                    self.mm(rot_ps[:, 0:n], self.rotb[:], t1b[u][:, 0:n], True, True, [U + "t1b", "rotb"], ["pj_rot"])
                    self.act(lnv[u][:, 0:n], ssq_ps[:, 0:n], AF.Ln, ["pj_ssq"], [U + "ln"], scale=1.0 / 128, bias=EPS)
                    self.act(lnv[u][:, 0:n], lnv[u][:, 0:n], AF.Exp, [U + "ln"], [U + "ln"], scale=-0.5)
                    self.tt(ta[u][:, 0:n], t1f[u][:, 0:n], cosT[:, c0:c0 + n], ALU.mult, [U + "t1f", "cos"], [U + "ta"])
                    self.tt(tb[u][:, 0:n], rot_ps[:, 0:n], sinT[:, c0:c0 + n], ALU.mult, ["pj_rot", "sin"], [U + "tb"])
                    self.tt(ta[u][:, 0:n], ta[u][:, 0:n], tb[u][:, 0:n], ALU.add, [U + "ta", U + "tb"], [U + "ta"])
                    if kind == "q" and pname != "S":
                        self.tt(o16[u][:, 0:n], ta[u][:, 0:n], lnv[u][:, 0:n], ALU.mult, [U + "ta", U + "ln"], [U + "o16"])
                    else:
                        self.tt(o32[u][:, 0:n], ta[u][:, 0:n], lnv[u][:, 0:n], ALU.mult, [U + "ta", U + "ln"], [U + "o32"])

                def b_():
                    if kind == "k" and pname != "S":
                        self.cp("act", o16[u][:, 0:n], o32[u][:, 0:n], [U + "o32"], [U + "o16"])
                    feat_major()
                    if kind == "k" or pname == "S":
                        tok_major()
                return (a0, a1, b_)

            def hook(c0, n):
                for t in range(c0 // 128, (c0 + n + 127) // 128):
                    do_norm("O", t)

            run_pass("H")
            run_pass("O")
            P.flush()

    def stage_pool(self):
        P = self.P
        with contextlib.ExitStack() as st:
            sb = self.sb
            mixedT = self.arena1
            L = 16 + OWN
            ue = [sb(st, "pl_ue%d" % i, [128, L], F32) for i in range(2)]
            pa = sb(st, "pl_a", [128, L], F32)
            pb = sb(st, "pl_b", [128, L], F32)
            diffT = sb(st, "pl_diff", [128, 8, OWN + SC], BF16)
            wp = sb(st, "pl_wp", [128, 8, 256], BF16)
            corr = sb(st, "pl_corr", [128, 4, 16], F32)
            ppl = sb(st, "pl_pp", [128, PW], F32)
            ps = self.psb(st, "pl_ps", 2)
            tps = self.psb(st, "pl_tps", 2)
            psr = Ring([("pl_ps%d" % i, ps[i]) for i in range(2)])
            for g_ in range(4):
                self.dma(wp[:, 2 * g_:2 * g_ + 2, :], self.w_pool[g_].rearrange("(i p) o -> p i o", p=128), [], ["wp"], q="pq")
            self.dma(corr[:].rearrange("p g t -> p (g t)"), self.corr_d[:, :], [], ["corr"])
            self.memset(mixedT[:, :, S0:S0 + 128], 0.0, ["mixS"], eng="dve")
            for c in range(8):
                g_ = c // 2
                w = 2 << g_
                b = c % 2
                UE = "pl_ue%d" % b
                self.dma(ue[b][:], self.uT_s[c, :, 112:128 + OWN], [], [UE])
                cur, curname = ue[b], UE
                shift = 1
                pp = [(pa, "pl_a"), (pb, "pl_b")]
                k = 0
                lo = 0
                while shift < w:
                    dst, dname = pp[k % 2]
                    lo += shift
                    self.tt(dst[:, lo:L], cur[:, lo:L], cur[:, lo - shift:L - shift], ALU.add, [curname], [dname])
                    cur, curname = dst, dname
                    shift *= 2
                    k += 1
                self.tt(cur[:, 16:32], cur[:, 16:32], corr[:, g_, :], ALU.mult, [curname, "corr"], [curname])
                self.stt(diffT[:, c, 0:OWN], cur[:, 16:L], 1.0 / w, ue[b][:, 16:L], ALU.mult, ALU.subtract,
                         [curname, UE], ["diff%d" % c])
                tpn = "pl_tps%d" % (c // 4)
                self.tr(tps[c // 4][:, (c % 4) * 128:(c % 4 + 1) * 128], ue[b][:, L - 128:L], self.identf[:], [UE, "identf"], [tpn])
                if c % 4 == 3:
                    self.cp("act", ppl[:, (c // 4) * 512:(c // 4 + 1) * 512], tps[c // 4][:, 0:512], [tpn], ["ppl"])
            self.dma(self.p_pool[:, :], ppl[113:128, :], ["ppl"], [self.u()])
            uext = sb(st, "pl_uext", [128, 8, 5, 19], F32)
            lv = [sb(st, "pl_lv%d" % i, [128, 8, 5, 19], F32) for i in range(4)]
            sp_in = sb(st, "pl_spin", [64, PW], F32)
            stok = sb(st, "pl_stok", [SC, PW], F32)
            self.memset(uext[:], 0.0, ["uext"], eng="dve")
            for li in range(4):
                self.memset(lv[li][:], 0.0, ["pl_lv%d" % li], eng="dve")
            self.dma(sp_in[0:60, :], self.state_pool.rearrange("b t n -> (b t) n"), [], ["spin"])
            tpS = tps[0]
            for c in range(8):
                self.tr(tpS[:, c * 60:(c + 1) * 60], sp_in[0:60, c * 128:(c + 1) * 128], self.identf[0:60, 0:60],
                        ["spin", "identf"], ["pl_tps0"])
            self.cp("dve", uext[:, :, 0:4, 0:15], tpS[:, 0:480].rearrange("p (c b t) -> p c b t", c=8, b=4),
                    ["pl_tps0"], ["uext"])
            self.cp("dve", uext[:, :, 0:4, 15:19], self.uT_S[:, :, 0:16].rearrange("p c (b t) -> p c b t", b=4),
                    ["uT_S"], ["uext"])
            self.dma(uext[:, :, 4, 0:17], self.uT_s[:, :, 111:128].rearrange("c p t -> p c t"), [], ["uext"])
            cur, curname = uext, "uext"
            shift, lo = 1, 0
            for li in range(4):
                dst, dname = lv[li], "pl_lv%d" % li
                lo += shift
                self.tt(dst[:, :, :, lo:19], cur[:, :, :, lo:19], cur[:, :, :, lo - shift:19 - shift], ALU.add, [curname], [dname])
                cur, curname = dst, dname
                shift *= 2
            dS = sb(st, "pl_dS", [128, 8, 5, 19], F32)
            for g_ in range(4):
                w = 2 << g_
                cs = slice(2 * g_, 2 * g_ + 2)
                self.ts(dS[:, cs, :, :], lv[g_][:, cs, :, :], 1.0 / w, None, ALU.mult, None, ["pl_lv%d" % g_], ["pl_dS%d" % g_])
                self.tt(dS[:, cs, :, :], dS[:, cs, :, :], uext[:, cs, :, :], ALU.subtract, ["pl_dS%d" % g_, "uext"], ["pl_dS%d" % g_])
                self.cp("dve", diffT[:, cs, OWN:OWN + 16].rearrange("p c (b t) -> p c b t", b=4), dS[:, cs, 0:4, 15:19],
                        ["pl_dS%d" % g_], ["diffS"])
                self.cp("dve", diffT[:, cs, OWN + 16:OWN + 18], dS[:, cs, 4, 15:17], ["pl_dS%d" % g_], ["diffS"])
            self.memset(diffT[:, :, OWN + 18:OWN + SC], 0.0, ["diffS"], eng="dve")
            for c in range(8):
                self.tr(tps[1][0:SC, (c % 4) * 128:(c % 4 + 1) * 128], self.uT_S[:, c, :], self.identf[:], ["uT_S", "identf"], ["pl_tps1"])
                if c % 4 == 3:
                    self.cp("act", stok[0:SC, (c // 4) * 512:(c // 4 + 1) * 512], tps[1][0:SC, 0:512], ["pl_tps1"], ["stok"])
            for b_ in range(4):
                self.dma(self.s_pool[b_, 11:15, :], stok[4 * b_:4 * b_ + 4, :], ["stok"], [self.u()])
            self.dma(self.s_pool[:, 0:11, :], self.state_pool[:, 4:15, :], [], [self.u()])
            groups = [(0, 512), (512, 512), (1024, 512), (1536, 512), (OWN, SC)]
            for g_ in range(4):
                for oc in range(2):
                    for (c0, n) in groups:
                        pn, pp_ = psr.next()
                        dn = ["diffS"] if c0 == OWN else ["diff%d" % (2 * g_), "diff%d" % (2 * g_ + 1)]
                        for ic in range(2):
                            self.mm(pp_[:, 0:n], wp[:, 2 * g_ + ic, oc * 128:(oc + 1) * 128], diffT[:, 2 * g_ + ic, c0:c0 + n],
                                    ic == 0, ic == 1, ["wp"] + dn, [pn])
                        ch = 2 * g_ + oc
                        self.act(mixedT[:, ch, c0:c0 + n], pp_[:, 0:n], AF.Copy, [pn, "pscale"],
                                 ["mix%d_%d" % (ch, c0)] + (["mixS"] if c0 == OWN else []), scale=self.pscale[:, ch:ch + 1])
            P.flush()

    def stage_attn(self):
        P = self.P
        with contextlib.ExitStack() as st:
            sb = self.sb
            mixedT = self.arena1
            kT = [sb(st, "at_k%d" % i, [128, EXT], BF16) for i in range(2)]
            vT = [sb(st, "at_v%d" % i, [128, EXT], BF16) for i in range(2)]
            qT = [sb(st, "at_q%d" % i, [128, OWN], BF16) for i in range(2)]
            NVT = 17 + 20 + 32
            Vt = [sb(st, "at_vt%d" % i, [128, NVT, 128], BF16) for i in range(2)]
            ACC = [sb(st, "at_acc%d" % i, [128, 2, OWN], F32) for i in range(2)]
            Eb = [sb(st, "at_e%d" % i, [128, 256], BF16) for i in range(5)]
            tmpf = sb(st, "at_tmp", [128, OWN], F32)
            s_ps = self.psb(st, "at_s", 4)
            nd_ps = self.psb(st, "at_nd", 2)
            vt_ps = self.psb(st, "at_vp", 2, BF16, 512)
            sr = Ring([("at_s%d" % i, s_ps[i]) for i in range(4)])
            ndr = Ring([("at_nd%d" % i, nd_ps[i]) for i in range(2)])
            vpr = Ring([("at_vp%d" % i, vt_ps[i]) for i in range(2)])
            er = Ring([("at_e%d" % i, Eb[i]) for i in range(5)])
            negBias = self.colv[:, 3:4]

            def load_head(h):
                b = h % 2
                self.dma(kT[b][:], self.kT_s[h, :, :], [], ["at_k%d" % b])
                self.dma(vT[b][:], self.vT_s[h, :, :], [], ["at_v%d" % b])
                self.dma(qT[b][:], self.qT_s[h, :, :], [], ["at_q%d" % b])

            vidx = {}
            n = 0
            for d in DILS:
                for r in range(d):
                    for j in range(16 // d + 1):
                        vidx[(d, r, j)] = n
                        n += 1
            assert n == NVT
            load_head(0)
            for h in range(NH):
                b = h % 2
                if h + 1 < NH:
                    load_head(h + 1)
                KN, VN, QN, VTN, AN = "at_k%d" % b, "at_v%d" % b, "at_q%d" % b, "at_vt%d" % b, "at_acc%d" % b
                keys = sorted(vidx, key=lambda kk: vidx[kk])
                for g0 in range(0, NVT, 4):
                    pn, pp_ = vpr.next()
                    grp = keys[g0:g0 + 4]
                    for q_, (d, r, j) in enumerate(grp):
                        e0 = HALO - 128 * d + d * 128 * j + r
                        self.tr(pp_[:, q_ * 128:(q_ + 1) * 128], vT[b][:, e0:e0 + 127 * d + 1:d], self.identb[:], [VN, "identb"], [pn])
                    ng = len(grp)
                    eng = "act" if (g0 // 4) % 2 == 0 else "dve"
                    self.cp(eng, Vt[b][:, g0:g0 + ng, :], pp_[:, 0:ng * 128].rearrange("p (a b) -> p a b", a=ng), [pn],
                            [VTN + "_%d" % (g0 // 4)])
                units = [(d, r, blk) for d in DILS for r in range(d) for blk in range(16 // d)]

                def s_part(d, r, blk):
                    eA = HALO - 128 * d + d * 128 * blk + r
                    eB = eA + 128 * d
                    q0 = eB - HALO
                    kA = kT[b][:, eA:eA + 127 * d + 1:d]
                    kB = kT[b][:, eB:eB + 127 * d + 1:d]
                    qB = qT[b][:, q0:q0 + 127 * d + 1:d]
                    sn, sp_ = sr.next()
                    nA = self.negAh if blk == 0 else self.negA
                    self.mm(sp_[:, 0:128], kA, qB, True, False, [KN, QN], [sn])
                    self.mm(sp_[:, 0:128], self.identb[:], nA[:], False, True, ["identb", "negA", "negAh"], [sn])
                    self.mm(sp_[:, 128:256], kB, qB, True, False, [KN, QN], [sn])
                    self.mm(sp_[:, 128:256], self.identb[:], self.negB[:], False, True, ["identb", "negB"], [sn])
                    en, eb = er.next()
                    self.act(eb[:], sp_[:, 0:256], AF.Exp, [sn, "colv2"], [en], scale=SCALE, bias=negBias)
                    return (d, r, blk, q0, en, eb)

                def pv_part(d, r, blk, q0, en, eb):
                    ia, ib = vidx[(d, r, blk)], vidx[(d, r, blk + 1)]
                    nn, np_ = ndr.next()
                    self.mm(np_[:, 0:128], Vt[b][:, ia, :], eb[:, 0:128], True, False, [VTN + "_%d" % (ia // 4), en], [nn])
                    self.mm(np_[:, 0:128], Vt[b][:, ib, :], eb[:, 128:256], False, True, [VTN + "_%d" % (ib // 4), en], [nn])
                    self.mm(np_[:, 128:256], self.onesb[:], eb[:, 0:128], True, False, ["onesb", en], [nn])
                    self.mm(np_[:, 128:256], self.onesb[:], eb[:, 128:256], False, True, ["onesb", en], [nn])
                    dst = ACC[b][:, :, q0:q0 + 127 * d + 1:d]
                    src = np_[:, 0:256].rearrange("p (a b) -> p a b", a=2)
                    if d == 1:
                        self.cp("dve", dst, src, [nn], [AN])
                    else:
                        self.tt(dst, src, dst, ALU.add, [nn, AN], [AN])

                pend = []
                for un_ in units:
                    pend.append(s_part(*un_))
                    if len(pend) > 1:
                        pv_part(*pend.pop(0))
                while pend:
                    pv_part(*pend.pop(0))
                self.act(tmpf[:], ACC[b][:, 1, :], AF.Ln, [AN], ["at_tmp"])
                self.act(tmpf[:], tmpf[:], AF.Exp, ["at_tmp"], ["at_tmp"], scale=-1.0)
                self.tt(mixedT[:, 8 + h, 0:OWN], ACC[b][:, 0, :], tmpf[:], ALU.mult, [AN, "at_tmp"], ["mixh%d" % h])
            P.flush()

    def stage_attn_s(self):
        return

    def stage_wout(self):
        P = self.P
        with contextlib.ExitStack() as st:
            sb = self.sb
            mixedT = self.arena1
            Kg = [sb(st, "as_k%d" % i, [128, 3, 1024], F32) for i in range(2)]
            Vg = [sb(st, "as_v%d" % i, [128, 3, 1024], F32) for i in range(2)]
            qb = [sb(st, "as_q%d" % i, [128, 1024], F32) for i in range(2)]
            prod = sb(st, "as_prod", [128, 3, 1024], F32)
            prodv = sb(st, "as_prodv", [128, 24, 128], BF16)
            Sg = sb(st, "as_S", [128, 24], F32)
            Eg = sb(st, "as_E", [128, 24], F32)
            Egb = sb(st, "as_Eb", [128, 24], BF16)
            gm = sb(st, "as_gm", [128, 2, 24], F32)
            num_ps = self.psb(st, "as_num", 1)[0]
            den_ps = self.psb(st, "as_den", 1)[0]
            self_ps = self.psb(st, "as_self", 1)[0]
            negBias = self.colv[:, 3:4]
            self.dma(gm[:].rearrange("p a b -> p (a b)"), self.gmask_d[:, :], [], ["gm"])
            numv = num_ps[:, 0:8 * SC].rearrange("p (h j) -> p h j", h=8)
            denv = den_ps[:, 0:8 * SC].rearrange("p (h j) -> p h j", h=8)
            ntok = 18

            def gather(j):
                b = j % 2
                for bi, d in enumerate(DILS):
                    if j < 16:
                        bl, t = j // 4, j % 4
                        if d == 1:
                            for (G, src, newsrc, nm) in ((Kg, self.cwk, self.kS_s, "as_k%d" % b), (Vg, self.cwv, self.vS_s, "as_v%d" % b)):
                                self.dma(G[b][:, bi, :], src[bl, 1920:2048, :], [], [nm])
                                if t:
                                    self.dma(G[b][0:t, bi, :], newsrc[4 * bl:4 * bl + t, :], [], [nm])
                        else:
                            r0 = 2048 + t - 128 * d
                            for (G, src, nm) in ((Kg, self.cwk, "as_k%d" % b), (Vg, self.cwv, "as_v%d" % b)):
                                self.dma(G[b][:, bi, :], src[bl, r0:r0 + 127 * d + 1:d, :], [], [nm])
                    else:
                        t = j - 16
                        r0 = HALO - 2 + t - 128 * d
                        for (G, src, nm) in ((Kg, self.kh_s, "as_k%d" % b), (Vg, self.vh_s, "as_v%d" % b)):
                            self.dma(G[b][:, bi, :], src[r0:r0 + 127 * d + 1:d, :], [], [nm])
                self.dma(qb[b][:], self.qS_s[j:j + 1, :].to_broadcast([128, 1024]), [], ["as_q%d" % b])

            def token_step(j):
                b = j % 2
                if j + 1 < ntok:
                    gather(j + 1)
                KN, VN, QN = "as_k%d" % b, "as_v%d" % b, "as_q%d" % b
                self.tt(prod[:], Kg[b][:], qb[b][:].unsqueeze(1).to_broadcast([128, 3, 1024]), ALU.mult, [KN, QN], ["as_prod"])
                self.red(Sg[:], prod[:].rearrange("p a (h d) -> p (a h) d", h=8), ALU.add, ["as_prod"], ["as_S"])
                self.act(Eg[:], Sg[:], AF.Exp, ["as_S", "colv2"], ["as_E"], scale=SCALE, bias=negBias)
                if j >= 16:
                    self.tt(Eg[:], Eg[:], gm[:, j - 16, :], ALU.mult, ["as_E", "gm"], ["as_E"])
                self.cp("dve", Egb[:], Eg[:], ["as_E"], ["as_Eb"])
                self.tt(prodv[:], Vg[b][:].rearrange("p a (h d) -> p (a h) d", h=8), Eg[:].unsqueeze(2).to_broadcast([128, 24, 128]),
                        ALU.mult, [VN, "as_E"], ["as_prodv"])
                for h in range(NH):
                    for bi in range(3):
                        self.mm(numv[:, h, j:j + 1], prodv[:, bi * 8 + h, :], self.onesb[:, 0:1], bi == 0, bi == 2,
                                ["as_prodv", "onesb"], ["as_num"])
                for bi in range(3):
                    self.mm(denv[:, :, j], self.onesb[:], Egb[:, bi * 8:(bi + 1) * 8], bi == 0, bi == 2, ["as_Eb", "onesb"], ["as_den"])

            def finalize_s():
                pqk = sb(st, "as_pqk", [128, 8, SC], BF16)
                Es = sb(st, "as_Es", [128, 8, SC], F32)
                numS = sb(st, "as_numS", [128, 8, SC], F32)
                denS = sb(st, "as_denS", [128, 8, SC], F32)
                self.tt(pqk[:], self.qT_S[:], self.kT_S[:], ALU.mult, ["qT_S", "kT_S"], ["as_pqk"])
                self.mm(self_ps[:, 0:8 * SC], self.onesb[:], pqk[:].rearrange("p h j -> p (h j)"), True, True, ["as_pqk", "onesb"], ["as_self"])
                self.act(Es[:].rearrange("p h j -> p (h j)"), self_ps[:, 0:8 * SC], AF.Exp, ["as_self", "colv2"], ["as_Es"],
                         scale=SCALE, bias=negBias)
                self.ts(Es[:], Es[:], 3.0, None, ALU.mult, None, ["as_Es"], ["as_Es"])
                self.tt(numS[:], Es[:], self.vT_S[:], ALU.mult, ["as_Es", "vT_S"], ["as_numS"])
                self.tt(numS[:, :, 0:ntok], numS[:, :, 0:ntok], numv[:, :, 0:ntok], ALU.add, ["as_numS", "as_num"], ["as_numS"])
                self.tt(denS[:, :, 0:ntok], Es[:, :, 0:ntok], denv[:, :, 0:ntok], ALU.add, ["as_Es", "as_den"], ["as_denS"])
                P.add("dve", lambda e: e.reciprocal(denS[:, :, 0:ntok], denS[:, :, 0:ntok]), ["as_denS"], ["as_denS"])
                self.tt(mixedT[:, 8:16, S0:S0 + ntok], numS[:, :, 0:ntok], denS[:, :, 0:ntok], ALU.mult, ["as_numS", "as_denS"], ["mixS2"])

            NX = 4
            wsl = [sb(st, "wo_w%d" % i, [128, KC, 512], BF16) for i in range(2)]
            xs = [sb(st, "wo_x%d" % i, [128, 512], F32) for i in range(NX)]
            ys = [sb(st, "wo_y%d" % i, [128, 512], F32) for i in range(NX)]
            ps = self.psb(st, "wo_ps", 4)
            psr = Ring([("wo_ps%d" % i, ps[i]) for i in range(4)])
            kk = [0]

            def wout_tile(nb_, t, sl):
                b = kk[0] % NX
                kk[0] += 1
                col0 = t * 128
                if t < 16:
                    xsrc = self.xall[HALO + col0:HALO + col0 + 128, nb_ * 512:(nb_ + 1) * 512]
                else:
                    xsrc = self.xS[:, nb_ * 512:(nb_ + 1) * 512]
                self.dma(xs[b][:], xsrc, [], ["wo_x%d" % b])
                pn, pp_ = psr.next()
                extra = ["mixS2"] if t == 16 else []
                for kc in range(KC):
                    self.mm(pp_[:, :], mixedT[:, kc, col0:col0 + 128], wsl[sl][:, kc, :], kc == 0, kc == KC - 1, ["wo_w%d" % sl] + extra, [pn])
                self.tt(ys[b][:], pp_[:, :], xs[b][:], ALU.add, [pn, "wo_x%d" % b], ["wo_y%d" % b])
                self.dma(self.x1_s[col0:col0 + 128, nb_ * 512:(nb_ + 1) * 512], ys[b][:], ["wo_y%d" % b], [self.u()], q="aq")

            wseq = [0, 1, 2, 3, 0, 1, 2, 3]
            self.load_wblock(wsl[0], "wo_w0", self.w_out, 0, 512)
            gather(0)
            ntiles = 64
            done_tok = 0
            ti = 0
            for wi, nb_ in enumerate(wseq):
                if wi + 1 < len(wseq):
                    s2 = (wi + 1) % 2
                    self.load_wblock(wsl[s2], "wo_w%d" % s2, self.w_out, wseq[wi + 1] * 512, 512)
                sl = wi % 2
                if wi < 4:
                    for t in range(16):
                        wout_tile(nb_, t, sl)
                        ti += 1
                        want = min(ntok, (ti * ntok + ntiles - 1) // ntiles)
                        while done_tok < want:
                            token_step(done_tok)
                            done_tok += 1
                    if wi == 3:
                        while done_tok < ntok:
                            token_step(done_tok)
                            done_tok += 1
                        finalize_s()
                else:
                    wout_tile(nb_, 16, sl)
            P.flush()

    def stage_mem(self):
        P = self.P
        with contextlib.ExitStack() as st:
            sb = self.sb
            mkT = sb(st, "mm_mkT", [128, 5, MH, 256], BF16)
            mv = sb(st, "mm_mv", [128, 5, 2, 512], BF16)
            omT = sb(st, "mm_omT", [128, MH, NT], BF16)
            h2T = self.arena1
            with contextlib.ExitStack() as st2:
                bufs = self.nt_bufs(st2, "m1")
                gain_b = sb(st2, "m1_gain", [128, D], F32)
                wk = sb(st2, "m1_wk", [128, KC, 512], BF16)
                wv = sb(st2, "m1_wv", [128, KC, 512], BF16)
                gkb = sb(st2, "m1_gkb", [128, 128], F32)
                k32 = sb(st2, "m1_k32", [128, 512], F32)
                k32b = sb(st2, "m1_k32b", [128, 512], F32)
                kst = sb(st2, "m1_kst", [128, 12], F32)
                k16 = [sb(st2, "m1_k16_%d" % i, [128, 512], BF16) for i in range(2)]
                raw_ps = self.psb(st2, "m1_raw", 2)
                rawr = Ring([("m1_raw%d" % i, raw_ps[i]) for i in range(2)])
                tp_ps = bufs[3]
                self.load_wblock(wk, "m1_wk", self.w_mem_k, 0, 512)
                self.load_wblock(wv, "m1_wv", self.w_mem_v, 0, 512)
                self.dma(gain_b[:], self.norm_mem_src[0:1, :].to_broadcast([128, D]), [], ["gain"])
                self.dma(gkb[:], self.mem_k_norm[0:1, :].to_broadcast([128, 128]), [], ["gkb"])
                mT = h2T
                for i in range(2):
                    self.norm_transpose(self.mem_prompt[i * 128:(i + 1) * 128, :], [], gain_b, "gain", mT, i * 128, "h2T%d" % i, bufs, i)
                self.nt_flush()
                v4 = lambda ap: ap.rearrange("p (h d) -> p h d", h=4)
                for mt in range(2):
                    rn, rp = rawr.next()
                    for kc in range(KC):
                        self.mm(rp[:, :], mT[:, kc, mt * 128:(mt + 1) * 128], wk[:, kc, :], kc == 0, kc == KC - 1, ["m1_wk", "h2T%d" % mt], [rn])
                    self.cp("act", k32b[:], rp[:, :], [rn], ["m1_k32b"])
                    self.tt(k32[:], k32b[:], k32b[:], ALU.mult, ["m1_k32b"], ["m1_k32"])
                    self.red(kst[:, 0:4], v4(k32[:]), ALU.add, ["m1_k32"], ["m1_ksta"])
                    self.act(kst[:, 4:8], kst[:, 0:4], AF.Ln, ["m1_ksta"], ["m1_kstb"], scale=1.0 / 128, bias=EPS)
                    self.act(kst[:, 8:12], kst[:, 4:8], AF.Exp, ["m1_kstb"], ["m1_kstc"], scale=-0.5)
                    self.tt(v4(k32[:]), v4(k32b[:]), kst[:, 8:12].unsqueeze(2).to_broadcast([128, 4, 128]), ALU.mult,
                            ["m1_k32b", "m1_kstc"], ["m1_k32"])
                    self.tt(v4(k32[:]), v4(k32[:]), gkb[:].unsqueeze(1).to_broadcast([128, 4, 128]), ALU.mult, ["m1_k32", "gkb"], ["m1_k32"])
                    self.dma(self.pmk[mt * 128:(mt + 1) * 128, :], k32[:], ["m1_k32"], [self.u()])
                    self.cp("act", k16[0][:], k32[:], ["m1_k32"], ["m1_k16_0"])
                    pt = tp_ps.next()
                    for hm in range(MH):
                        self.tr(pt[1][:, hm * 128:(hm + 1) * 128], k16[0][:, hm * 128:(hm + 1) * 128], self.identb[:], ["m1_k16_0", "identb"], [pt[0]])
                    self.cp("dve", mkT[:, 4, :, mt * 128:(mt + 1) * 128], pt[1][:, 0:512].rearrange("p (a b) -> p a b", a=4), [pt[0]], ["mkT"])
                    rn, rp = rawr.next()
                    for kc in range(KC):
                        self.mm(rp[:, :], mT[:, kc, mt * 128:(mt + 1) * 128], wv[:, kc, :], kc == 0, kc == KC - 1, ["m1_wv", "h2T%d" % mt], [rn])
                    self.cp("act", k32b[:], rp[:, :], [rn], ["m1_k32b"])
                    self.dma(self.pmv[mt * 128:(mt + 1) * 128, :], k32b[:], ["m1_k32b"], [self.u()])
                    self.cp("dve", mv[:, 4, mt, :], k32b[:], ["m1_k32b"], ["mv"])
                ki = 0
                for bl in range(4):
                    for mt in range(2):
                        kb = ki % 2
                        ki += 1
                        KN = "m1_k16_%d" % kb
                        self.dma(k16[kb][:], self.cmk[bl, mt * 128:(mt + 1) * 128, :], [], [KN], q="pq")
                        pt = tp_ps.next()
                        for hm in range(MH):
                            self.tr(pt[1][:, hm * 128:(hm + 1) * 128], k16[kb][:, hm * 128:(hm + 1) * 128], self.identb[:], [KN, "identb"], [pt[0]])
                        self.cp("dve", mkT[:, bl, :, mt * 128:(mt + 1) * 128], pt[1][:, 0:512].rearrange("p (a b) -> p a b", a=4), [pt[0]], ["mkT"])
                        self.dma(mv[:, bl, mt, :], self.cmv[bl, mt * 128:(mt + 1) * 128, :], [], ["mv"], q="pq")
                P.flush()
            with contextlib.ExitStack() as st2:
                bufs = self.nt_bufs(st2, "m2")
                gain_b = sb(st2, "m2_gain", [128, D], F32)
                wq = sb(st2, "m2_wq", [128, KC, 512], BF16)
                raw_ps = self.psb(st2, "m2_raw", 2)
                ssq_ps = self.psb(st2, "m2_ssq", 1)[0]
                s_ps = self.psb(st2, "m2_s", 2)
                o_ps = self.psb(st2, "m2_o", 1)[0]
                rawr = Ring([("m2_raw%d" % i, raw_ps[i]) for i in range(2)])
                self.load_wblock(wq, "m2_wq", self.w_mem_q, 0, 512)
                self.dma(gain_b[:], self.norm_mem[0:1, :].to_broadcast([128, D]), [], ["gain"])
                for i in range(17):
                    self.norm_transpose(self.x1_s[i * 128:(i + 1) * 128, :], [], gain_b, "gain", h2T, i * 128, "h2T%d" % i, bufs, i)
                self.nt_flush()
                nb = 3
                sqb = [sb(st2, "m2_sq%d" % i, [128, 512], BF16) for i in range(nb)]
                t1f = [sb(st2, "m2_t1f%d" % i, [128, 512], F32) for i in range(nb)]
                lnv = [sb(st2, "m2_ln%d" % i, [128, 512], F32) for i in range(nb)]
                qm = [sb(st2, "m2_qm%d" % i, [128, 512], BF16) for i in range(nb)]
                Em = [sb(st2, "m2_E%d" % i, [128, 2, 512], BF16) for i in range(nb)]
                lnd = [sb(st2, "m2_lnd%d" % i, [128, 512], F32) for i in range(nb)]
                negBm = self.colv[:, 4:5]
                groups = [(0, 512), (512, 512), (1024, 512), (1536, 512), (S0, SC)]
                self.memset(omT[:, :, S0:S0 + 128], 0.0, ["omTpad"], eng="dve")

                def make_unit(hm, c0, n, u):
                    U = "m2_u%d_" % u
                    if c0 < S0:
                        segs = [(4, 0, n)]
                        nv = n
                    else:
                        segs = [(0, 0, 4), (1, 4, 4), (2, 8, 4), (3, 12, 4), (4, 16, 2)]
                        nv = 18
                    st_ = {}

                    def px():
                        rn, rp = rawr.next()
                        tiles = ["h2T%d" % t for t in range(c0 // 128, (c0 + n + 127) // 128)]
                        for kc in range(KC):
                            self.mm(rp[:, 0:n], wq[:, kc, hm * 128:(hm + 1) * 128], h2T[:, kc, c0:c0 + n], kc == 0, kc == KC - 1,
                                    ["m2_wq"] + tiles, [rn])
                        self.act(sqb[u][:, 0:n], rp[:, 0:n], AF.Square, [rn], [U + "sq"])
                        self.ts(t1f[u][:, 0:n], rp[:, 0:n], self.colv[:, 2:3], None, ALU.mult, None, [rn, "colv", U + "sq"], [U + "t1f"])

                    def py():
                        self.mm(ssq_ps[:, 0:n], self.onesb[:], sqb[u][:, 0:n], True, True, [U + "sq", "onesb"], ["m2_ssq"])
                        self.act(lnv[u][:, 0:n], ssq_ps[:, 0:n], AF.Ln, ["m2_ssq"], [U + "ln"], scale=1.0 / 128, bias=EPS)
                        self.act(lnv[u][:, 0:n], lnv[u][:, 0:n], AF.Exp, [U + "ln"], [U + "ln"], scale=-0.5)
                        self.tt(qm[u][:, 0:n], t1f[u][:, 0:n], lnv[u][:, 0:n], ALU.mult, [U + "t1f", U + "ln"], [U + "qm"])
                        for mc in range(2):
                            for (sidx, o0, nn) in segs:
                                self.mm(s_ps[mc][:, o0:o0 + nn], mkT[:, sidx, hm, mc * 128:(mc + 1) * 128], qm[u][:, o0:o0 + nn],
                                        True, True, ["mkT", U + "qm"], ["m2_s%d" % mc])
                        for mc in range(2):
                            self.act(Em[u][:, mc, 0:nv], s_ps[mc][:, 0:nv], AF.Exp, ["m2_s%d" % mc, "colv2"], [U + "E"], scale=SCALE, bias=negBm)

                    def pz():
                        for (sidx, o0, nn) in segs:
                            for mc in range(2):
                                self.mm(o_ps[:, o0:o0 + nn], mv[:, sidx, mc, hm * 128:(hm + 1) * 128], Em[u][:, mc, o0:o0 + nn],
                                        mc == 0, mc == 1, ["mv", U + "E"], ["m2_o"])
                        for mc in range(2):
                            self.mm(ssq_ps[:, 0:nv], self.onesb[:], Em[u][:, mc, 0:nv], mc == 0, mc == 1, ["onesb", U + "E"], ["m2_ssq"])
                        self.act(lnd[u][:, 0:nv], ssq_ps[:, 0:nv], AF.Ln, ["m2_ssq"], [U + "lnd"])
                        self.act(lnd[u][:, 0:nv], lnd[u][:, 0:nv], AF.Exp, [U + "lnd"], [U + "lnd"], scale=-1.0)
                        self.tt(omT[:, hm, c0:c0 + nv], o_ps[:, 0:nv], lnd[u][:, 0:nv], ALU.mult, ["m2_o", U + "lnd"],
                                ["omT%d_%d" % (hm, c0)] + (["omTpad"] if c0 == S0 else []))
                    return (px, py, pz)

                pipe = [None, None]
                cnt = 0
                for hm in range(MH):
                    for (c0, n) in groups:
                        ph = make_unit(hm, c0, n, cnt % nb)
                        cnt += 1
                        ph[0]()
                        if pipe[0] is not None:
                            pipe[0][1]()
                        if pipe[1] is not None:
                            pipe[1][2]()
                        pipe[1] = pipe[0]
                        pipe[0] = ph
                pipe[0][1]()
                pipe[1][2]()
                pipe[0][2]()
                P.flush()
            with contextlib.ExitStack() as st2:
                h3T = self.arena1
                bufs = self.nt_bufs(st2, "m3")
                gain_b = sb(st2, "m3_gain", [128, D], F32)
                wmo = sb(st2, "m3_wmo", [128, MH, D], BF16)
                x1t = [sb(st2, "m3_x1t%d" % i, [128, D], F32) for i in range(2)]
                big_ps = self.psb(st2, "m3_big", 4)
                self.dma(wmo[:, :, 0:1024], self.w_mem_o[:, 0:1024].rearrange("(k p) n -> p k n", p=128), [], ["m3_wmo"], q="pq")
                self.dma(wmo[:, :, 1024:2048], self.w_mem_o[:, 1024:2048].rearrange("(k p) n -> p k n", p=128), [], ["m3_wmo"], q="pq")
                self.dma(gain_b[:], self.norm_ffn[0:1, :].to_broadcast([128, D]), [], ["gain"])
                xt = bufs[0]
                self._nt_evac = "aaad"
                self.dma(x1t[0][:], self.x1_s[0:128, :], [], ["m3_x1t0"])
                for t in range(17):
                    b = t % 2
                    col0 = t * 128
                    if t + 1 < 17:
                        self.dma(x1t[1 - b][:], self.x1_s[col0 + 128:col0 + 256, :], [], ["m3_x1t%d" % (1 - b)])
                    for nb_ in range(4):
                        for hm in range(MH):
                            self.mm(big_ps[nb_][:, :], omT[:, hm, col0:col0 + 128], wmo[:, hm, nb_ * 512:(nb_ + 1) * 512], hm == 0, hm == MH - 1,
                                    ["m3_wmo"], ["m3_big%d" % nb_])
                        self.tt(xt[b][:, nb_ * 512:(nb_ + 1) * 512], big_ps[nb_][:, :], x1t[b][:, nb_ * 512:(nb_ + 1) * 512], ALU.add,
                                ["m3_big%d" % nb_, "m3_x1t%d" % b], ["nt_x%d" % b])
                    self.dma(self.x2_s[col0:col0 + 128, :], xt[b][:], ["nt_x%d" % b], [self.u()], q="pq")
                    self.nt_flush()
                    self.norm_transpose(None, [], gain_b, "gain", h3T, col0, "h3T%d" % t, bufs, t)
                self.nt_flush()
                self._nt_evac = "adad"
                P.flush()

    def stage_ffn_a(self):
        P = self.P
        with contextlib.ExitStack() as st:
            sb = self.sb
            h3T = self.arena1
            wg = [sb(st, "fa_wg%d" % i, [128, KC, 512], BF16) for i in range(2)]
            wu = [sb(st, "fa_wu%d" % i, [128, KC, 512], BF16) for i in range(2)]
            gs = [sb(st, "fa_gs%d" % i, [128, 2 + OWN], F32) for i in range(2)]
            cbuf = [sb(st, "fa_c%d" % i, [128, 512], F32) for i in range(2)]
            sbuf_ = [sb(st, "fa_s%d" % i, [128, 512], F32) for i in range(2)]
            abuf = [sb(st, "fa_a%d" % i, [128, 512], BF16) for i in range(3)]
            gS = sb(st, "fa_gS", [128, SC], F32)
            extS = sb(st, "fa_extS", [128, 4, 6], F32)
            cS = sb(st, "fa_cS", [128, 4, 4], F32)
            aS = sb(st, "fa_aS", [128, SC], BF16)
            convst = sb(st, "fa_convst", [128, NFC, 4, 2], F32)
            sconvT = sb(st, "fa_sconvT", [128, 4, 2, 64], F32)
            pconvT = sb(st, "fa_pconvT", [128, 2, 64], F32)
            scin = sb(st, "fa_scin", [8, DFF], F32)
            outst = sb(st, "fa_outst", [128, 1024], F32)
            g_ps = self.psb(st, "fa_g", 2)
            u_ps = self.psb(st, "fa_u", 2)
            sm_ps = self.psb(st, "fa_sm", 2)
            gr = Ring([("fa_g%d" % i, g_ps[i]) for i in range(2)])
            ur = Ring([("fa_u%d" % i, u_ps[i]) for i in range(2)])
            self.dma(scin[:], self.state_conv.rearrange("b t n -> (b t) n"), [], ["fa_scin"])
            for fc in range(NFC):
                self.tr(sm_ps[0][:, fc * 8:(fc + 1) * 8], scin[0:8, fc * 128:(fc + 1) * 128], self.identf[0:8, 0:8], ["fa_scin", "identf"], ["fa_sm0"])
            self.cp("dve", convst[:].rearrange("p c b t -> p (c b t)"), sm_ps[0][:, 0:NFC * 8], ["fa_sm0"], ["fa_convst"])
            self.memset(aS[:], 0.0, ["fa_aS"], eng="dve")
            self.memset(sconvT[:], 0.0, ["fa_sconvT"], eng="dve")
            self.memset(pconvT[:], 0.0, ["fa_pconvT"], eng="dve")

            def loadw(blk):
                s = blk % 2
                self.load_wblock(wg[s], "fa_wg%d" % s, self.w_gate, blk * 512, 512)
                self.load_wblock(wu[s], "fa_wu%d" % s, self.w_up, blk * 512, 512)

            loadw(0)
            groups = [(0, 512), (512, 512), (1024, 512), (1536, 512)]
            tiles_of = lambda c0, n: ["h3T%d" % t for t in range(c0 // 128, (c0 + n + 127) // 128)]
            ka = 0
            import os
            nfc_run = int(os.environ.get("FFA_NFC", NFC))
            for fc in range(nfc_run):
                blk, un = fc // 4, fc % 4
                if un == 0 and blk + 1 < NFC // 4:
                    loadw(blk + 1)
                s = blk % 2
                gb_ = fc % 2
                GS = "fa_gs%d" % gb_
                w0, w1, w2, bb = self.cw[:, fc, 0:1], self.cw[:, fc, 1:2], self.cw[:, fc, 2:3], self.cb[:, fc:fc + 1]
                gn, gp = gr.next()
                un_, up = ur.next()
                for kc in range(KC):
                    self.mm(gp[:, 0:SC], wg[s][:, kc, un * 128:(un + 1) * 128], h3T[:, kc, S0:S0 + SC], kc == 0, kc == KC - 1, ["fa_wg%d" % s, "h3T16"], [gn])
                for kc in range(KC):
                    self.mm(up[:, 0:SC], wu[s][:, kc, un * 128:(un + 1) * 128], h3T[:, kc, S0:S0 + SC], kc == 0, kc == KC - 1, ["fa_wu%d" % s, "h3T16"], [un_])
                self.cp("act", gS[:], gp[:, 0:SC], [gn], ["fa_gS"])
                self.cp("dve", extS[:, :, 0:2], convst[:, fc, :, :], ["fa_convst"], ["fa_extS"])
                self.cp("dve", extS[:, :, 2:6], gS[:, 0:16].rearrange("p (b t) -> p b t", b=4), ["fa_gS"], ["fa_extS"])
                self.cp("dve", sconvT[:, :, :, fc], gS[:, 0:16].rearrange("p (b t) -> p b t", b=4)[:, :, 2:4], ["fa_gS"], ["fa_sconvT"])
                self.ts(gs[gb_][:, 0:2], gS[:, 16:18], self.hv[:, 0:1], None, ALU.mult, None, ["fa_gS", "hv"], [GS + "_pre"])
                self.ts(cS[:], extS[:, :, 2:6], w2, bb, ALU.mult, ALU.add, ["fa_extS", "cw", "cb"], ["fa_cS"])
                self.stt(cS[:], extS[:, :, 1:5], w1, cS[:], ALU.mult, ALU.add, ["fa_extS", "fa_cS"], ["fa_cS"])
                self.stt(cS[:], extS[:, :, 0:4], w0, cS[:], ALU.mult, ALU.add, ["fa_extS", "fa_cS"], ["fa_cS"])
                self.act(cS[:], cS[:], AF.Silu, ["fa_cS"], ["fa_cS"])
                self.tt(aS[:, 0:16].rearrange("p (b t) -> p b t", b=4), cS[:], up[:, 0:16].rearrange("p (b t) -> p b t", b=4), ALU.mult,
                        ["fa_cS", un_], ["fa_aS"])
                self.dma(self.a_s[fc, :, S0:S0 + SC], aS[:], ["fa_aS"], [self.u()])
                for (c0, n) in groups:
                    gn, gp = gr.next()
                    un_, up = ur.next()
                    tl = tiles_of(c0, n)
                    for kc in range(KC):
                        self.mm(gp[:, 0:n], wg[s][:, kc, un * 128:(un + 1) * 128], h3T[:, kc, c0:c0 + n], kc == 0, kc == KC - 1, ["fa_wg%d" % s] + tl, [gn])
                    for kc in range(KC):
                        self.mm(up[:, 0:n], wu[s][:, kc, un * 128:(un + 1) * 128], h3T[:, kc, c0:c0 + n], kc == 0, kc == KC - 1, ["fa_wu%d" % s] + tl, [un_])
                    GSc = GS + "_%d" % c0
                    GSp = GS + ("_%d" % (c0 - 512) if c0 > 0 else "_pre")
                    self.cp("act", gs[gb_][:, 2 + c0:2 + c0 + n], gp[:, 0:n], [gn], [GSc])
                    cb_ = ka % 2
                    ab_ = ka % 3
                    ka += 1
                    CN, SN, ANm = "fa_c%d" % cb_, "fa_s%d" % cb_, "fa_a%d" % ab_
                    self.ts(cbuf[cb_][:, 0:n], gs[gb_][:, 2 + c0:2 + c0 + n], w2, bb, ALU.mult, ALU.add, [GSc, "cw", "cb"], [CN])
                    self.stt(cbuf[cb_][:, 0:n], gs[gb_][:, 1 + c0:1 + c0 + n], w1, cbuf[cb_][:, 0:n], ALU.mult, ALU.add, [GSc, GSp, CN], [CN])
                    self.stt(cbuf[cb_][:, 0:n], gs[gb_][:, c0:c0 + n], w0, cbuf[cb_][:, 0:n], ALU.mult, ALU.add, [GSc, GSp, CN], [CN])
                    self.act(sbuf_[cb_][:, 0:n], cbuf[cb_][:, 0:n], AF.Silu, [CN], [SN])
                    self.tt(abuf[ab_][:, 0:n], sbuf_[cb_][:, 0:n], up[:, 0:n], ALU.mult, [SN, un_], [ANm])
                    self.dma(self.a_s[fc, :, c0:c0 + n], abuf[ab_][:, 0:n], [ANm], [self.u()])
                self.cp("dve", pconvT[:, :, fc], gs[gb_][:, OWN:OWN + 2], [GS + "_1536"], ["fa_pconvT"])
            epi = os.environ.get("FFA_EPI", "ps")
            if epi == "0":
                P.flush()
                return
            self.tr(sm_ps[1][:, 0:128], pconvT[:].rearrange("p t c -> p (t c)"), self.identf[:], ["fa_pconvT", "identf"], ["fa_sm1"])
            self.cp("act", outst[:, 0:128], sm_ps[1][:, 0:128], ["fa_sm1"], ["fa_outst"])
            for t in range(2 if "p" in epi else 0):
                self.dma(self.pconv[t:t + 1, :].rearrange("o (c p) -> (o c) p", p=128), outst[t * 64:t * 64 + NFC, 0:128], ["fa_outst"], [self.u()])
            kl = [int(ch) for ch in epi if ch.isdigit()] if any(ch.isdigit() for ch in epi) else list(range(4))
            for k in (kl if "s" in epi else []):
                pdst = sm_ps[1][:, 128 + k * 128:128 + (k + 1) * 128] if k < 3 else sm_ps[0][:, 384:512]
                self.tr(pdst, sconvT[:, k, :, :].rearrange("p t c -> p (t c)"), self.identf[:], ["fa_sconvT", "identf"], ["fa_sm1" if k < 3 else "fa_sm0"])
                self.cp("act", outst[:, 128 + k * 128:128 + (k + 1) * 128], pdst, ["fa_sm1" if k < 3 else "fa_sm0"], ["fa_outst2_%d" % k])
                for t in range(2):
                    self.dma(self.s_conv[k, t:t + 1, :].rearrange("o (c p) -> (o c) p", p=128),
                             outst[t * 64:t * 64 + NFC, 128 + k * 128:128 + (k + 1) * 128], ["fa_outst2_%d" % k], [self.u()])
            P.flush()

    def stage_ffn_b(self):
        P = self.P
        with contextlib.ExitStack() as st:
            sb = self.sb
            TG = 512
            wd = [sb(st, "fb_wd%d" % i, [128, NFC, 512], BF16) for i in range(2)]
            ag = [sb(st, "fb_ag%d" % i, [128, NFC, TG], BF16) for i in range(2)]
            xs = [sb(st, "fb_x%d" % i, [128, 512], F32) for i in range(4)]
            ys = [sb(st, "fb_y%d" % i, [128, 512], F32) for i in range(4)]
            ps = self.psb(st, "fb_ps", 6)
            psr = Ring([("fb_ps%d" % i, ps[i]) for i in range(6)])

            def loadwd(nb_):
                s = nb_ % 2
                src = self.w_down[:, nb_ * 512:(nb_ + 1) * 512].rearrange("(k p) n -> p k n", p=128)
                for qq in range(4):
                    self.dma(wd[s][:, 11 * qq:11 * qq + 11, :], src[:, 11 * qq:11 * qq + 11, :], [], ["fb_wd%d_%d" % (s, qq)], q="pq")

            tgroups = [(i * TG, TG) for i in range(OWN // TG)] + [(S0, 128)]
            seq = [(nb_, gi) for nb_ in range(4) for gi in range(len(tgroups))]

            def loada(idx):
                nb_, gi = seq[idx]
                c0, n = tgroups[gi]
                s = idx % 2
                nn = n if c0 < S0 else SC
                for qq in range(2):
                    self.dma(ag[s][:, 22 * qq:22 * qq + 22, 0:nn], self.a_s[22 * qq:22 * qq + 22, :, c0:c0 + nn].rearrange("c p t -> p c t"), [],
                             ["fb_ag%d_%d" % (s, qq)], q="aq")

            loadwd(0)
            loada(0)
            k = 0
            for idx, (nb_, gi) in enumerate(seq):
                if gi == 0 and nb_ + 1 < 4:
                    loadwd(nb_ + 1)
                if idx + 1 < len(seq):
                    loada(idx + 1)
                s = idx % 2
                ws = nb_ % 2
                c0, n = tgroups[gi]
                for t0 in range(0, n, 128):
                    b = k % 4
                    k += 1
                    row0 = c0 + t0
                    m = 128 if c0 < S0 else SC
                    self.dma(xs[b][:], self.x2_s[row0:row0 + 128, nb_ * 512:(nb_ + 1) * 512], [], ["fb_x%d" % b], q="aq")
                    pn, pp_ = psr.next()
                    for fc in range(NFC):
                        self.mm(pp_[0:m, :], ag[s][:, fc, t0:t0 + m], wd[ws][:, fc, :], fc == 0, fc == NFC - 1, ["fb_ag%d_%d" % (s, fc // 22), "fb_wd%d_%d" % (ws, fc // 11)], [pn])
                    self.tt(ys[b][0:m, :], pp_[0:m, :], xs[b][0:m, :], ALU.add, [pn, "fb_x%d" % b], ["fb_y%d" % b])
                    if c0 < S0:
                        self.dma(self.y[row0:row0 + 128, nb_ * 512:(nb_ + 1) * 512], ys[b][:], ["fb_y%d" % b], [self.u()])
                    else:
                        self.dma(self.yS[:, nb_ * 512:(nb_ + 1) * 512], ys[b][0:16, :], ["fb_y%d" % b], [self.u()])
            P.flush()


def _rope_tables(pos):
    half = 64
    inv = (1.0 / (10000.0 ** (np.arange(half, dtype=np.float32) * np.float32(2.0 / 128)))).astype(np.float32)
    ang = pos.astype(np.float32)[:, None] * inv[None, :]
    cos = np.cos(ang).astype(np.float32).T
    sin = np.sin(ang).astype(np.float32).T
    cosT = np.concatenate([cos, cos], axis=0)
    sinT = np.concatenate([-sin, sin], axis=0)
    return np.ascontiguousarray(cosT), np.ascontiguousarray(sinT)


def make_in_maps(inp):
    f = lambda a: np.ascontiguousarray(np.asarray(a, dtype=np.float32))
    xp = f(inp["x_prompt"])[0]
    xs_all = f(inp["x_sample"])
    shared = {
        "w_in": f(inp["w_in"])[0], "w_pool": f(inp["w_pool"])[0], "w_out": f(inp["w_out"])[0],
        "w_mem_q": f(inp["w_mem_q"])[0], "w_mem_k": f(inp["w_mem_k"])[0], "w_mem_v": f(inp["w_mem_v"])[0],
        "w_mem_o": f(inp["w_mem_o"])[0], "w_gate": f(inp["w_gate"])[0], "w_up": f(inp["w_up"])[0],
        "w_down": f(inp["w_down"])[0],
        "norm_mix": f(inp["norm_mix"]), "norm_mem": f(inp["norm_mem"]), "norm_mem_src": f(inp["norm_mem_src"]),
        "norm_ffn": f(inp["norm_ffn"]), "q_norm": f(inp["q_norm"]), "k_norm": f(inp["k_norm"]),
        "mem_q_norm": f(inp["mem_q_norm"]), "mem_k_norm": f(inp["mem_k_norm"]), "pool_scale": f(inp["pool_scale"]),
        "conv_w": f(inp["conv_w"])[0], "conv_b": f(inp["conv_b"]), "mem_prompt": f(inp["mem_prompt"])[0],
    }
    maps = []
    for c in range(NCORES):
        start = c * OWN
        xall = np.zeros((EXT, D), np.float32)
        lo = start - HALO
        s0 = max(lo, 0)
        xall[s0 - lo:, :] = xp[s0:start + OWN]
        xS = np.zeros((128, D), np.float32)
        xS[0:16] = xs_all[4 * c:4 * c + 4].reshape(16, D)
        xS[16:18] = xall[HALO - 2:HALO]
        pos = np.arange(lo, start + OWN)
        cosT, sinT = _rope_tables(pos)
        posS = np.zeros(SC, np.int64)
        posS[0:16] = np.tile(PAST + np.arange(4), 4)
        posS[16:18] = [start - 2, start - 1]
        cosS, sinS = _rope_tables(posS)
        valid = 1.0 if c > 0 else 0.0
        hv = np.zeros((128, 2), np.float32)
        hv[:, 0] = valid
        hv[:, 1] = NEG * (1.0 - valid)
        corr = np.ones((128, 4, 16), np.float32)
        for g_ in range(4):
            w = 2 << g_
            p = start + np.arange(16)
            corr[:, g_, :] = (w / np.minimum(p + 1, w)).astype(np.float32)[None, :]
        gmask = np.ones((128, 2, 3, 8), np.float32)
        for t in range(2):
            for bi, d in enumerate(DILS):
                kp = (start - 2 + t) - d * (128 - np.arange(128))
                gmask[:, t, bi, :] = (kp >= 0).astype(np.float32)[:, None]
        m = dict(shared)
        m.update({
            "xall": xall, "xS": xS, "cosT": cosT, "sinT": sinT, "cosS": cosS, "sinS": sinS, "hv": hv,
            "corr": corr.reshape(128, 64), "gmask": gmask.reshape(128, 48),
            "state_pool": f(inp["state_pool"])[0, 4 * c:4 * c + 4],
            "cwk": f(inp["cache_win_k"])[0, 4 * c:4 * c + 4].reshape(4, 2048, 1024),
            "cwv": f(inp["cache_win_v"])[0, 4 * c:4 * c + 4].reshape(4, 2048, 1024),
            "cmk": f(inp["cache_mem_k"])[0, 4 * c:4 * c + 4].reshape(4, 256, 512),
            "cmv": f(inp["cache_mem_v"])[0, 4 * c:4 * c + 4].reshape(4, 256, 512),
            "state_conv": f(inp["state_conv"])[0, 4 * c:4 * c + 4],
        })
        maps.append(m)
    return maps


_NC_CACHE = {}


def get_nc(debug=False, stop_after=None):
    key = (debug, stop_after)
    if key not in _NC_CACHE:
        b = Builder(debug=debug, stop_after=stop_after)
        b.build()
        _NC_CACHE[key] = b
    return _NC_CACHE[key]


def kernel(**inputs):
    b = get_nc()
    maps = make_in_maps(inputs)
    keys = set(b.din.keys())
    maps = [{k: v for k, v in m.items() if k in keys} for m in maps]
    res = run_bass_kernel_spmd(b.nc, maps, core_ids=list(range(NCORES)))
    r = res.results
    cat = lambda k: np.concatenate([np.asarray(r[c][k]) for c in range(NCORES)], axis=0)
    y_prompt = cat("y").reshape(1, NCORES * OWN, D)
    y_sample = cat("yS").reshape(32, 4, D)
    last = r[NCORES - 1]
    p_state_pool = np.asarray(last["p_pool"]).reshape(1, 1, 15, PW)
    p_win_k = np.asarray(last["pk"]).reshape(1, 1, 2048, NH, 128)
    p_win_v = np.asarray(last["pv"]).reshape(1, 1, 2048, NH, 128)
    p_mem_k = np.asarray(r[0]["pmk"]).reshape(1, 1, 256, MH, 128)
    p_mem_v = np.asarray(r[0]["pmv"]).reshape(1, 1, 256, MH, 128)
    p_state_conv = np.asarray(last["pconv"]).reshape(1, 1, 2, DFF)
    s_state_pool = cat("s_pool").reshape(1, 32, 15, PW)
    s_k = cat("s_k").reshape(1, 32, 4, NH, 128)
    s_v = cat("s_v").reshape(1, 32, 4, NH, 128)
    s_conv = cat("s_conv").reshape(1, 32, 2, DFF)
    outs = (y_prompt, y_sample, p_state_pool, p_win_k, p_win_v, p_mem_k, p_mem_v, p_state_conv,
            s_state_pool, s_k, s_v, s_conv)
    return tuple(np.ascontiguousarray(o, dtype=np.float32) for o in outs)
```
